# Optimizing a Trainium2 kernel written in Bass

```python
import jax, jax.numpy as jnp
from jax import lax
import numpy as np

D_MODEL = 1024
BATCH = 32
SEQ = 2048
DEPTH = 1
DEC_BATCH = 16
DEC_SEQ = 16
PAST_LEN = 4096

CHUNK = 64
Q_BLOCK = 128
RET_HEADS = 4
RET_KEY_DIM = 128
RET_VAL_DIM = 256
FOX_HEADS = 8
FOX_HEAD_DIM = 64
D_FF = 2816
PLE_DIM = 256
N_BRANCH = 2
ROPE_BASE = 10000.0
EPS = 1e-6
RET_QK_W = RET_HEADS * RET_KEY_DIM
RET_V_W = RET_HEADS * RET_VAL_DIM
FOX_W = FOX_HEADS * FOX_HEAD_DIM
IN_SIZES = (RET_QK_W, RET_QK_W, RET_V_W, RET_V_W, FOX_W, FOX_W, FOX_W, FOX_HEADS, N_BRANCH * D_MODEL)
IN_COLS = RET_QK_W * 2 + RET_V_W * 2 + FOX_W * 3 + FOX_HEADS + N_BRANCH * D_MODEL

kernel_name = "hybrid_retention_fox_streaming_step"


def rms_norm(x, g=None):
    xf = x.astype(jnp.float32)
    y = xf * lax.rsqrt(jnp.mean(xf * xf, axis=-1, keepdims=True) + EPS)
    if g is not None:
        y = y * g.astype(jnp.float32)
    return y.astype(x.dtype)


def swiglu(x, w_in, w_out):
    gu = x @ w_in
    g, u = gu[..., :D_FF], gu[..., D_FF:]
    return (jax.nn.silu(g) * u) @ w_out


def rotary(x, pos):
    d = x.shape[-1]
    half = d // 2
    freqs = ROPE_BASE ** (-jnp.arange(half, dtype=jnp.float32) / half)
    ang = pos[:, None] * freqs[None, :]
    c = jnp.cos(ang)[None, :, None, :]
    s = jnp.sin(ang)[None, :, None, :]
    xf = x.astype(jnp.float32)
    x1, x2 = xf[..., :half], xf[..., half:]
    return jnp.concatenate([x1 * c - x2 * s, x1 * s + x2 * c], axis=-1).astype(x.dtype)


def retention(q, k, v, s0, chunk_len):
    B, S, H, dk = q.shape
    dv = v.shape[-1]
    nc = S // chunk_len
    log_gamma = jnp.log(1.0 - 2.0 ** (-5.0 - jnp.arange(H, dtype=jnp.float32)))
    idx = jnp.arange(chunk_len, dtype=jnp.float32)
    intra_decay = jnp.exp(log_gamma[:, None, None] * jnp.abs(idx[:, None] - idx[None, :]))
    inter_decay = jnp.exp(log_gamma[None, :] * (idx[:, None] + 1.0))[None, :, :, None]
    kv_decay = jnp.exp(log_gamma[None, :] * (chunk_len - 1.0 - idx[:, None]))[None, :, :, None]
    chunk_decay = jnp.exp(log_gamma * chunk_len)[None, :, None, None]

    def to_chunks(a):
        return a.reshape(B, nc, chunk_len, H, a.shape[-1]).transpose(1, 0, 2, 3, 4)

    def step(state, qkv):
        qc, kc, vc = (a.astype(jnp.float32) for a in qkv)
        inter = jnp.einsum('bchk,bhkv->bchv', qc, state) * inter_decay
        att = jnp.einsum('bihk,bjhk->bhij', qc, kc) * intra_decay[None]
        intra = jnp.einsum('bhij,bjhv->bihv', att, vc)
        new_state = state * chunk_decay + jnp.einsum('bjhk,bjhv->bhkv', kc * kv_decay, vc)
        return new_state, (inter + intra).astype(v.dtype)

    s_final, o = lax.scan(step, s0.astype(jnp.float32), (to_chunks(q), to_chunks(k), to_chunks(v)))
    o = o.transpose(1, 0, 2, 3, 4).reshape(B, S, H, dv)
    return o, s_final.astype(s0.dtype)


def fox_prompt(q, k, v, logf):
    B, S, H, d = q.shape
    scale = d ** -0.5
    c = jnp.cumsum(logf, axis=1).transpose(0, 2, 1)
    outs = []
    for i0 in range(0, S, Q_BLOCK):
        i1 = i0 + Q_BLOCK
        s = jnp.einsum('bqhd,bkhd->bhqk', q[:, i0:i1], k[:, :i1], preferred_element_type=jnp.float32) * scale
        bias = c[:, :, i0:i1, None] - c[:, :, None, :i1]
        mask = (i0 + jnp.arange(Q_BLOCK))[:, None] >= jnp.arange(i1)[None, :]
        w = jax.nn.softmax(jnp.where(mask, s + bias, -jnp.inf), axis=-1)
        outs.append(jnp.einsum('bhqk,bkhd->bqhd', w.astype(v.dtype), v[:, :i1]))
    return jnp.concatenate(outs, axis=1)


def fox_sample(q, k, v, logf, ck, cv, clogf):
    L = q.shape[1]
    P = ck.shape[1]
    scale = q.shape[-1] ** -0.5
    cn = jnp.cumsum(logf, axis=1).transpose(0, 2, 1)
    clf = clogf.astype(jnp.float32)
    suf = (lax.cumsum(clf, axis=1, reverse=True) - clf).transpose(0, 2, 1)
    s_past = jnp.einsum('bqhd,bkhd->bhqk', q, ck.astype(q.dtype), preferred_element_type=jnp.float32) * scale
    s_past = s_past + cn[..., :, None] + suf[..., None, :]
    s_new = jnp.einsum('bqhd,bkhd->bhqk', q, k, preferred_element_type=jnp.float32) * scale
    s_new = s_new + cn[..., :, None] - cn[..., None, :]
    mask = jnp.arange(L)[:, None] >= jnp.arange(L)[None, :]
    s_new = jnp.where(mask, s_new, -jnp.inf)
    w = jax.nn.softmax(jnp.concatenate([s_past, s_new], axis=-1), axis=-1).astype(v.dtype)
    return (jnp.einsum('bhqk,bkhd->bqhd', w[..., :P], cv.astype(v.dtype))
            + jnp.einsum('bhqk,bkhd->bqhd', w[..., P:], v))


def token_mixer(x, pos, ret_s0, fox_cache, prm, ret_chunk):
    B, S, _ = x.shape
    h = rms_norm(x, prm['norm_mix_g'])
    z = h @ prm['w_in_mix']
    parts = []
    off = 0
    for n in IN_SIZES:
        parts.append(z[..., off:off + n])
        off += n
    rq, rk, rv, rg, fq, fk, fv, ff, zg = parts
    q = rotary(rq.reshape(B, S, RET_HEADS, RET_KEY_DIM), pos)
    k = rotary(rk.reshape(B, S, RET_HEADS, RET_KEY_DIM), pos) * (RET_KEY_DIM ** -0.5)
    v = rv.reshape(B, S, RET_HEADS, RET_VAL_DIM)
    o_ret, ret_state = retention(q, k, v, ret_s0, ret_chunk)
    o_ret = rms_norm(o_ret).reshape(B, S, RET_V_W) * jax.nn.silu(rg)
    br_ret = o_ret @ prm['w_br_ret']
    fq = rms_norm(fq.reshape(B, S, FOX_HEADS, FOX_HEAD_DIM), prm['q_norm_g'])
    fk = rms_norm(fk.reshape(B, S, FOX_HEADS, FOX_HEAD_DIM), prm['k_norm_g'])
    fv = fv.reshape(B, S, FOX_HEADS, FOX_HEAD_DIM)
    logf = jax.nn.log_sigmoid(ff.astype(jnp.float32) + prm['b_forget'].astype(jnp.float32))
    if fox_cache is None:
        o_fox = fox_prompt(fq, fk, fv, logf)
    else:
        o_fox = fox_sample(fq, fk, fv, logf, *fox_cache)
    br_fox = o_fox.reshape(B, S, FOX_W) @ prm['w_br_fox']
    g = jax.nn.sigmoid(zg).reshape(B, S, N_BRANCH, D_MODEL)
    y = (g[..., 0, :] * br_ret + g[..., 1, :] * br_fox) @ prm['w_out']
    return y, ret_state, fk, fv, logf


def layer(x, p, pos, ret_s0, fox_cache, prm, ret_chunk):
    x = x + 0.5 * swiglu(rms_norm(x, prm['norm_ffn1_g']), prm['ffn1_w_in'], prm['ffn1_w_out'])
    y, ret_state, fk, fv, logf = token_mixer(x, pos, ret_s0, fox_cache, prm, ret_chunk)
    x = x + y
    x = x + 0.5 * swiglu(rms_norm(x, prm['norm_ffn2_g']), prm['ffn2_w_in'], prm['ffn2_w_out'])
    gate = jax.nn.sigmoid(rms_norm(x, prm['norm_ple_g']) @ prm['w_ple_gate'])
    x = x + (p @ prm['w_ple']) * gate
    return x, ret_state, fk, fv, logf


def setup_inputs(seed: int = 0) -> dict:
    key = jax.random.key(seed)
    ks = jax.random.split(key, 32)
    f32 = jnp.float32

    def nrm(k, shape, scale):
        return jax.random.normal(k, shape, f32) * scale

    def gain(k, shape):
        return 1.0 + 0.05 * jax.random.normal(k, shape, f32)

    return {
        'x_prompt': nrm(ks[0], (BATCH, SEQ, D_MODEL), 1.0),
        'x_sample': nrm(ks[1], (DEC_BATCH, DEC_SEQ, D_MODEL), 1.0),
        'p_prompt': nrm(ks[2], (DEPTH, BATCH, SEQ, PLE_DIM), 1.0),
        'p_sample': nrm(ks[3], (DEPTH, DEC_BATCH, DEC_SEQ, PLE_DIM), 1.0),
        'state_ret': nrm(ks[4], (DEPTH, DEC_BATCH, RET_HEADS, RET_KEY_DIM, RET_VAL_DIM), 0.1),
        'cache_fox_k': nrm(ks[5], (DEPTH, DEC_BATCH, PAST_LEN, FOX_HEADS, FOX_HEAD_DIM), 1.0),
        'cache_fox_v': nrm(ks[6], (DEPTH, DEC_BATCH, PAST_LEN, FOX_HEADS, FOX_HEAD_DIM), 1.0),
        'cache_fox_logf': jax.nn.log_sigmoid(2.0 + jax.random.normal(ks[7], (DEPTH, DEC_BATCH, PAST_LEN, FOX_HEADS), f32)),
        'norm_ffn1_g': gain(ks[8], (DEPTH, D_MODEL)),
        'ffn1_w_in': nrm(ks[9], (DEPTH, D_MODEL, 2 * D_FF), D_MODEL ** -0.5),
        'ffn1_w_out': nrm(ks[10], (DEPTH, D_FF, D_MODEL), D_FF ** -0.5),
        'norm_mix_g': gain(ks[11], (DEPTH, D_MODEL)),
        'w_in_mix': nrm(ks[12], (DEPTH, D_MODEL, IN_COLS), D_MODEL ** -0.5),
        'b_forget': 1.0 + 0.1 * jax.random.normal(ks[13], (DEPTH, FOX_HEADS), f32),
        'q_norm_g': gain(ks[14], (DEPTH, FOX_HEAD_DIM)),
        'k_norm_g': gain(ks[15], (DEPTH, FOX_HEAD_DIM)),
        'w_br_ret': nrm(ks[16], (DEPTH, RET_V_W, D_MODEL), RET_V_W ** -0.5),
        'w_br_fox': nrm(ks[17], (DEPTH, FOX_W, D_MODEL), FOX_W ** -0.5),
        'w_out': nrm(ks[18], (DEPTH, D_MODEL, D_MODEL), D_MODEL ** -0.5),
        'norm_ffn2_g': gain(ks[19], (DEPTH, D_MODEL)),
        'ffn2_w_in': nrm(ks[20], (DEPTH, D_MODEL, 2 * D_FF), D_MODEL ** -0.5),
        'ffn2_w_out': nrm(ks[21], (DEPTH, D_FF, D_MODEL), D_FF ** -0.5),
        'norm_ple_g': gain(ks[22], (DEPTH, D_MODEL)),
        'w_ple': nrm(ks[23], (DEPTH, PLE_DIM, D_MODEL), PLE_DIM ** -0.5),
        'w_ple_gate': nrm(ks[24], (DEPTH, D_MODEL, D_MODEL), D_MODEL ** -0.5),
    }


def reference(x_prompt, x_sample, p_prompt, p_sample, state_ret, cache_fox_k, cache_fox_v, cache_fox_logf,
              norm_ffn1_g, ffn1_w_in, ffn1_w_out, norm_mix_g, w_in_mix, b_forget, q_norm_g, k_norm_g,
              w_br_ret, w_br_fox, w_out, norm_ffn2_g, ffn2_w_in, ffn2_w_out, norm_ple_g, w_ple, w_ple_gate):
    B, S, _ = x_prompt.shape
    DB, L, _ = x_sample.shape
    pos_prompt = jnp.arange(S, dtype=jnp.float32)
    pos_sample = PAST_LEN + jnp.arange(L, dtype=jnp.float32)
    xp, xs = x_prompt, x_sample
    rp_l, kp_l, vp_l, lp_l = [], [], [], []
    rs_l, ks_l, vs_l, ls_l = [], [], [], []
    for i in range(DEPTH):
        prm = {
            'norm_ffn1_g': norm_ffn1_g[i], 'ffn1_w_in': ffn1_w_in[i], 'ffn1_w_out': ffn1_w_out[i],
            'norm_mix_g': norm_mix_g[i], 'w_in_mix': w_in_mix[i], 'b_forget': b_forget[i],
            'q_norm_g': q_norm_g[i], 'k_norm_g': k_norm_g[i], 'w_br_ret': w_br_ret[i],
            'w_br_fox': w_br_fox[i], 'w_out': w_out[i], 'norm_ffn2_g': norm_ffn2_g[i],
            'ffn2_w_in': ffn2_w_in[i], 'ffn2_w_out': ffn2_w_out[i], 'norm_ple_g': norm_ple_g[i],
            'w_ple': w_ple[i], 'w_ple_gate': w_ple_gate[i],
        }
        s0_prompt = jnp.zeros((B, RET_HEADS, RET_KEY_DIM, RET_VAL_DIM), state_ret.dtype)
        xp, rp, kp, vp, lp = layer(xp, p_prompt[i], pos_prompt, s0_prompt, None, prm, CHUNK)
        xs, rs, ks_, vs, ls = layer(xs, p_sample[i], pos_sample, state_ret[i],
                                    (cache_fox_k[i], cache_fox_v[i], cache_fox_logf[i]), prm, L)
        rp_l.append(rp); kp_l.append(kp); vp_l.append(vp); lp_l.append(lp)
        rs_l.append(rs); ks_l.append(ks_); vs_l.append(vs); ls_l.append(ls)
    new_state_ret_prompt = jnp.stack(rp_l)
    new_fox_k_prompt = jnp.stack(kp_l)
    new_fox_v_prompt = jnp.stack(vp_l)
    new_fox_logf_prompt = jnp.stack(lp_l)
    new_state_ret_sample = jnp.stack(rs_l)
    new_fox_k_sample = jnp.stack(ks_l)
    new_fox_v_sample = jnp.stack(vs_l)
    new_fox_logf_sample = jnp.stack(ls_l)
    return (xp, xs, new_state_ret_prompt, new_fox_k_prompt, new_fox_v_prompt, new_fox_logf_prompt,
            new_state_ret_sample, new_fox_k_sample, new_fox_v_sample, new_fox_logf_sample)
```

```python
import contextlib
import numpy as np
import concourse.bass as bass
import concourse.mybir as mybir
from concourse.bass_utils import run_bass_kernel_spmd

F32 = mybir.dt.float32
BF16 = mybir.dt.bfloat16
AF = mybir.ActivationFunctionType
ALU = mybir.AluOpType
AX = mybir.AxisListType

NCORES = 8
D = 1024
S = 2048
NSEQ = 4
NSS = 2
L = 16
PAST = 4096
DFF = 2816
INC = 6664
EPS = 1e-6
G = 512
NG = S // G
SLOT = 4096
NSLOT = 3
NEG = -30000.0
ENGS = ("pe", "act", "dve", "pool", "sp")


class KB:
    def __init__(self, nc, es):
        self.nc = nc
        self.es = es
        self.streams = {e: [] for e in ENGS}
        self.sem = {}
        self.cnt = {}
        self.waited = {e: {} for e in ENGS}
        self.res = {}
        self.sbytes = 0
        for e in ENGS:
            self._mksem(e)

    def _mksem(self, key):
        if key not in self.sem:
            self.sem[key] = self.es.enter_context(self.nc.semaphore("s_" + str(key)))
            self.cnt[key] = 0
        return self.sem[key]

    def sb(self, name, shape, dt):
        n = 1
        for s in shape[1:]:
            n *= s
        self.sbytes += n * (4 if dt == F32 else 2)
        return self.es.enter_context(self.nc.sbuf_tensor(name, list(shape), dt))

    def ps(self, name, shape, dt):
        return self.es.enter_context(self.nc.psum_tensor(name, list(shape), dt))

    def _deps(self, eng, reads, writes, sreads):
        deps = {}

        def need(k, v):
            if v > deps.get(k, 0):
                deps[k] = v

        skip = eng if eng == "pe" else None
        for r in reads:
            st = self.res.get(r)
            if st and st["w"] and st["w"][0] != skip:
                need(*st["w"])
        for r in sreads:
            st = self.res.get(r)
            if st and st["w"]:
                need(*st["w"])
        for w in writes:
            st = self.res.get(w)
            if st:
                if st["w"] and st["w"][0] != skip:
                    need(*st["w"])
                for k, v in st["r"].items():
                    if k != skip:
                        need(k, v)
        waits = []
        wd = self.waited[eng]
        for k, v in deps.items():
            if wd.get(k, 0) < v:
                wd[k] = v
                waits.append((k, v))
        return waits

    def _mark(self, key, val, reads, writes):
        for r in reads:
            st = self.res.setdefault(r, {"w": None, "r": {}})
            st["r"][key] = val
        for w in writes:
            self.res[w] = {"w": (key, val), "r": {}}

    def op(self, eng, fn, reads=(), writes=(), sreads=()):
        waits = self._deps(eng, reads, writes, sreads)
        self.cnt[eng] += 1
        self.streams[eng].append((waits, fn, (eng, 1)))
        self._mark(eng, self.cnt[eng], tuple(reads) + tuple(sreads), writes)

    def dma(self, queue, out, in_, reads=(), writes=(), dkey=None, slow=False):
        self._mksem(dkey)
        waits = self._deps(queue, reads, writes, ())
        self.cnt[dkey] += 16
        if slow:
            fn = lambda e, o=out, i=in_: e.dma_start(out=o, in_=i, allow_slow_non_contiguous=True)
        else:
            fn = lambda e, o=out, i=in_: e.dma_start(out=o, in_=i)
        self.streams[queue].append((waits, fn, (dkey, 16)))
        self._mark(dkey, self.cnt[dkey], reads, writes)

    def barrier(self):
        for e in ENGS:
            waits = []
            for kk, v in self.cnt.items():
                if kk != e and v > 0 and self.waited[e].get(kk, 0) < v:
                    self.waited[e][kk] = v
                    waits.append((kk, v))
            self.streams[e].append((waits, None, None))

    def wait_all(self, eng, keys):
        waits = [(k, self.cnt[k]) for k in keys if self.cnt[k] > 0]
        self.streams[eng].append((waits, None, None))

    def replay(self):
        emap = {"pe": "tensor", "act": "scalar", "dve": "vector", "pool": "gpsimd", "sp": "sync"}
        with self.nc.Block() as block:
            for e in ENGS:
                stream = self.streams[e]
                if not stream:
                    continue

                def body(engine, stream=stream):
                    for waits, fn, inc in stream:
                        for kk, v in waits:
                            engine.wait_ge(self.sem[kk], v)
                        if fn is not None:
                            ins = fn(engine)
                            ins.then_inc(self.sem[inc[0]], inc[1])

                getattr(block, emap[e])(body)

    def mm(self, out, lhsT, rhs, start, stop, reads, writes):
        self.op("pe", lambda e: e.matmul(out, lhsT, rhs, start=start, stop=stop),
                reads=reads, writes=writes)

    def tr(self, out, in_, ident, reads, writes):
        self.op("pe", lambda e: e.transpose(out, in_, ident), reads=reads, writes=writes)

    def act(self, out, in_, func, reads, writes, sreads=(), bias=None, scale=None, accum_out=None):
        kw = {}
        if bias is not None:
            kw["bias"] = bias
        if scale is not None:
            kw["scale"] = scale
        if accum_out is not None:
            kw["accum_out"] = accum_out
        self.op("act", lambda e: e.activation(out=out, in_=in_, func=func, **kw),
                reads=reads, writes=writes, sreads=sreads)

    def cp(self, eng, out, in_, reads, writes):
        if eng == "act":
            self.op("act", lambda e: e.copy(out=out, in_=in_), reads=reads, writes=writes)
        else:
            self.op(eng, lambda e: e.tensor_copy(out=out, in_=in_), reads=reads, writes=writes)

    def tt(self, eng, out, in0, in1, op, reads, writes):
        self.op(eng, lambda e: e.tensor_tensor(out=out, in0=in0, in1=in1, op=op),
                reads=reads, writes=writes)

    def ts(self, eng, out, in0, s1, s2, op0, op1, reads, writes, sreads=()):
        if s2 is None:
            self.op(eng, lambda e: e.tensor_scalar(out=out, in0=in0, scalar1=s1, scalar2=None, op0=op0),
                    reads=reads, writes=writes, sreads=sreads)
        else:
            self.op(eng, lambda e: e.tensor_scalar(out=out, in0=in0, scalar1=s1, scalar2=s2, op0=op0, op1=op1),
                    reads=reads, writes=writes, sreads=sreads)

    def stt(self, eng, out, in0, scalar, in1, op0, op1, reads, writes, sreads=()):
        self.op(eng, lambda e: e.scalar_tensor_tensor(out=out, in0=in0, scalar=scalar, in1=in1,
                                                      op0=op0, op1=op1),
                reads=reads, writes=writes, sreads=sreads)


def _consts():
    c = {}
    c["ident"] = np.eye(128, dtype=np.float32)
    s = np.arange(128)
    c["tri"] = (s[:, None] <= s[None, :]).astype(np.float32)
    c["triS"] = (s[:, None] > s[None, :]).astype(np.float32)
    c["ones"] = np.ones((128, 128), np.float32)
    c["maskT"] = np.where(s[:, None] <= s[None, :], 0.0, NEG).astype(np.float32)
    blk = np.full(48, -1)
    blk[0:16] = 0
    blk[32:48] = 1
    pos = np.zeros(48)
    pos[0:16] = np.arange(16)
    pos[32:48] = np.arange(16)
    s48 = np.arange(48)
    same = (blk[:, None] == blk[None, :]) & (blk[:, None] >= 0)
    c["tri_s"] = np.zeros((128, 128), np.float32)
    c["tri_s"][:48, :48] = (same & (pos[:, None] <= pos[None, :])).astype(np.float32)
    c["mask_s"] = np.full((128, 128), NEG, np.float32)
    c["mask_s"][:48, :48] = np.where(same & (pos[:, None] <= pos[None, :]), 0.0, NEG)
    lg = np.log(1.0 - 2.0 ** (-5.0 - np.arange(4, dtype=np.float64)))
    i = np.arange(128)
    ch = i // 64
    DT = np.zeros((128, 4, 128), np.float64)
    for h in range(4):
        jj, ii = np.meshgrid(i, i, indexing="ij")
        samec = ch[jj] == ch[ii]
        later = (ch[ii] == 1) & (ch[jj] == 0)
        DT[:, h, :] = np.where(samec, np.exp(lg[h] * np.abs(ii - jj)),
                               np.where(later, np.exp(lg[h] * (ii - jj)), 0.0))
    KS = 128.0 ** -0.5
    c["DT"] = (DT * KS).astype(np.float32)
    idec = np.exp(lg[None, :, None] * (i[None, None, :] + 1.0))
    c["idec"] = np.broadcast_to(idec, (128, 4, 128)).astype(np.float32).copy()
    c["kdec"] = np.zeros((128, 8), np.float32)
    c["kdec"][:, 0:4] = np.exp(lg[None, :] * (127.0 - i[:, None])) * KS
    c["cdec"] = [float(np.exp(lg[h] * 128.0)) for h in range(4)]
    DTs = np.zeros((128, 4, 48), np.float64)
    for h in range(4):
        DTs[:48, h, :48] = np.where(same, np.exp(lg[h] * np.abs(pos[None, :] - pos[:, None])), 0.0)
    c["DT_s"] = (DTs * KS).astype(np.float32)
    for sq in range(2):
        t = np.zeros((128, 4, 48), np.float64)
        sel = blk == sq
        for h in range(4):
            t[:, h, :48] = np.where(sel, np.exp(lg[h] * (pos + 1.0)), 0.0)[None, :]
        c["idec_s%d" % sq] = t.astype(np.float32)
    kd = np.zeros((128, 8), np.float32)
    for sq in range(2):
        sel = blk == sq
        for h in range(4):
            kd[:48, sq * 4 + h] = np.where(sel, np.exp(lg[h] * (15.0 - pos)), 0.0) * KS
    c["kdec_s"] = kd
    c["cdec_s"] = [float(np.exp(lg[h] * 16.0)) for h in range(4)]
    half = 64
    freqs = (10000.0 ** (-np.arange(half, dtype=np.float32) / half)).astype(np.float32)
    posp = np.arange(S, dtype=np.float32)
    ang = (posp[:, None] * freqs[None, :]).astype(np.float32)
    cs = np.cos(ang).astype(np.float32).reshape(16, 128, 64).transpose(1, 0, 2)
    sn = np.sin(ang).astype(np.float32).reshape(16, 128, 64).transpose(1, 0, 2)
    c["cos"] = np.ascontiguousarray(cs)
    c["sin"] = np.ascontiguousarray(sn)
    poss = np.zeros(128, np.float32)
    poss[:48] = (PAST + pos).astype(np.float32)
    angs = (poss[:, None] * freqs[None, :]).astype(np.float32)
    c["cos_s"] = np.cos(angs).astype(np.float32)
    c["sin_s"] = np.sin(angs).astype(np.float32)
    return c


CONST_SHAPES = {
    "ident": [128, 128], "tri": [128, 128], "triS": [128, 128], "ones": [128, 128],
    "maskT": [128, 128], "tri_s": [128, 128], "mask_s": [128, 128],
    "DT": [128, 4, 128], "idec": [128, 4, 128], "kdec": [128, 8],
    "DT_s": [128, 4, 48], "idec_s0": [128, 4, 48], "idec_s1": [128, 4, 48], "kdec_s": [128, 8],
    "cos": [128, 16, 64], "sin": [128, 16, 64], "cos_s": [128, 64], "sin_s": [128, 64],
}
LATE_CONSTS = ("DT_s", "idec_s0", "idec_s1", "cos", "sin", "cos_s", "sin_s")

IN_SHAPES = {
    "x_p": [NSEQ, S, D], "p_p": [NSEQ, S, 256], "x_s": [NSS, L, D], "p_s": [NSS, L, 256],
    "st_in": [NSS, 4, 128, 256], "ck": [NSS, PAST, 512], "cv": [NSS, PAST, 512], "clf": [NSS, PAST, 8],
    "norm_ffn1_g": [D], "ffn1_w_in": [D, 2 * DFF], "ffn1_w_out": [DFF, D], "norm_mix_g": [D],
    "w_in_mix": [D, INC], "b_forget": [8], "q_norm_g": [64], "k_norm_g": [64],
    "w_br_ret": [D, D], "w_br_fox": [512, D], "w_out": [D, D], "norm_ffn2_g": [D],
    "ffn2_w_in": [D, 2 * DFF], "ffn2_w_out": [DFF, D], "norm_ple_g": [D], "w_ple": [256, D],
    "w_ple_gate": [D, D],
}
OUT_SHAPES = {
    "y_p": [NSEQ, S, D], "y_s": [NSS, L, D], "st_p": [NSEQ, 4, 128, 256], "k_p": [NSEQ, S, 512],
    "v_p": [NSEQ, S, 512], "lf_p": [NSEQ, S, 8], "st_s": [NSS, 4, 128, 256], "k_s": [NSS, L, 512],
    "v_s": [NSS, L, 512], "lf_s": [NSS, L, 8],
}


def build_program(nseq=NSEQ, do_sample=True, ngroups=NG):
    CT = _consts()
    nc = bass.Bass("TRN2", target_bir_lowering=False)
    I = {n: nc.dram_tensor(n, s, F32, kind="ExternalInput").ap() for n, s in IN_SHAPES.items()}
    C = {n: nc.dram_tensor("c_" + n, s, F32, kind="ExternalInput").ap() for n, s in CONST_SHAPES.items()}
    O = {n: nc.dram_tensor(n, s, F32, kind="ExternalOutput").ap() for n, s in OUT_SHAPES.items()}

    wv = {
        "f1a": I["ffn1_w_in"].rearrange("(kc p) n -> p kc n", p=128),
        "f1b": I["ffn1_w_out"].rearrange("(kc p) n -> p kc n", p=128),
        "mix": I["w_in_mix"].rearrange("(kc p) n -> p kc n", p=128),
        "brr": I["w_br_ret"].rearrange("(kc p) n -> p kc n", p=128),
        "brf": I["w_br_fox"].rearrange("(h p) n -> p h n", p=64),
        "wo": I["w_out"].rearrange("(kc p) n -> p kc n", p=128),
        "f2a": I["ffn2_w_in"].rearrange("(kc p) n -> p kc n", p=128),
        "f2b": I["ffn2_w_out"].rearrange("(kc p) n -> p kc n", p=128),
        "pg": I["w_ple_gate"].rearrange("(kc p) n -> p kc n", p=128),
        "pl": I["w_ple"].rearrange("(kc p) n -> p kc n", p=128),
    }
    chunks = []

    def ffn_chunks(tag, a, b, gname):
        for pc in range(11):
            chunks.append((f"{tag}a{pc}", [(wv[a], 128, 8, pc * 256, 256, gname, 0),
                                           (wv[a], 128, 8, DFF + pc * 256, 256, gname, 2048)]))
        for dc in range(8):
            chunks.append((f"{tag}b{dc}", [(wv[b], 128, 22, dc * 128, 128, None, 0)]))

    ffn_chunks("f1", "f1a", "f1b", "norm_ffn1_g")
    for cidx in range(9):
        chunks.append((f"mx{cidx}", [(wv["mix"], 128, 8, cidx * 512, 512, "norm_mix_g", 0)]))
    chunks.append(("mxff", [(wv["mix"], 128, 8, 4608, 8, "norm_mix_g", 0)]))
    for dc in range(8):
        chunks.append((f"mgA{dc}", [(wv["brr"], 128, 8, dc * 128, 128, None, 0),
                                    (wv["mix"], 128, 8, 4616 + dc * 128, 128, "norm_mix_g", 1024)]))
        chunks.append((f"mgB{dc}", [(wv["brf"], 64, 8, dc * 128, 128, None, 0),
                                    (wv["mix"], 128, 8, 4616 + 1024 + dc * 128, 128, "norm_mix_g", 1024)]))
    for cidx in range(2):
        chunks.append((f"wo{cidx}", [(wv["wo"], 128, 8, cidx * 512, 512, None, 0)]))
    ffn_chunks("f2", "f2a", "f2b", "norm_ffn2_g")
    for cidx in range(2):
        chunks.append((f"pg{cidx}", [(wv["pg"], 128, 8, cidx * 512, 512, "norm_ple_g", 0)]))
    chunks.append(("pl", [(wv["pl"], 128, 2, 0, 1024, None, 0)]))
    NCH = len(chunks)
    cidx_of = {name: i for i, (name, _) in enumerate(chunks)}
    wscr = nc.dram_tensor("wscr", [NCH, 128, SLOT], BF16).ap()

    with contextlib.ExitStack() as es:
        k = KB(nc, es)
        xT = k.sb("xT", [128, 8, G], F32)
        hT = k.sb("hT", [128, 8, G], BF16)
        R2 = k.sb("R2", [128, 4096], BF16)
        R1 = k.sb("R1", [128, 22 * G], BF16)
        rstd = k.sb("rstd", [128, G], F32)
        wring = [k.sb(f"wr{i}", [128, SLOT], BF16) for i in range(NSLOT)]
        io = [k.sb(f"io{i}", [128, 1024], F32) for i in range(2)]
        pst = [k.sb("pst0", [128, 256], F32)]
        pbf = k.sb("pbf", [128, 256], BF16)
        pT = k.sb("pT", [128, 2, G], BF16)
        cs = {n: k.sb("k_" + n, s, F32) for n, s in CONST_SHAPES.items() if n not in LATE_CONSTS}
        cosg = k.sb("cosg", [128, 4, 64], F32)
        sing = k.sb("sing", [128, 4, 64], F32)
        ident_b = k.sb("ident_b", [128, 128], BF16)
        ones_b = k.sb("ones_b", [128, 128], BF16)
        maskT_b = k.sb("maskT_b", [128, 128], BF16)
        masks_b = k.sb("masks_b", [128, 128], BF16)
        gq = k.sb("gq", [128, 64], F32)
        gk = k.sb("gk", [128, 64], F32)
        bfg = k.sb("bfg", [128, 8], F32)
        gcols = {n: k.sb("gc_" + n, [128, 8], F32) for n in
                 ("norm_ffn1_g", "norm_mix_g", "norm_ffn2_g", "norm_ple_g")}
        qa = [k.sb(f"qa{i}", [128, 8, 66], BF16) for i in range(2)]

        cqs = k.sb("cqs", [128, 4, 8], F32)
        ka = [k.sb(f"ka{i}", [128, 8, 66], BF16) for i in range(2)]
        qTt = [k.sb(f"qTt{i}", [128, 4, 128], BF16) for i in range(2)]
        qdTt = [k.sb(f"qdTt{i}", [128, 4, 128], BF16) for i in range(2)]
        kTt = [k.sb(f"kTt{i}", [128, 4, 128], BF16) for i in range(2)]
        attT = [k.sb(f"attT{i}", [128, 128], BF16) for i in range(4)]
        om = [k.sb("om0", [128, 1024], BF16)]
        omT = k.sb("omT", [128, 8, G], BF16)
        PT = [k.sb(f"PT{i}", [128, G], BF16) for i in range(3)]
        kTa = k.sb("kTa", [128, 8, S], BF16)
        va_flat = k.sb("va", [128, 16 * 8 * 66], BF16)
        va = va_flat[:, :].rearrange("p (a h c) -> p a h c", a=16, h=8)
        biasK = k.sb("biasK", [128, 16, 8], F32)
        state = [k.sb("state0", [128, 4, 256], F32)]
        state_b = [k.sb("stateb0", [128, 4, 256], BF16)]
        junk = sq512_ph = None
        sm = k.sb("sm", [128, 256], F32)
        rot = k.sb("rot", [128, 4, 256], F32)
        sq512 = k.sb("sq512", [128, 512], F32)
        kn32 = k.sb("kn32", [128, 512], F32)
        lfc = k.sb("lfc", [128, 8], F32)
        rden = rot[:, 0:2, :].rearrange("p a n -> p (a n)")
        bcs = rden
        qTa_own = k.sb("qTa", [128, 8, G], BF16)
        kTs = k.sb("kTs", [128, 8, 48], BF16)
        sgt = [k.sb(f"sgt{i}", [128, G], F32) for i in range(2)]
        junk = sq512[:, 0:256]
        junk4v = [(sgt[0][:, 0:256], "sgt0"), (sgt[0][:, 256:512], "sgt0"),
                  (sgt[1][:, 0:256], "sgt1"), (sgt[1][:, 256:512], "sgt1")]
        vaf = kTa[:, :, :].rearrange("p h n -> p (h n)")
        voff = [0]

        def carve(nel, dt):
            nb = nel * (2 if dt == F32 else 1)
            a = vaf[:, voff[0]: voff[0] + nb]
            voff[0] += nb
            assert voff[0] <= 8 * S
            return a.bitcast(F32) if dt == F32 else a

        clf_sb = carve(256, F32).rearrange("p (t h) -> p t h", h=8)
        suf = carve(256, F32).rearrange("p (t h) -> p t h", h=8)
        sufc = carve(264, F32).rearrange("p (t h) -> p t h", h=8)
        kcT = [carve(1024, BF16).rearrange("p (h n) -> p h n", h=8) for i in range(2)]
        vca = [carve(528, BF16).rearrange("p (h c) -> p h c", h=8) for i in range(2)]
        PTs = [carve(128, BF16).rearrange("p (h q) -> p h q", h=8) for i in range(2)]
        kd2 = carve(512, BF16)
        qdTt2 = carve(512, BF16).rearrange("p (h n) -> p h n", h=4)
        late = {n: carve(192, F32).rearrange("p (h n) -> p h n", h=4) for n in ("DT_s", "idec_s0", "idec_s1")}
        cstage = [(io[0][:, :], "io0"), (io[1][:, :], "io1")]
        for i_ in range(3):
            cstage.append((carve(1024, F32), f"cst{i_}"))
        for i_ in range(3):
            cstage.append((va_flat[:, 528 + i_ * 2048: 528 + (i_ + 1) * 2048].bitcast(F32), f"cst{3 + i_}"))
        cst_i = [0]
        state.append(carve(1024, F32).rearrange("p (h n) -> p h n", h=4))
        state_b.append(carve(1024, BF16).rearrange("p (h n) -> p h n", h=4))
        pf = [k.ps(f"pf{i}", [128, 512], F32) for i in range(6)]
        pb = [k.ps(f"pb{i}", [128, 1024], BF16) for i in range(2)]
        rr = {"f": 0, "b": 0, "w": 0, "io": 0}

        def nf():
            i = rr["f"] % 4
            rr["f"] += 1
            return pf[i], f"pf{i}"

        nf_all = nf

        def nb():
            i = rr["b"] % 2
            rr["b"] += 1
            return pb[i], f"pb{i}"

        def nio():
            i = rr["io"] % 2
            rr["io"] += 1
            return io[i], f"io{i}"

        sqT = R2[:, :].rearrange("p (a n) -> p a n", a=8)
        mT = sqT
        k_rot = R2[:, 0:2048].rearrange("p (a n) -> p a n", a=4)
        k_dec = R2[:, 2048:4096].rearrange("p (a n) -> p a n", a=4)
        actT = R1[:, :].rearrange("p (a n) -> p a n", a=22)
        v_tm = R1[:, 0:4096].rearrange("p (a n) -> p a n", a=4)
        sg_tm = R1[:, 4096:8192].rearrange("p (a n) -> p a n", a=4)
        q_rot = R1[:, 8192:10240].rearrange("p (a n) -> p a n", a=4)
        qTa = qTa_own
        ofT = R1[:, 4096:8192].rearrange("p (a n) -> p a n", a=8)

        for n in cs:
            k.dma("pool", cs[n][:], C[n], writes=["c_" + n], dkey="d_const")
        k.dma("pool", gq[:], I["q_norm_g"].partition_broadcast(128), writes=["gq"], dkey="d_const")
        k.dma("pool", gk[:], I["k_norm_g"].partition_broadcast(128), writes=["gk"], dkey="d_const")
        k.dma("pool", bfg[:], I["b_forget"].partition_broadcast(128), writes=["bfg"], dkey="d_const")
        for n in gcols:
            k.dma("pool", gcols[n][:], I[n].rearrange("(kc p) -> p kc", p=128), writes=["gc_" + n],
                  dkey="d_const", slow=True)
        for n in ["c_" + n for n in cs] + ["gc_" + n for n in gcols] + ["gq", "gk", "bfg"]:
            if n in k.res:
                k.res[n]["w"] = ("d_const", k.cnt["d_const"])
        k.ts("dve", gq[:], gq[:], 0.125, None, ALU.mult, None, ["gq"], ["gq"])
        k.cp("dve", ident_b[:], cs["ident"][:], ["c_ident"], ["ident_b"])
        k.cp("dve", ones_b[:], cs["ones"][:], ["c_ones"], ["ones_b"])
        k.cp("dve", maskT_b[:], cs["maskT"][:], ["c_maskT"], ["maskT_b"])
        k.cp("dve", masks_b[:], cs["mask_s"][:], ["c_mask_s"], ["masks_b"])
        for i in range(2):
            k.op("dve", lambda e, t=qa[i]: e.memset(t[:], 0.0), writes=[f"qa{i}"])
            k.op("dve", lambda e, t=ka[i]: e.memset(t[:], 1.0), writes=[f"ka{i}"])
        k.op("dve", lambda e: e.memset(va_flat[:, :], 1.0), writes=[f"va{T}" for T in range(16)])

        R1f = R1[:, :].bitcast(F32)
        kTaf = kTa[:, :, :].rearrange("p h n -> p (h n)").bitcast(F32)
        stg = [(xT[:, :, :].rearrange("p a n -> p (a n)"), "stg0"), (R1f, "stg1"),
               (kTaf[:, 0:4096], "stg2"), (kTaf[:, 4096:8192], "stg3")]
        seglist = []
        for ci, (cname, segs) in enumerate(chunks):
            for j, sg_ in enumerate(segs):
                seglist.append((ci, sg_, j == len(segs) - 1))
        ceng = ["dve", "act"]
        ce = [0]
        LA = 2

        def p_load(idx):
            ci, (src, npart, nkc, c0, w, gname, off), last = seglist[idx]
            st, stkey = stg[idx % 4]
            stv = st[0:npart, 0:nkc * w].rearrange("p (a n) -> p a n", a=nkc)
            k.dma("sp", stv, src[:, :, c0:c0 + w], writes=[stkey], dkey="d_" + stkey)

        def p_cast(idx):
            ci, (src, npart, nkc, c0, w, gname, off), last = seglist[idx]
            st, stkey = stg[idx % 4]
            slot, skey = wring[ci % NSLOT], f"wr{ci % NSLOT}"
            n_el = nkc * w
            stv = st[0:npart, 0:n_el].rearrange("p (a n) -> p a n", a=nkc)
            dst = slot[0:npart, off:off + n_el].rearrange("p (a n) -> p a n", a=nkc)
            if gname is None:
                eng = ceng[ce[0] % 2]
                ce[0] += 1
                k.cp(eng, dst, stv, [stkey], [skey])
            else:
                for kc in range(nkc):
                    eng = ceng[ce[0] % 2]
                    ce[0] += 1
                    gc = gcols[gname][:, kc:kc + 1]
                    if eng == "act":
                        k.act(dst[:, kc, :], stv[:, kc, :], AF.Copy, [stkey], [skey],
                              sreads=["gc_" + gname], scale=gc)
                    else:
                        k.ts(eng, dst[:, kc, :], stv[:, kc, :], gc, None, ALU.mult, None,
                             [stkey], [skey], sreads=["gc_" + gname])
            if last:
                k.dma("pool", wscr[ci], slot[:, :], reads=[skey], writes=[f"ws{ci}"], dkey="d_wst_" + skey)

        for idx in range(len(seglist) + LA):
            if idx < len(seglist):
                p_load(idx)
            if idx - LA >= 0:
                p_cast(idx - LA)
        k.barrier()

        def wload(cname):
            ci = cidx_of[cname]
            i = rr["w"] % NSLOT
            rr["w"] += 1
            k.dma("sp", wring[i][:, :], wscr[ci], reads=[f"ws{ci}"], writes=[f"wr{i}"],
                  dkey=f"d_wr{i}")
            return wring[i], f"wr{i}"

        def rmsnorm(n):
            ps, pk = nf()
            for kc in range(8):
                sk = "R2a" if kc < 4 else "R2b"
                k.act(sqT[:, kc, 0:n], xT[:, kc, 0:n], AF.Square, [f"xT{kc}"], [sk])
                k.mm(ps[:, 0:n], ones_b[:, :], sqT[:, kc, 0:n], kc == 0, kc == 7, ["ones_b", sk], [pk])
            k.act(rstd[:, 0:n], ps[:, 0:n], AF.Ln, [pk], ["rstd"], bias=EPS, scale=1.0 / D)
            k.act(rstd[:, 0:n], rstd[:, 0:n], AF.Exp, ["rstd"], ["rstd"], scale=-0.5)
            for kc in range(8):
                eng = "pool" if kc in (3, 6) else "dve"
                k.tt(eng, hT[:, kc, 0:n], xT[:, kc, 0:n], rstd[:, 0:n], ALU.mult, [f"xT{kc}", "rstd"], [f"hT{kc}"])

        def r1key(j):
            return "R1a" if j < 8 else ("R1b" if j < 16 else "R1c")

        def ffn(tag, n):
            rmsnorm(n)
            for pc in range(11):
                w, wk = wload(f"{tag}a{pc}")
                for sub in range(2):
                    j = pc * 2 + sub
                    gp, gkey = nf()
                    up, ukey = nf()
                    for kc in range(8):
                        k.mm(gp[:, 0:n], w[:, kc * 256 + sub * 128: kc * 256 + sub * 128 + 128],
                             hT[:, kc, 0:n], kc == 0, kc == 7, [wk, f"hT{kc}"], [gkey])
                    for kc in range(8):
                        k.mm(up[:, 0:n], w[:, 2048 + kc * 256 + sub * 128: 2048 + kc * 256 + sub * 128 + 128],
                             hT[:, kc, 0:n], kc == 0, kc == 7, [wk, f"hT{kc}"], [ukey])
                    sg = sgt[j % 2]
                    k.act(sg[:, 0:n], gp[:, 0:n], AF.Silu, [gkey], [f"sgt{j % 2}"])
                    k.tt("dve", actT[:, j, 0:n], sg[:, 0:n], up[:, 0:n], ALU.mult,
                         [f"sgt{j % 2}", ukey], [r1key(j)])
            for dc in range(8):
                w, wk = wload(f"{tag}b{dc}")
                yp, yk = nf()
                for j in range(22):
                    k.mm(yp[:, 0:n], w[:, j * 128:(j + 1) * 128], actT[:, j, 0:n], j == 0, j == 21,
                         [wk, r1key(j)], [yk])
                k.stt("dve", xT[:, dc, 0:n], yp[:, 0:n], 0.5, xT[:, dc, 0:n], ALU.mult, ALU.add,
                      [yk, f"xT{dc}"], [f"xT{dc}"])

        def load_group(xsrc_tiles, psrc_tiles, n):
            for (xparts, pparts, t0, np_) in zip_tiles(xsrc_tiles, psrc_tiles):
                xi, xk = nio()
                if len(xparts) > 1 or xparts[0][2] != 128:
                    k.op("pool", lambda e, t=xi: e.memset(t[:], 0.0), writes=[xk])
                for (ap, r0, nr) in xparts:
                    k.dma("pool", xi[r0:r0 + nr, :], ap, writes=[xk], dkey="d_" + xk)
                for half in range(2):
                    ps, pk = nf()
                    for q4 in range(4):
                        kc = half * 4 + q4
                        k.tr(ps[:, q4 * 128: q4 * 128 + np_], xi[0:np_, kc * 128:(kc + 1) * 128],
                             cs["ident"][0:np_, 0:np_], [xk, "c_ident"], [pk])
                    src = ps[:, :].rearrange("p (a n) -> p a n", a=4)[:, :, 0:np_]
                    k.cp("act" if half == 0 else "dve", xT[:, half * 4: half * 4 + 4, t0:t0 + np_], src,
                         [pk], [f"xT{half * 4 + q4}" for q4 in range(4)])
                pt, ptk = pst[0], "pst0"
                if len(pparts) > 1 or pparts[0][2] != 128:
                    k.op("pool", lambda e, t=pt: e.memset(t[:], 0.0), writes=[ptk])
                for (ap, r0, nr) in pparts:
                    k.dma("pool", pt[r0:r0 + nr, :], ap, writes=[ptk], dkey="d_" + ptk)
                k.cp("act", pbf[0:np_, :], pt[0:np_, :], [ptk], ["pbf"])
                bp, bk = nb()
                for kc in range(2):
                    k.tr(bp[:, kc * 128: kc * 128 + np_], pbf[0:np_, kc * 128:(kc + 1) * 128],
                         ident_b[0:np_, 0:np_], ["pbf", "ident_b"], [bk])
                src = bp[:, 0:256].rearrange("p (a n) -> p a n", a=2)[:, :, 0:np_]
                k.cp("dve", pT[:, :, t0:t0 + np_], src, [bk], ["pT"])

        def zip_tiles(xs, ps_):
            return [(x[0], p[0], x[1], x[2]) for x, p in zip(xs, ps_)]

        def store_group(ydst_tiles, n):
            for (parts, t0, np_) in ydst_tiles:
                yo, yk = nio()
                for half in range(2):
                    ps, pk = nf()
                    for q4 in range(4):
                        kc = half * 4 + q4
                        k.tr(ps[0:np_, q4 * 128:(q4 + 1) * 128], xT[:, kc, t0:t0 + np_],
                             cs["ident"][:, :], [f"xT{kc}", "c_ident"], [pk])
                    k.cp("act" if half == 0 else "dve", yo[0:np_, half * 512:(half + 1) * 512],
                         ps[0:np_, :], [pk], [yk])
                for (ap, r0, nr) in parts:
                    k.dma("pool", ap, yo[r0:r0 + nr, :], reads=[yk], dkey="d_" + yk)

        def ple(n):
            rmsnorm(n)
            wp, wpk = wload("pl")
            for cidx in range(2):
                w, wk = wload(f"pg{cidx}")
                for dl in range(4):
                    dc = cidx * 4 + dl
                    gp, gkey = nf()
                    pp, pkey = nf()
                    for kc in range(8):
                        k.mm(gp[:, 0:n], w[:, kc * 512 + dl * 128: kc * 512 + dl * 128 + 128],
                             hT[:, kc, 0:n], kc == 0, kc == 7, [wk, f"hT{kc}"], [gkey])
                    for kc in range(2):
                        k.mm(pp[:, 0:n], wp[:, kc * 1024 + dc * 128: kc * 1024 + dc * 128 + 128],
                             pT[:, kc, 0:n], kc == 0, kc == 1, [wpk, "pT"], [pkey])
                    sg = sgt[dc % 2]
                    k.act(sg[:, 0:n], gp[:, 0:n], AF.Sigmoid, [gkey], [f"sgt{dc % 2}"])
                    k.tt("dve", sg[:, 0:n], sg[:, 0:n], pp[:, 0:n], ALU.mult, [f"sgt{dc % 2}", pkey],
                         [f"sgt{dc % 2}"])
                    k.tt("pool", xT[:, dc, 0:n], xT[:, dc, 0:n], sg[:, 0:n], ALU.add,
                         [f"xT{dc}", f"sgt{dc % 2}"], [f"xT{dc}"])

        def rotary(ps, pk, dst, cos_ap, sin_ap, np_, dkey_):
            x = ps[0:np_, :].rearrange("p (h t d) -> p h t d", h=4, t=2)
            x1 = x[:, :, 0, :]
            x2 = x[:, :, 1, :]
            cb = cos_ap.unsqueeze(1).broadcast_to([np_, 4, 64])
            sb_ = sin_ap.unsqueeze(1).broadcast_to([np_, 4, 64])
            t = [rot[0:np_, i, :].rearrange("p (h d) -> p h d", h=4) for i in range(4)]
            d = dst.rearrange("p (h t d) -> p h t d", h=4, t=2)
            k.tt("dve", t[0], x1, cb, ALU.mult, [pk, "cosg"], ["rot0"])
            k.tt("dve", t[1], x2, sb_, ALU.mult, [pk, "sing"], ["rot1"])
            k.tt("dve", t[2], x1, sb_, ALU.mult, [pk, "sing"], ["rot2"])
            k.tt("dve", t[3], x2, cb, ALU.mult, [pk, "cosg"], ["rot3"])
            k.tt("pool", d[:, :, 0, :], t[0], t[1], ALU.subtract, ["rot0", "rot1"], [dkey_])
            k.tt("pool", d[:, :, 1, :], t[2], t[3], ALU.add, ["rot2", "rot3"], [dkey_])

        def qknorm(ps, pk, np_, gtab, gkey):
            k.act(sq512[0:np_, :], ps[0:np_, :], AF.Square, [pk], ["sq512"])
            k.op("dve", lambda e: e.reduce_sum(out=sm[0:np_, 0:8],
                                               in_=sq512[0:np_, :].rearrange("p (h d) -> p h d", h=8),
                                               axis=AX.X), reads=["sq512"], writes=["sm"])
            k.act(sm[0:np_, 0:8], sm[0:np_, 0:8], AF.Sqrt, ["sm"], ["sm"], bias=EPS, scale=1.0 / 64)
            k.op("dve", lambda e: e.reciprocal(out=sm[0:np_, 0:8], in_=sm[0:np_, 0:8]),
                 reads=["sm"], writes=["sm"])
            k3 = kn32[0:np_, :].rearrange("p (h d) -> p h d", h=8)
            k.tt("dve", k3, ps[0:np_, :].rearrange("p (h d) -> p h d", h=8),
                 sm[0:np_, 0:8].unsqueeze(2).broadcast_to([np_, 8, 64]), ALU.mult, [pk, "sm"], ["kn32"])
            k.tt("dve", k3, k3, gtab[0:np_, :].unsqueeze(1).broadcast_to([np_, 8, 64]), ALU.mult,
                 ["kn32", gkey], ["kn32"])
            return k3

        def mixer(n, tiles, ret_cfg, fox_fn, interleave):
            nt = len(tiles)
            rmsnorm(n)
            def proj_gen(cname, width, evac, fb=nf):
                w, wk = wload(cname)
                for ti, (t0, np_, cA, sA) in enumerate(tiles):
                    ps, pk = fb()
                    for kc in range(8):
                        k.mm(ps[0:np_, 0:width], hT[:, kc, t0:t0 + np_], w[:, kc * width:(kc + 1) * width],
                             kc == 0, kc == 7, [f"hT{kc}", wk], [pk])
                    evac(ti, t0, np_, cA, sA, ps, pk)
                    yield

            def proj(cname, width, evac):
                for _ in proj_gen(cname, width, evac):
                    pass

            def mk_rot(lst):
                st_ = [0]

                def fb():
                    i = lst[st_[0] % len(lst)]
                    st_[0] += 1
                    return i
                return fb

            proj("mx0", 512, lambda ti, t0, np_, cA, sA, ps, pk:
                 rotary(ps, pk, q_rot[0:np_, ti, :], cA, sA, np_, "R1c"))
            proj("mx1", 512, lambda ti, t0, np_, cA, sA, ps, pk:
                 rotary(ps, pk, k_rot[0:np_, ti, :], cA, sA, np_, "R2a"))
            for half in range(2):
                proj(f"mx{2 + half}", 512, lambda ti, t0, np_, cA, sA, ps, pk, half=half:
                     k.cp("act", v_tm[0:np_, ti, half * 512:(half + 1) * 512], ps[0:np_, :], [pk], ["R1a"]))
            for half in range(2):
                proj(f"mx{4 + half}", 512, lambda ti, t0, np_, cA, sA, ps, pk, half=half:
                     k.act(sg_tm[0:np_, ti, half * 512:(half + 1) * 512], ps[0:np_, :], AF.Silu, [pk], ["R1b"]))
            pend_om = []

            def retention_gen(nf, nb):
              for ti, (t0, np_, cA, sA) in enumerate(tiles):
                cfgs = ret_cfg[ti]
                b = ti % 2
                bp, bk = nb()
                for h in range(4):
                    k.tr(bp[:, h * 128: h * 128 + np_], q_rot[0:np_, ti, h * 128:(h + 1) * 128],
                         ident_b[0:np_, 0:np_], ["R1c", "ident_b"], [bk])
                qps = bp[:, 0:512].rearrange("p (h n) -> p h n", h=4)[:, :, 0:np_]
                k.cp("act", qTt[b][:, :, 0:np_], qps, [bk], [f"qTt{b}"])
                qd_list = []
                for si, cfg in enumerate(cfgs):
                    qd = qdTt[b] if si == 0 else qdTt2
                    qdk = f"qdTt{b}" if si == 0 else "qdTt2"
                    k.tt("dve", qd[:, :, 0:np_], qTt[b][:, :, 0:np_], cs[cfg["idec"]][:, :, 0:np_], ALU.mult,
                         [f"qTt{b}", "c_" + cfg["idec"]], [qdk])
                    qd_list.append((qd, qdk))
                for si, cfg in enumerate(cfgs):
                    for h in range(4):
                        col = cfg["kdec_col"] + h
                        dstk = k_dec[0:np_, ti, h * 128:(h + 1) * 128] if si == 0 else \
                            kd2[0:np_, h * 128:(h + 1) * 128]
                        k.ts("pool", dstk, k_rot[0:np_, ti, h * 128:(h + 1) * 128],
                             cs[cfg["kdec"]][0:np_, col:col + 1], None, ALU.mult, None,
                             ["R2a"], ["R2b" if si == 0 else "kd2"], sreads=["c_" + cfg["kdec"]])
                yield
                bp2, bk2 = nb()
                for h in range(4):
                    k.tr(bp2[:, h * 128: h * 128 + np_], k_rot[0:np_, ti, h * 128:(h + 1) * 128],
                         ident_b[0:np_, 0:np_], ["R2a", "ident_b"], [bk2])
                k.cp("act", kTt[b][:, :, 0:np_],
                     bp2[:, 0:512].rearrange("p (h n) -> p h n", h=4)[:, :, 0:np_], [bk2], [f"kTt{b}"])
                omt, omk = om[0], "om0"
                nh = 4
                ap_, apk = nf()
                for h in range(nh):
                    k.mm(ap_[0:np_, h * 128: h * 128 + np_], kTt[b][:, h, 0:np_], qTt[b][:, h, 0:np_], True, True,
                         [f"kTt{b}", f"qTt{b}"], [apk])
                yield
                for h in range(nh):
                    k.tt("dve", attT[h][0:np_, 0:np_], ap_[0:np_, h * 128: h * 128 + np_],
                         cs[cfgs[0]["DT"]][0:np_, h, 0:np_], ALU.mult, [apk, "c_" + cfgs[0]["DT"]], [f"attT{h}"])
                yield
                obank = {}
                for h in range(nh):
                    if h % 2 == 0:
                        ob, obk = pf[4 + h // 2], f"pf{4 + h // 2}"
                    obank[h] = (ob[0:np_, (h % 2) * 256:(h % 2) * 256 + 256], obk)
                    op_, opk = obank[h]
                    for si, cfg in enumerate(cfgs):
                        qd, qdk = qd_list[si]
                        k.mm(op_, qd[:, h, 0:np_], cfg["state_b"][:, h, :], si == 0, False,
                             [qdk, cfg["sbkey"]], [opk])
                    k.mm(op_, attT[h][0:np_, 0:np_], v_tm[0:np_, ti, h * 256:(h + 1) * 256],
                         False, True, [f"attT{h}", "R1a"], [opk])
                sbank = {}
                cnt_s = 0
                for h in range(nh):
                    for si, cfg in enumerate(cfgs):
                        if cnt_s % 2 == 0:
                            sb2, sbk2 = nf()
                        sp_ = sb2[:, (cnt_s % 2) * 256:(cnt_s % 2) * 256 + 256]
                        cnt_s += 1
                        kd_ap = k_dec[0:np_, ti, h * 128:(h + 1) * 128] if si == 0 else \
                            kd2[0:np_, h * 128:(h + 1) * 128]
                        k.mm(sp_, kd_ap, v_tm[0:np_, ti, h * 256:(h + 1) * 256], True, True,
                             ["R2b" if si == 0 else "kd2", "R1a"], [sbk2])
                        sbank[(h, si)] = (sp_, sbk2)
                if pend_om:
                    pend_om.pop(0)()
                yield
                for h in range(nh):
                    for si, cfg in enumerate(cfgs):
                        sp_, spk = sbank[(h, si)]
                        k.stt("dve", cfg["state"][:, h, :], cfg["state"][:, h, :], cfg["cdec"][h],
                              sp_, ALU.mult, ALU.add, [cfg["skey"], spk], [cfg["skey"]])
                        k.cp("pool", cfg["state_b"][:, h, :], cfg["state"][:, h, :], [cfg["skey"]],
                             [cfg["sbkey"]])
                yield
                for h in range(nh):
                    op_, opk = obank[h]
                    k.act(junk4v[h][0][0:np_, :], op_, AF.Square, [opk], [junk4v[h][1]])
                for h in range(nh):
                    k.op("dve", lambda e, h=h, np_=np_: e.reduce_sum(out=sm[0:np_, 16 + h:17 + h],
                                                                    in_=junk4v[h][0][0:np_, :], axis=AX.X),
                         reads=[junk4v[h][1]], writes=["smo"])
                if nh:
                    k.act(sm[0:np_, 16:20], sm[0:np_, 16:20], AF.Sqrt, ["smo"], ["smo"], bias=EPS, scale=1.0 / 256)
                    k.op("dve", lambda e, np_=np_: e.reciprocal(out=sm[0:np_, 16:20], in_=sm[0:np_, 16:20]),
                         reads=["smo"], writes=["smo"])
                for h in range(nh):
                    op_, opk = obank[h]
                    k.stt("dve", omt[0:np_, h * 256:(h + 1) * 256], op_,
                          sm[0:np_, 16 + h:17 + h], sg_tm[0:np_, ti, h * 256:(h + 1) * 256],
                          ALU.mult, ALU.mult, [opk, "R1b"], [omk], sreads=["smo"])

                def om_tr(t0=t0, np_=np_, omt=omt, omk=omk):
                    bp3, bk3 = nb()
                    for kc in range(8):
                        k.tr(bp3[:, kc * 128: kc * 128 + np_], omt[0:np_, kc * 128:(kc + 1) * 128],
                             ident_b[0:np_, 0:np_], [omk, "ident_b"], [bk3])
                    k.cp("act", omT[:, :, t0:t0 + np_],
                         bp3[:, :].rearrange("p (a n) -> p a n", a=8)[:, :, 0:np_], [bk3], ["omT"])
                pend_om.append(om_tr)
                yield
              while pend_om:
                pend_om.pop(0)()
            pbf = [pb[0][:, :].bitcast(F32), pb[1][:, :].bitcast(F32)]

            def merge1_gen():
                for dc in range(8):
                    w, wk = wload(f"mgA{dc}")
                    for kc in range(8):
                        k.mm(pbf[0][:, 0:n], w[:, kc * 128:(kc + 1) * 128], omT[:, kc, 0:n], kc == 0, kc == 7,
                             [wk, "omT"], ["pb0"])
                        yield
                    for kc in range(8):
                        k.mm(pbf[1][:, 0:n], w[:, 1024 + kc * 128: 1024 + (kc + 1) * 128], hT[:, kc, 0:n],
                             kc == 0, kc == 7, [wk, f"hT{kc}"], ["pb1"])
                        yield
                    sg = sgt[dc % 2]
                    sk = f"sgt{dc % 2}"
                    k.act(sg[:, 0:n], pbf[1][:, 0:n], AF.Exp, ["pb1"], [sk], scale=-1.0)
                    k.ts("dve", sg[:, 0:n], sg[:, 0:n], 1.0, None, ALU.add, None, [sk], [sk])
                    k.op("dve", lambda e, sg=sg: e.reciprocal(out=sg[:, 0:n], in_=sg[:, 0:n]),
                         reads=[sk], writes=[sk])
                    k.tt("dve", mT[:, dc, 0:n], sg[:, 0:n], pbf[0][:, 0:n], ALU.mult, [sk, "pb0"],
                         ["R2a" if dc < 4 else "R2b"])
                    yield

            filler = merge1_gen()
            if interleave:
                fb_rf = mk_rot([(pf[2], "pf2"), (pf[3], "pf3")])
                fb_rb = mk_rot([(pb[0], "pb0")])
                fb_ff = mk_rot([(pf[0], "pf0"), (pf[1], "pf1")])
                fb_fb = mk_rot([(pb[1], "pb1")])
            else:
                fb_rf, fb_rb, fb_ff, fb_fb = nf, nb, nf, nb
            gen_r = retention_gen(fb_rf, fb_rb)
            gen_f = fox_fn(proj_gen, filler, fb_ff, fb_fb)
            if interleave:
                r_alive, f_alive = True, True
                while r_alive or f_alive:
                    if f_alive:
                        v_ = next(gen_f, "END")
                        if v_ == "ATT" or v_ == "END":
                            f_alive = False
                    if r_alive:
                        r_alive = next(gen_r, "END") != "END"
            else:
                for _ in gen_r:
                    pass
            for _ in gen_f:
                pass
            for _ in filler:
                pass
            for dc in range(8):
                w, wk = wload(f"mgB{dc}")
                bfp, bfk = nf()
                for h in range(8):
                    k.mm(bfp[:, 0:n], w[0:64, h * 128:(h + 1) * 128], ofT[0:64, h, 0:n],
                         h == 0, h == 7, [wk, "R1b"], [bfk])
                gfp, gfk = nf()
                for kc in range(8):
                    k.mm(gfp[:, 0:n], w[:, 1024 + kc * 128: 1024 + (kc + 1) * 128], hT[:, kc, 0:n],
                         kc == 0, kc == 7, [wk, f"hT{kc}"], [gfk])
                sg2 = sgt[dc % 2]
                sk2 = f"sgt{dc % 2}"
                mkey = "R2a" if dc < 4 else "R2b"
                k.act(sg2[:, 0:n], gfp[:, 0:n], AF.Sigmoid, [gfk], [sk2])
                k.tt("dve", sg2[:, 0:n], sg2[:, 0:n], bfp[:, 0:n], ALU.mult, [sk2, bfk], [sk2])
                k.tt("pool", mT[:, dc, 0:n], mT[:, dc, 0:n], sg2[:, 0:n], ALU.add, [mkey, sk2], [mkey])
            for cidx in range(2):
                w, wk = wload(f"wo{cidx}")
                for dl in range(4):
                    dc = cidx * 4 + dl
                    yp, yk = nf()
                    for kc in range(8):
                        k.mm(yp[:, 0:n], w[:, kc * 512 + dl * 128: kc * 512 + dl * 128 + 128], mT[:, kc, 0:n],
                             kc == 0, kc == 7, [wk, "R2a" if kc < 4 else "R2b"], [yk])
                    k.tt("dve", xT[:, dc, 0:n], xT[:, dc, 0:n], yp[:, 0:n], ALU.add, [f"xT{dc}", yk], [f"xT{dc}"])

        def logf_tile(ps, pk, np_, dst_sm):
            u = sm[0:np_, 32:40]
            nu = sm[0:np_, 40:48]
            k.tt("dve", u, ps[0:np_, 0:8], bfg[0:np_, :], ALU.add, [pk, "bfg"], ["smu"])
            k.ts("dve", nu, u, -1.0, None, ALU.mult, None, ["smu"], ["smnu"])
            k.tt("dve", nu, nu, u, ALU.min, ["smnu", "smu"], ["smnu"])
            k.act(nu, nu, AF.Exp, ["smnu"], ["smnu"])
            k.act(nu, nu, AF.Ln, ["smnu"], ["smnu"], bias=1.0)
            k.ts("dve", u, u, 0.0, None, ALU.min, None, ["smu"], ["smu"])
            k.tt("dve", sm[0:np_, dst_sm:dst_sm + 8], u, nu, ALU.subtract, ["smu", "smnu"], ["smlf"])

        def fox_prompt_factory(seq, g):
            def fox(proj_gen, filler, nf, nb):
                n = G
                tiles_g = [g * 4 + ti for ti in range(4)]

                def ev_f(ti, t0, np_, cA, sA, ps, pk):
                    T = tiles_g[ti]
                    logf_tile(ps, pk, np_, 48)
                    lf = sm[0:np_, 48:56]
                    k.cp("dve", sm[0:np_, 64 + ti * 8: 72 + ti * 8], lf, ["smlf"], [f"lfo{ti}"])
                    k.dma("pool", O["lf_p"][seq, T * 128:(T + 1) * 128, :], sm[0:np_, 64 + ti * 8: 72 + ti * 8],
                          reads=[f"lfo{ti}"], dkey=f"d_lfo{ti}")
                    cp_, cpk = nf()
                    k.mm(cp_[0:np_, 0:8], cs["tri"][0:np_, 0:np_], lf, True, True, ["c_tri", "smlf"], [cpk])
                    tp_, tpk = nf()
                    k.mm(tp_[:, 0:8], cs["ones"][0:np_, :], lf, True, True, ["c_ones", "smlf"], [tpk])
                    cq = cqs[0:np_, ti, :]
                    k.tt("dve", cq, cp_[0:np_, 0:8], lfc[0:np_, :], ALU.add, [cpk, "lfc"], [f"cq{ti}"])
                    k.tt("dve", lfc[:, :], lfc[:, :], tp_[:, 0:8], ALU.add, ["lfc", tpk], ["lfc"])
                    k.ts("dve", biasK[0:np_, T, :], cq, -1.0, None, ALU.mult, None, [f"cq{ti}"], [f"bK{T}"])

                def ev_q(ti, t0, np_, cA, sA, ps, pk):
                    k3 = qknorm(ps, pk, np_, gq, "gq")
                    b = ti % 2
                    k.cp("act", qa[b][0:np_, :, 0:64], k3, ["kn32"], [f"qa{b}"])
                    k.cp("dve", qa[b][0:np_, :, 64:65], cqs[0:np_, ti, :].unsqueeze(2), [f"cq{ti}"], [f"qa{b}"])
                    bp, bk = nb()
                    for h in range(8):
                        k.tr(bp[0:65, h * 128:(h + 1) * 128], qa[b][0:np_, h, 0:65], ident_b[:, :],
                             [f"qa{b}", "ident_b"], [bk])
                    k.cp("act", qTa[0:65, :, t0:t0 + np_],
                         bp[0:65, :].rearrange("p (h n) -> p h n", h=8), [bk], ["qTa"])

                def ev_k(ti, t0, np_, cA, sA, ps, pk):
                    k3 = qknorm(ps, pk, np_, gk, "gk")
                    b = ti % 2
                    k.cp("act", ka[b][0:np_, :, 0:64], k3, ["kn32"], [f"ka{b}"])
                    T = tiles_g[ti]
                    k.dma("pool", O["k_p"][seq, T * 128:(T + 1) * 128, :], kn32[0:np_, :], reads=["kn32"],
                          dkey="d_kn32")
                    bp, bk = nb()
                    for h in range(8):
                        k.tr(bp[0:65, h * 128:(h + 1) * 128], ka[b][0:np_, h, 0:65], ident_b[:, :],
                             [f"ka{b}", "ident_b"], [bk])
                    k.cp("dve", kTa[0:65, :, T * 128:(T + 1) * 128],
                         bp[0:65, :].rearrange("p (h n) -> p h n", h=8), [bk], [f"kTa{T}"])

                def ev_v(ti, t0, np_, cA, sA, ps, pk):
                    T = tiles_g[ti]
                    vo, vk = nio()
                    k.cp("act", vo[0:np_, 0:512], ps[0:np_, :], [pk], [vk])
                    k.dma("pool", O["v_p"][seq, T * 128:(T + 1) * 128, :], vo[0:np_, 0:512], reads=[vk],
                          dkey="d_" + vk)
                    k.cp("dve", va[0:np_, T, :, 0:64], vo[0:np_, 0:512].rearrange("p (h d) -> p h d", h=8),
                         [vk], [f"va{T}"])

                yield from proj_gen("mxff", 8, ev_f, nf)
                yield from proj_gen("mx6", 512, ev_q, nf)
                yield from proj_gen("mx7", 512, ev_k, nf)
                yield from proj_gen("mx8", 512, ev_v, nf)
                yield "ATT"
                nf = nf_all
                nkt = g * 4 + 4
                steps = [(h, kt) for h in range(8) for kt in range(nkt)]
                Sinfo = {}

                def emit_S(i):
                    h, kt = steps[i]
                    sp_, spk = nf()
                    lhs = kTa[0:65, h, kt * 128:(kt + 1) * 128]
                    if kt < g * 4:
                        q0 = 0
                        k.mm(sp_[:, 0:G], lhs, qTa[0:65, h, 0:G], True, True, [f"kTa{kt}", "qTa"], [spk])
                    else:
                        q0 = (kt - g * 4) * 128
                        k.mm(sp_[:, q0:q0 + 128], lhs, qTa[0:65, h, q0:q0 + 128], True, False,
                             [f"kTa{kt}", "qTa"], [spk])
                        k.mm(sp_[:, q0:q0 + 128], ident_b[:, :], maskT_b[:, :], False, True,
                             ["ident_b", "maskT_b"], [spk])
                        if q0 + 128 < G:
                            k.mm(sp_[:, q0 + 128:G], lhs, qTa[0:65, h, q0 + 128:G], True, True,
                                 [f"kTa{kt}", "qTa"], [spk])
                    Sinfo[i] = (sp_, spk, q0)

                def finalize(h):
                    acc, acck = pf[4 + h % 2], f"pf{4 + h % 2}"
                    k.op("dve", lambda e, acc=acc: e.reciprocal(out=rden[64:65, :], in_=acc[64:65, :]),
                         reads=[acck], writes=["rden", "rot0", "rot1"])
                    bc, bck = nf()
                    k.mm(bc[0:64, :], cs["ones"][64:65, 0:64], rden[64:65, :], True, True, ["c_ones", "rden"],
                         [bck])
                    k.cp("act", bcs[0:64, :], bc[0:64, :], [bck], ["bcs", "rot0", "rot1"])
                    k.tt("dve", ofT[0:64, h, :], acc[0:64, :], bcs[0:64, :], ALU.mult, [acck, "bcs"], ["R1b"])

                pending = []
                nfill = -(-140 // max(len(steps), 1))
                for i, (h, kt) in enumerate(steps):
                    for _ in range(nfill):
                        next(filler, None)
                    if i == 0:
                        emit_S(0)
                    if i + 1 < len(steps):
                        emit_S(i + 1)
                    sp_, spk, q0 = Sinfo.pop(i)
                    acc, acck = pf[4 + h % 2], f"pf{4 + h % 2}"
                    pt, ptk = PT[i % 3], f"PT{i % 3}"
                    k.act(pt[:, q0:G], sp_[:, q0:G], AF.Exp, [spk], [ptk], sreads=[f"bK{kt}"],
                          bias=biasK[:, kt, h:h + 1])
                    k.mm(acc[0:65, q0:G], va[:, kt, h, 0:65], pt[:, q0:G], kt == 0, kt == nkt - 1,
                         [f"va{kt}", ptk], [acck])
                    if kt == nkt - 1:
                        pending.append((i + 2, h))
                    while pending and pending[0][0] <= i:
                        finalize(pending.pop(0)[1])
                for _, h in pending:
                    finalize(h)
            return fox

        k.op("dve", lambda e: e.memset(lfc[:, :], 0.0), writes=["lfc"])
        for seq in range(nseq):
            k.op("pool", lambda e: e.memset(state[0][:], 0.0), writes=["state0"])
            k.op("pool", lambda e: e.memset(state_b[0][:], 0.0), writes=["stateb0"])
            k.op("dve", lambda e: e.memset(lfc[:, :], 0.0), writes=["lfc"])
            for g in range(ngroups):
                n = G
                xs, ps_, ys = [], [], []
                tiles = []
                for ti in range(4):
                    r0 = g * G + ti * 128
                    xs.append(([(I["x_p"][seq, r0:r0 + 128, :], 0, 128)], ti * 128, 128))
                    ps_.append(([(I["p_p"][seq, r0:r0 + 128, :], 0, 128)], ti * 128, 128))
                    ys.append(([(O["y_p"][seq, r0:r0 + 128, :], 0, 128)], ti * 128, 128))
                    tiles.append((ti * 128, 128, cosg[:, ti, :], sing[:, ti, :]))
                k.dma("pool", cosg[:, :, :], C["cos"][:, g * 4:(g + 1) * 4, :], writes=["cosg"], dkey="d_cosg")
                k.dma("pool", sing[:, :, :], C["sin"][:, g * 4:(g + 1) * 4, :], writes=["sing"], dkey="d_sing")
                load_group(xs, ps_, n)
                ffn("f1", n)
                cfg = {"idec": "idec", "kdec": "kdec", "kdec_col": 0, "DT": "DT", "state": state[0],
                       "state_b": state_b[0], "skey": "state0", "sbkey": "stateb0", "cdec": CT["cdec"]}
                mixer(n, tiles, [[cfg]] * 4, fox_prompt_factory(seq, g), True)
                ffn("f2", n)
                ple(n)
                store_group(ys, n)
            for h in range(4):
                k.dma("pool", O["st_p"][seq, h], state[0][:, h, :], reads=["state0"], dkey="d_stout")

        if do_sample:
            k.barrier()
            n = 48
            for nm in ("DT_s", "idec_s0", "idec_s1"):
                cs[nm] = late[nm]
                k.dma("pool", late[nm], C[nm], writes=["c_" + nm], dkey="d_late_" + nm)
            k.dma("pool", cosg[:, 0, :], C["cos_s"], writes=["cosg"], dkey="d_cosg")
            k.dma("pool", sing[:, 0, :], C["sin_s"], writes=["sing"], dkey="d_sing")
            for i in range(2):
                k.op("dve", lambda e, t=vca[i]: e.memset(t, 1.0), writes=[f"vca{i}"])
            xs = [([(I["x_s"][0], 0, 16), (I["x_s"][1], 32, 16)], 0, 48)]
            ps_ = [([(I["p_s"][0], 0, 16), (I["p_s"][1], 32, 16)], 0, 48)]
            ys = [([(O["y_s"][0], 0, 16), (O["y_s"][1], 32, 16)], 0, 48)]
            tiles = [(0, 48, cosg[0:48, 0, :], sing[0:48, 0, :])]
            for sq in range(2):
                for h in range(4):
                    k.dma("pool", state[sq][:, h, :], I["st_in"][sq, h], writes=[f"state{sq}"],
                          dkey=f"d_stin{sq}")
                k.cp("dve", state_b[sq][:, :, :], state[sq][:, :, :], [f"state{sq}"], [f"stateb{sq}"])
            load_group(xs, ps_, n)
            ffn("f1", n)
            cfgs = [{"idec": f"idec_s{sq}", "kdec": "kdec_s", "kdec_col": sq * 4, "DT": "DT_s",
                     "state": state[sq], "state_b": state_b[sq], "skey": f"state{sq}",
                     "sbkey": f"stateb{sq}", "cdec": CT["cdec_s"]} for sq in range(2)]

            def fox_sample(proj_gen, filler, nf, nb):
                np_ = 48

                def ev_f(ti, t0, np_, cA, sA, ps, pk):
                    logf_tile(ps, pk, np_, 48)
                    lf = sm[0:np_, 48:56]
                    k.cp("dve", sm[0:np_, 64:72], lf, ["smlf"], ["lfo0"])
                    for sq in range(2):
                        k.dma("pool", O["lf_s"][sq], sm[sq * 32: sq * 32 + 16, 64:72], reads=["lfo0"],
                              dkey="d_lfo0")
                    cp_, cpk = nf()
                    k.mm(cp_[0:np_, 0:8], cs["tri_s"][0:np_, 0:np_], lf, True, True, ["c_tri_s", "smlf"], [cpk])
                    cq = cqs[0:np_, 0, :]
                    k.cp("dve", cq, cp_[0:np_, 0:8], [cpk], ["cq0"])
                    k.ts("dve", biasK[0:np_, 0, :], cq, -1.0, None, ALU.mult, None, ["cq0"], ["bK0"])

                def ev_q(ti, t0, np_, cA, sA, ps, pk):
                    k3 = qknorm(ps, pk, np_, gq, "gq")
                    k.op("dve", lambda e: e.memset(qa[0][:], 0.0), writes=["qa0"])
                    k.cp("act", qa[0][0:np_, :, 0:64], k3, ["kn32"], ["qa0"])
                    k.cp("dve", qa[0][0:np_, :, 64:65], cqs[0:np_, 0, :].unsqueeze(2), ["cq0"], ["qa0"])
                    bp, bk = nb()
                    for h in range(8):
                        k.tr(bp[0:65, h * 128: h * 128 + np_], qa[0][0:np_, h, 0:65], ident_b[0:np_, 0:np_],
                             ["qa0", "ident_b"], [bk])
                    k.cp("act", qTa[0:65, :, 0:np_],
                         bp[0:65, :].rearrange("p (h n) -> p h n", h=8)[:, :, 0:np_], [bk], ["qTa"])

                def ev_k(ti, t0, np_, cA, sA, ps, pk):
                    k3 = qknorm(ps, pk, np_, gk, "gk")
                    k.op("dve", lambda e: e.memset(ka[0][:], 1.0), writes=["ka0"])
                    k.cp("act", ka[0][0:np_, :, 0:64], k3, ["kn32"], ["ka0"])
                    for sq in range(2):
                        k.dma("pool", O["k_s"][sq], kn32[sq * 32: sq * 32 + 16, :], reads=["kn32"],
                              dkey="d_kn32")
                    bp, bk = nb()
                    for h in range(8):
                        k.tr(bp[0:65, h * 128: h * 128 + np_], ka[0][0:np_, h, 0:65], ident_b[0:np_, 0:np_],
                             ["ka0", "ident_b"], [bk])
                    k.cp("dve", kTs[0:65, :, 0:np_],
                         bp[0:65, :].rearrange("p (h n) -> p h n", h=8)[:, :, 0:np_], [bk], ["kTs"])

                def ev_v(ti, t0, np_, cA, sA, ps, pk):
                    vo, vk = nio()
                    k.cp("act", vo[0:np_, 0:512], ps[0:np_, :], [pk], [vk])
                    for sq in range(2):
                        k.dma("pool", O["v_s"][sq], vo[sq * 32: sq * 32 + 16, 0:512], reads=[vk], dkey="d_" + vk)
                    k.cp("dve", va[0:np_, 0, :, 0:64], vo[0:np_, 0:512].rearrange("p (h d) -> p h d", h=8),
                         [vk], ["va0"])

                yield from proj_gen("mxff", 8, ev_f, nf)
                yield from proj_gen("mx6", 512, ev_q, nf)
                yield from proj_gen("mx7", 512, ev_k, nf)
                yield from proj_gen("mx8", 512, ev_v, nf)
                yield "ATT"
                k.op("dve", lambda e: e.memset(ofT[0:64, :, 0:48], 0.0), writes=["R1b"])
                for sq in range(2):
                    qs = sq * 32
                    k.dma("pool", clf_sb[:, :, :], I["clf"][sq].rearrange("(t p) h -> p t h", p=128),
                          writes=["clf_sb"], dkey="d_clf")
                    wp_, wpk = nf()
                    k.mm(wp_[:, 0:256], cs["triS"][:, :], clf_sb[:, :, :].rearrange("p t h -> p (t h)"),
                         True, True, ["c_triS", "clf_sb"], [wpk])
                    tp_, tpk = nf()
                    k.mm(tp_[:, 0:256], cs["ones"][:, :], clf_sb[:, :, :].rearrange("p t h -> p (t h)"),
                         True, True, ["c_ones", "clf_sb"], [tpk])
                    k.op("dve", lambda e: e.memset(sufc[:, 32, :], 0.0), writes=["sufc"])
                    tot = tp_[:, 0:256].rearrange("p (t h) -> p t h", h=8)
                    for t in range(31, 0, -1):
                        k.tt("dve", sufc[:, t, :], sufc[:, t + 1, :], tot[:, t, :], ALU.add, ["sufc", tpk], ["sufc"])
                    k.tt("dve", suf[:, 0:31, :], wp_[:, 0:248].rearrange("p (t h) -> p t h", h=8),
                         sufc[:, 1:32, :], ALU.add, [wpk, "sufc"], ["suf"])
                    k.cp("dve", suf[:, 31, :], wp_[:, 248:256], [wpk], ["suf"])
                    accs, accsk = sgt[0], "sgt0"
                    slots = {}
                    banks = {}

                    def st0(t):
                        b = t % 2
                        if t % 2 == 0:
                            ks_ = cstage[cst_i[0] % 8]
                            vs_ = cstage[(cst_i[0] + 1) % 8]
                            cst_i[0] += 2
                            k.dma("sp", ks_[0].rearrange("p (t c) -> p t c", t=2),
                                  I["ck"][sq, t * 128:(t + 2) * 128, :].rearrange("(t p) c -> p t c", p=128),
                                  writes=[ks_[1]], dkey="d_sp_" + ks_[1])
                            k.dma("sp", vs_[0].rearrange("p (t c) -> p t c", t=2),
                                  I["cv"][sq, t * 128:(t + 2) * 128, :].rearrange("(t p) c -> p t c", p=128),
                                  writes=[vs_[1]], dkey="d_sp_" + vs_[1])
                            slots[t // 2] = (ks_, vs_)
                        (kslot, kik), (vslot, vik) = slots[t // 2]
                        ki = kslot[:, (t % 2) * 512:(t % 2 + 1) * 512]
                        vi = vslot[:, (t % 2) * 512:(t % 2 + 1) * 512]
                        k.cp("act", ka[b][:, :, 0:64], ki.rearrange("p (h d) -> p h d", h=8), [kik], [f"ka{b}"])
                        k.cp("dve", vca[b][:, :, 0:64], vi.rearrange("p (h d) -> p h d", h=8), [vik],
                             [f"vca{b}"])
                        bp, bk = nb()
                        for h in range(8):
                            k.tr(bp[0:65, h * 128:(h + 1) * 128], ka[b][:, h, 0:65], ident_b[:, :],
                                 [f"ka{b}", "ident_b"], [bk])
                        banks[("bp", t)] = (bp, bk)

                    def st1(t):
                        b = t % 2
                        bp, bk = banks.pop(("bp", t))
                        k.cp("act", kcT[b][0:65, :, :], bp[0:65, :].rearrange("p (h n) -> p h n", h=8), [bk],
                             [f"kcT{b}"])
                        sp_, spk = nf()
                        for h in range(8):
                            k.mm(sp_[:, h * 16:(h + 1) * 16], kcT[b][0:65, h, :], qTa[0:65, h, qs:qs + 16],
                                 True, True, [f"kcT{b}", "qTa"], [spk])
                        banks[("sp", t)] = (sp_, spk)

                    def st2(t):
                        b = t % 2
                        sp_, spk = banks.pop(("sp", t))
                        for h in range(8):
                            k.act(PTs[b][:, h, :], sp_[:, h * 16:(h + 1) * 16], AF.Exp, [spk], [f"PTs{b}"],
                                  sreads=["suf"], bias=suf[:, t, h:h + 1])
                        op2, op2k = nf()
                        for h in range(8):
                            k.mm(op2[0:65, h * 16:(h + 1) * 16], vca[b][:, h, 0:65], PTs[b][:, h, :],
                                 True, True, [f"vca{b}", f"PTs{b}"], [op2k])
                        if t == 0:
                            k.cp("dve", accs[0:65, 0:128], op2[0:65, 0:128], [op2k], [accsk])
                        else:
                            k.tt("dve", accs[0:65, 0:128], accs[0:65, 0:128], op2[0:65, 0:128], ALU.add,
                                 [accsk, op2k], [accsk])

                    for it in range(32 + 2):
                        if 0 <= it - 2 < 32:
                            st2(it - 2)
                        if 0 <= it - 1 < 32:
                            st1(it - 1)
                        if it < 32:
                            st0(it)
                    sp_, spk = nf()
                    for h in range(8):
                        k.mm(sp_[0:48, h * 16:(h + 1) * 16], kTs[0:65, h, 0:48], qTa[0:65, h, qs:qs + 16],
                             True, False, ["kTs", "qTa"], [spk])
                        k.mm(sp_[0:48, h * 16:(h + 1) * 16], ident_b[0:48, 0:48], masks_b[0:48, qs:qs + 16],
                             False, True, ["ident_b", "masks_b"], [spk])
                    for h in range(8):
                        k.act(PTs[0][0:48, h, :], sp_[0:48, h * 16:(h + 1) * 16], AF.Exp, [spk], ["PTs0"],
                              sreads=["bK0"], bias=biasK[0:48, 0, h:h + 1])
                    op2, op2k = nf()
                    for h in range(8):
                        k.mm(op2[0:65, h * 16:(h + 1) * 16], va[0:48, 0, h, 0:65], PTs[0][0:48, h, :],
                             True, True, ["va0", "PTs0"], [op2k])
                    k.tt("dve", accs[0:65, 0:128], accs[0:65, 0:128], op2[0:65, 0:128], ALU.add,
                         [accsk, op2k], [accsk])
                    k.op("dve", lambda e: e.reciprocal(out=rden[64:65, 0:128], in_=accs[64:65, 0:128]),
                         reads=[accsk], writes=["rden", "rot0", "rot1"])
                    bc, bck = nf()
                    k.mm(bc[0:64, 0:128], cs["ones"][64:65, 0:64], rden[64:65, 0:128], True, True,
                         ["c_ones", "rden"], [bck])
                    k.cp("act", bcs[0:64, 0:128], bc[0:64, 0:128], [bck], ["bcs", "rot0", "rot1"])
                    k.tt("dve", ofT[0:64, :, qs:qs + 16], accs[0:64, 0:128].rearrange("p (h q) -> p h q", h=8),
                         bcs[0:64, 0:128].rearrange("p (h q) -> p h q", h=8), ALU.mult, [accsk, "bcs"], ["R1b"])

            mixer(n, tiles, [cfgs], fox_sample, False)
            for sq in range(2):
                for h in range(4):
                    k.dma("pool", O["st_s"][sq, h], state[sq][:, h, :], reads=[f"state{sq}"], dkey="d_stout")
            ffn("f2", n)
            ple(n)
            store_group(ys, n)

        k.wait_all("pool", [kk for kk in k.sem if str(kk).startswith("d_")])
        k.replay()
        print("SBUF bytes/partition:", k.sbytes, " instr:", {e: len(s) for e, s in k.streams.items()})
    return nc


_PROG = {}


def kernel(**inp):
    f = lambda a: np.ascontiguousarray(np.asarray(a, dtype=np.float32))
    if "nc" not in _PROG:
        _PROG["nc"] = build_program()
    nc = _PROG["nc"]
    CT = _consts()
    in_maps = []
    for c in range(NCORES):
        m = {}
        m["x_p"] = f(inp["x_prompt"][c * NSEQ:(c + 1) * NSEQ])
        m["p_p"] = f(inp["p_prompt"][0, c * NSEQ:(c + 1) * NSEQ])
        m["x_s"] = f(inp["x_sample"][c * NSS:(c + 1) * NSS])
        m["p_s"] = f(inp["p_sample"][0, c * NSS:(c + 1) * NSS])
        m["st_in"] = f(inp["state_ret"][0, c * NSS:(c + 1) * NSS])
        m["ck"] = f(inp["cache_fox_k"][0, c * NSS:(c + 1) * NSS]).reshape(NSS, PAST, 512)
        m["cv"] = f(inp["cache_fox_v"][0, c * NSS:(c + 1) * NSS]).reshape(NSS, PAST, 512)
        m["clf"] = f(inp["cache_fox_logf"][0, c * NSS:(c + 1) * NSS])
        for n in ("norm_ffn1_g", "ffn1_w_in", "ffn1_w_out", "norm_mix_g", "w_in_mix", "b_forget",
                  "q_norm_g", "k_norm_g", "w_br_ret", "w_br_fox", "w_out", "norm_ffn2_g", "ffn2_w_in",
                  "ffn2_w_out", "norm_ple_g", "w_ple", "w_ple_gate"):
            m[n] = f(inp[n][0])
        for n in CONST_SHAPES:
            m["c_" + n] = f(CT[n])
        in_maps.append(m)
    res = run_bass_kernel_spmd(nc, in_maps, core_ids=list(range(NCORES)))
    R = res.results
    cat = lambda n: np.concatenate([np.asarray(r[n], dtype=np.float32) for r in R], axis=0)
    y_p = cat("y_p")
    y_s = cat("y_s")
    st_p = cat("st_p")[None]
    k_p = cat("k_p").reshape(1, 32, S, 8, 64)
    v_p = cat("v_p").reshape(1, 32, S, 8, 64)
    lf_p = cat("lf_p")[None]
    st_s = cat("st_s")[None]
    k_s = cat("k_s").reshape(1, 16, L, 8, 64)
    v_s = cat("v_s").reshape(1, 16, L, 8, 64)
    lf_s = cat("lf_s")[None]
    return (y_p, y_s, st_p, k_p, v_p, lf_p, st_s, k_s, v_s, lf_s)
```

```python
import contextlib
import numpy as np
import concourse.bass as bass
import concourse.mybir as mybir
from concourse.bass_utils import run_bass_kernel_spmd

F32 = mybir.dt.float32
BF16 = mybir.dt.bfloat16
AF = mybir.ActivationFunctionType
ALU = mybir.AluOpType
AX = mybir.AxisListType

NCORES = 8
D = 1024
S = 2048
NSEQ = 4
NSS = 2
L = 16
PAST = 4096
DFF = 2816
INC = 6664
EPS = 1e-6
G = 512
NG = S // G
SLOT = 4096
NSLOT = 3
NEG = -30000.0
ENGS = ("pe", "act", "dve", "pool", "sp")


class KB:
    def __init__(self, nc, es):
        self.nc = nc
        self.es = es
        self.streams = {e: [] for e in ENGS}
        self.sem = {}
        self.cnt = {}
        self.waited = {e: {} for e in ENGS}
        self.res = {}
        self.sbytes = 0
        for e in ENGS:
            self._mksem(e)

    def _mksem(self, key):
        if key not in self.sem:
            self.sem[key] = self.es.enter_context(self.nc.semaphore("s_" + str(key)))
            self.cnt[key] = 0
        return self.sem[key]

    def sb(self, name, shape, dt):
        n = 1
        for s in shape[1:]:
            n *= s
        self.sbytes += n * (4 if dt == F32 else 2)
        return self.es.enter_context(self.nc.sbuf_tensor(name, list(shape), dt))

    def ps(self, name, shape, dt):
        return self.es.enter_context(self.nc.psum_tensor(name, list(shape), dt))

    def _deps(self, eng, reads, writes, sreads):
        deps = {}

        def need(k, v):
            if v > deps.get(k, 0):
                deps[k] = v

        skip = eng if eng == "pe" else None
        for r in reads:
            st = self.res.get(r)
            if st and st["w"] and st["w"][0] != skip:
                need(*st["w"])
        for r in sreads:
            st = self.res.get(r)
            if st and st["w"]:
                need(*st["w"])
        for w in writes:
            st = self.res.get(w)
            if st:
                if st["w"] and st["w"][0] != skip:
                    need(*st["w"])
                for k, v in st["r"].items():
                    if k != skip:
                        need(k, v)
        waits = []
        wd = self.waited[eng]
        for k, v in deps.items():
            if wd.get(k, 0) < v:
                wd[k] = v
                waits.append((k, v))
        return waits

    def _mark(self, key, val, reads, writes):
        for r in reads:
            st = self.res.setdefault(r, {"w": None, "r": {}})
            st["r"][key] = val
        for w in writes:
            self.res[w] = {"w": (key, val), "r": {}}

    def op(self, eng, fn, reads=(), writes=(), sreads=()):
        waits = self._deps(eng, reads, writes, sreads)
        self.cnt[eng] += 1
        self.streams[eng].append((waits, fn, (eng, 1)))
        self._mark(eng, self.cnt[eng], tuple(reads) + tuple(sreads), writes)

    def dma(self, queue, out, in_, reads=(), writes=(), dkey=None, slow=False):
        self._mksem(dkey)
        waits = self._deps(queue, reads, writes, ())
        self.cnt[dkey] += 16
        if slow:
            fn = lambda e, o=out, i=in_: e.dma_start(out=o, in_=i, allow_slow_non_contiguous=True)
        else:
            fn = lambda e, o=out, i=in_: e.dma_start(out=o, in_=i)
        self.streams[queue].append((waits, fn, (dkey, 16)))
        self._mark(dkey, self.cnt[dkey], reads, writes)

    def barrier(self):
        for e in ENGS:
            waits = []
            for kk, v in self.cnt.items():
                if kk != e and v > 0 and self.waited[e].get(kk, 0) < v:
                    self.waited[e][kk] = v
                    waits.append((kk, v))
            self.streams[e].append((waits, None, None))

    def wait_all(self, eng, keys):
        waits = [(k, self.cnt[k]) for k in keys if self.cnt[k] > 0]
        self.streams[eng].append((waits, None, None))

    def replay(self):
        emap = {"pe": "tensor", "act": "scalar", "dve": "vector", "pool": "gpsimd", "sp": "sync"}
        with self.nc.Block() as block:
            for e in ENGS:
                stream = self.streams[e]
                if not stream:
                    continue

                def body(engine, stream=stream):
                    for waits, fn, inc in stream:
                        for kk, v in waits:
                            engine.wait_ge(self.sem[kk], v)
                        if fn is not None:
                            ins = fn(engine)
                            ins.then_inc(self.sem[inc[0]], inc[1])

                getattr(block, emap[e])(body)

    def mm(self, out, lhsT, rhs, start, stop, reads, writes):
        self.op("pe", lambda e: e.matmul(out, lhsT, rhs, start=start, stop=stop),
                reads=reads, writes=writes)

    def tr(self, out, in_, ident, reads, writes):
        self.op("pe", lambda e: e.transpose(out, in_, ident), reads=reads, writes=writes)

    def act(self, out, in_, func, reads, writes, sreads=(), bias=None, scale=None, accum_out=None):
        kw = {}
        if bias is not None:
            kw["bias"] = bias
        if scale is not None:
            kw["scale"] = scale
        if accum_out is not None:
            kw["accum_out"] = accum_out
        self.op("act", lambda e: e.activation(out=out, in_=in_, func=func, **kw),
                reads=reads, writes=writes, sreads=sreads)

    def cp(self, eng, out, in_, reads, writes):
        if eng == "act":
            self.op("act", lambda e: e.copy(out=out, in_=in_), reads=reads, writes=writes)
        else:
            self.op(eng, lambda e: e.tensor_copy(out=out, in_=in_), reads=reads, writes=writes)

    def tt(self, eng, out, in0, in1, op, reads, writes):
        self.op(eng, lambda e: e.tensor_tensor(out=out, in0=in0, in1=in1, op=op),
                reads=reads, writes=writes)

    def ts(self, eng, out, in0, s1, s2, op0, op1, reads, writes, sreads=()):
        if s2 is None:
            self.op(eng, lambda e: e.tensor_scalar(out=out, in0=in0, scalar1=s1, scalar2=None, op0=op0),
                    reads=reads, writes=writes, sreads=sreads)
        else:
            self.op(eng, lambda e: e.tensor_scalar(out=out, in0=in0, scalar1=s1, scalar2=s2, op0=op0, op1=op1),
                    reads=reads, writes=writes, sreads=sreads)

    def stt(self, eng, out, in0, scalar, in1, op0, op1, reads, writes, sreads=()):
        self.op(eng, lambda e: e.scalar_tensor_tensor(out=out, in0=in0, scalar=scalar, in1=in1,
                                                      op0=op0, op1=op1),
                reads=reads, writes=writes, sreads=sreads)


def _consts():
    c = {}
    c["ident"] = np.eye(128, dtype=np.float32)
    s = np.arange(128)
    c["tri"] = (s[:, None] <= s[None, :]).astype(np.float32)
    c["triS"] = (s[:, None] > s[None, :]).astype(np.float32)
    c["ones"] = np.ones((128, 128), np.float32)
    c["maskT"] = np.where(s[:, None] <= s[None, :], 0.0, NEG).astype(np.float32)
    blk = np.full(48, -1)
    blk[0:16] = 0
    blk[32:48] = 1
    pos = np.zeros(48)
    pos[0:16] = np.arange(16)
    pos[32:48] = np.arange(16)
    s48 = np.arange(48)
    same = (blk[:, None] == blk[None, :]) & (blk[:, None] >= 0)
    c["tri_s"] = np.zeros((128, 128), np.float32)
    c["tri_s"][:48, :48] = (same & (pos[:, None] <= pos[None, :])).astype(np.float32)
    c["mask_s"] = np.full((128, 128), NEG, np.float32)
    c["mask_s"][:48, :48] = np.where(same & (pos[:, None] <= pos[None, :]), 0.0, NEG)
    lg = np.log(1.0 - 2.0 ** (-5.0 - np.arange(4, dtype=np.float64)))
    i = np.arange(128)
    ch = i // 64
    DT = np.zeros((128, 4, 128), np.float64)
    for h in range(4):
        jj, ii = np.meshgrid(i, i, indexing="ij")
        samec = ch[jj] == ch[ii]
        later = (ch[ii] == 1) & (ch[jj] == 0)
        DT[:, h, :] = np.where(samec, np.exp(lg[h] * np.abs(ii - jj)),
                               np.where(later, np.exp(lg[h] * (ii - jj)), 0.0))
    KS = 128.0 ** -0.5
    c["DT"] = (DT * KS).astype(np.float32)
    idec = np.exp(lg[None, :, None] * (i[None, None, :] + 1.0))
    c["idec"] = np.broadcast_to(idec, (128, 4, 128)).astype(np.float32).copy()
    c["kdec"] = np.zeros((128, 8), np.float32)
    c["kdec"][:, 0:4] = np.exp(lg[None, :] * (127.0 - i[:, None])) * KS
    c["cdec"] = [float(np.exp(lg[h] * 128.0)) for h in range(4)]
    DTs = np.zeros((128, 4, 48), np.float64)
    for h in range(4):
        DTs[:48, h, :48] = np.where(same, np.exp(lg[h] * np.abs(pos[None, :] - pos[:, None])), 0.0)
    c["DT_s"] = (DTs * KS).astype(np.float32)
    for sq in range(2):
        t = np.zeros((128, 4, 48), np.float64)
        sel = blk == sq
        for h in range(4):
            t[:, h, :48] = np.where(sel, np.exp(lg[h] * (pos + 1.0)), 0.0)[None, :]
        c["idec_s%d" % sq] = t.astype(np.float32)
    kd = np.zeros((128, 8), np.float32)
    for sq in range(2):
        sel = blk == sq
        for h in range(4):
            kd[:48, sq * 4 + h] = np.where(sel, np.exp(lg[h] * (15.0 - pos)), 0.0) * KS
    c["kdec_s"] = kd
    c["cdec_s"] = [float(np.exp(lg[h] * 16.0)) for h in range(4)]
    half = 64
    freqs = (10000.0 ** (-np.arange(half, dtype=np.float32) / half)).astype(np.float32)
    posp = np.arange(S, dtype=np.float32)
    ang = (posp[:, None] * freqs[None, :]).astype(np.float32)
    cs = np.cos(ang).astype(np.float32).reshape(16, 128, 64).transpose(1, 0, 2)
    sn = np.sin(ang).astype(np.float32).reshape(16, 128, 64).transpose(1, 0, 2)
    c["cos"] = np.ascontiguousarray(cs)
    c["sin"] = np.ascontiguousarray(sn)
    poss = np.zeros(128, np.float32)
    poss[:48] = (PAST + pos).astype(np.float32)
    angs = (poss[:, None] * freqs[None, :]).astype(np.float32)
    c["cos_s"] = np.cos(angs).astype(np.float32)
    c["sin_s"] = np.sin(angs).astype(np.float32)
    return c


CONST_SHAPES = {
    "ident": [128, 128], "tri": [128, 128], "triS": [128, 128], "ones": [128, 128],
    "maskT": [128, 128], "tri_s": [128, 128], "mask_s": [128, 128],
    "DT": [128, 4, 128], "idec": [128, 4, 128], "kdec": [128, 8],
    "DT_s": [128, 4, 48], "idec_s0": [128, 4, 48], "idec_s1": [128, 4, 48], "kdec_s": [128, 8],
    "cos": [128, 16, 64], "sin": [128, 16, 64], "cos_s": [128, 64], "sin_s": [128, 64],
}
LATE_CONSTS = ("DT_s", "idec_s0", "idec_s1", "cos", "sin", "cos_s", "sin_s")

IN_SHAPES = {
    "x_p": [NSEQ, S, D], "p_p": [NSEQ, S, 256], "x_s": [NSS, L, D], "p_s": [NSS, L, 256],
    "st_in": [NSS, 4, 128, 256], "ck": [NSS, PAST, 512], "cv": [NSS, PAST, 512], "clf": [NSS, PAST, 8],
    "norm_ffn1_g": [D], "ffn1_w_in": [D, 2 * DFF], "ffn1_w_out": [DFF, D], "norm_mix_g": [D],
    "w_in_mix": [D, INC], "b_forget": [8], "q_norm_g": [64], "k_norm_g": [64],
    "w_br_ret": [D, D], "w_br_fox": [512, D], "w_out": [D, D], "norm_ffn2_g": [D],
    "ffn2_w_in": [D, 2 * DFF], "ffn2_w_out": [DFF, D], "norm_ple_g": [D], "w_ple": [256, D],
    "w_ple_gate": [D, D],
}
OUT_SHAPES = {
    "y_p": [NSEQ, S, D], "y_s": [NSS, L, D], "st_p": [NSEQ, 4, 128, 256], "k_p": [NSEQ, S, 512],
    "v_p": [NSEQ, S, 512], "lf_p": [NSEQ, S, 8], "st_s": [NSS, 4, 128, 256], "k_s": [NSS, L, 512],
    "v_s": [NSS, L, 512], "lf_s": [NSS, L, 8],
}


def build_program(nseq=NSEQ, do_sample=True, ngroups=NG):
    CT = _consts()
    nc = bass.Bass("TRN2", target_bir_lowering=False)
    I = {n: nc.dram_tensor(n, s, F32, kind="ExternalInput").ap() for n, s in IN_SHAPES.items()}
    C = {n: nc.dram_tensor("c_" + n, s, F32, kind="ExternalInput").ap() for n, s in CONST_SHAPES.items()}
    O = {n: nc.dram_tensor(n, s, F32, kind="ExternalOutput").ap() for n, s in OUT_SHAPES.items()}

    wv = {
        "f1a": I["ffn1_w_in"].rearrange("(kc p) n -> p kc n", p=128),
        "f1b": I["ffn1_w_out"].rearrange("(kc p) n -> p kc n", p=128),
        "mix": I["w_in_mix"].rearrange("(kc p) n -> p kc n", p=128),
        "brr": I["w_br_ret"].rearrange("(kc p) n -> p kc n", p=128),
        "brf": I["w_br_fox"].rearrange("(h p) n -> p h n", p=64),
        "wo": I["w_out"].rearrange("(kc p) n -> p kc n", p=128),
        "f2a": I["ffn2_w_in"].rearrange("(kc p) n -> p kc n", p=128),
        "f2b": I["ffn2_w_out"].rearrange("(kc p) n -> p kc n", p=128),
        "pg": I["w_ple_gate"].rearrange("(kc p) n -> p kc n", p=128),
        "pl": I["w_ple"].rearrange("(kc p) n -> p kc n", p=128),
    }
    chunks = []

    def ffn_chunks(tag, a, b, gname):
        for pc in range(11):
            chunks.append((f"{tag}a{pc}", [(wv[a], 128, 8, pc * 256, 256, gname, 0),
                                           (wv[a], 128, 8, DFF + pc * 256, 256, gname, 2048)]))
        for dc in range(8):
            chunks.append((f"{tag}b{dc}", [(wv[b], 128, 22, dc * 128, 128, None, 0)]))

    ffn_chunks("f1", "f1a", "f1b", "norm_ffn1_g")
    for cidx in range(9):
        chunks.append((f"mx{cidx}", [(wv["mix"], 128, 8, cidx * 512, 512, "norm_mix_g", 0)]))
    chunks.append(("mxff", [(wv["mix"], 128, 8, 4608, 8, "norm_mix_g", 0)]))
    for dc in range(8):
        chunks.append((f"mgA{dc}", [(wv["brr"], 128, 8, dc * 128, 128, None, 0),
                                    (wv["mix"], 128, 8, 4616 + dc * 128, 128, "norm_mix_g", 1024)]))
        chunks.append((f"mgB{dc}", [(wv["brf"], 64, 8, dc * 128, 128, None, 0),
                                    (wv["mix"], 128, 8, 4616 + 1024 + dc * 128, 128, "norm_mix_g", 1024)]))
    for cidx in range(2):
        chunks.append((f"wo{cidx}", [(wv["wo"], 128, 8, cidx * 512, 512, None, 0)]))
    ffn_chunks("f2", "f2a", "f2b", "norm_ffn2_g")
    for cidx in range(2):
        chunks.append((f"pg{cidx}", [(wv["pg"], 128, 8, cidx * 512, 512, "norm_ple_g", 0)]))
    chunks.append(("pl", [(wv["pl"], 128, 2, 0, 1024, None, 0)]))
    NCH = len(chunks)
    cidx_of = {name: i for i, (name, _) in enumerate(chunks)}
    wscr = nc.dram_tensor("wscr", [NCH, 128, SLOT], BF16).ap()

    with contextlib.ExitStack() as es:
        k = KB(nc, es)
        xT = k.sb("xT", [128, 8, G], F32)
        hT = k.sb("hT", [128, 8, G], BF16)
        R2 = k.sb("R2", [128, 4096], BF16)
        R1 = k.sb("R1", [128, 22 * G], BF16)
        rstd = k.sb("rstd", [128, G], F32)
        wring = [k.sb(f"wr{i}", [128, SLOT], BF16) for i in range(NSLOT)]
        io = [k.sb(f"io{i}", [128, 1024], F32) for i in range(2)]
        pst = [k.sb("pst0", [128, 256], F32)]
        pbf = k.sb("pbf", [128, 256], BF16)
        pT = k.sb("pT", [128, 2, G], BF16)
        cs = {n: k.sb("k_" + n, s, F32) for n, s in CONST_SHAPES.items() if n not in LATE_CONSTS}
        cosg = k.sb("cosg", [128, 4, 64], F32)
        sing = k.sb("sing", [128, 4, 64], F32)
        ident_b = k.sb("ident_b", [128, 128], BF16)
        ones_b = k.sb("ones_b", [128, 128], BF16)
        maskT_b = k.sb("maskT_b", [128, 128], BF16)
        masks_b = k.sb("masks_b", [128, 128], BF16)
        gq = k.sb("gq", [128, 64], F32)
        gk = k.sb("gk", [128, 64], F32)
        bfg = k.sb("bfg", [128, 8], F32)
        gcols = {n: k.sb("gc_" + n, [128, 8], F32) for n in
                 ("norm_ffn1_g", "norm_mix_g", "norm_ffn2_g", "norm_ple_g")}
        qa = [k.sb(f"qa{i}", [128, 8, 66], BF16) for i in range(2)]

        cqs = k.sb("cqs", [128, 4, 8], F32)
        ka = [k.sb(f"ka{i}", [128, 8, 66], BF16) for i in range(2)]
        qTt = [k.sb(f"qTt{i}", [128, 4, 128], BF16) for i in range(2)]
        qdTt = [k.sb(f"qdTt{i}", [128, 4, 128], BF16) for i in range(2)]
        kTt = [k.sb(f"kTt{i}", [128, 4, 128], BF16) for i in range(2)]
        attT = [k.sb(f"attT{i}", [128, 128], BF16) for i in range(4)]
        om = [k.sb("om0", [128, 1024], BF16)]
        omT = k.sb("omT", [128, 8, G], BF16)
        PT = [k.sb(f"PT{i}", [128, G], BF16) for i in range(3)]
        kTa = k.sb("kTa", [128, 8, S], BF16)
        va_flat = k.sb("va", [128, 16 * 8 * 66], BF16)
        va = va_flat[:, :].rearrange("p (a h c) -> p a h c", a=16, h=8)
        biasK = k.sb("biasK", [128, 16, 8], F32)
        state = [k.sb("state0", [128, 4, 256], F32)]
        state_b = [k.sb("stateb0", [128, 4, 256], BF16)]
        junk = sq512_ph = None
        sm = k.sb("sm", [128, 256], F32)
        rot = k.sb("rot", [128, 4, 256], F32)
        sq512 = k.sb("sq512", [128, 512], F32)
        kn32 = k.sb("kn32", [128, 512], F32)
        lfc = k.sb("lfc", [128, 8], F32)
        rden = rot[:, 0:2, :].rearrange("p a n -> p (a n)")
        bcs = rden
        qTa_own = k.sb("qTa", [128, 8, G], BF16)
        kTs = k.sb("kTs", [128, 8, 48], BF16)
        sgt = [k.sb(f"sgt{i}", [128, G], F32) for i in range(2)]
        junk = sq512[:, 0:256]
        junk4v = [(sgt[0][:, 0:256], "sgt0"), (sgt[0][:, 256:512], "sgt0"),
                  (sgt[1][:, 0:256], "sgt1"), (sgt[1][:, 256:512], "sgt1")]
        vaf = kTa[:, :, :].rearrange("p h n -> p (h n)")
        voff = [0]

        def carve(nel, dt):
            nb = nel * (2 if dt == F32 else 1)
            a = vaf[:, voff[0]: voff[0] + nb]
            voff[0] += nb
            assert voff[0] <= 8 * S
            return a.bitcast(F32) if dt == F32 else a

        clf_sb = carve(256, F32).rearrange("p (t h) -> p t h", h=8)
        suf = carve(256, F32).rearrange("p (t h) -> p t h", h=8)
        sufc = carve(264, F32).rearrange("p (t h) -> p t h", h=8)
        kcT = [carve(1024, BF16).rearrange("p (h n) -> p h n", h=8) for i in range(2)]
        vca = [carve(528, BF16).rearrange("p (h c) -> p h c", h=8) for i in range(2)]
        PTs = [carve(128, BF16).rearrange("p (h q) -> p h q", h=8) for i in range(2)]
        kd2 = carve(512, BF16)
        qdTt2 = carve(512, BF16).rearrange("p (h n) -> p h n", h=4)
        late = {n: carve(192, F32).rearrange("p (h n) -> p h n", h=4) for n in ("DT_s", "idec_s0", "idec_s1")}
        cstage = [(io[0][:, :], "io0"), (io[1][:, :], "io1")]
        for i_ in range(3):
            cstage.append((carve(1024, F32), f"cst{i_}"))
        for i_ in range(3):
            cstage.append((va_flat[:, 528 + i_ * 2048: 528 + (i_ + 1) * 2048].bitcast(F32), f"cst{3 + i_}"))
        cst_i = [0]
        state.append(carve(1024, F32).rearrange("p (h n) -> p h n", h=4))
        state_b.append(carve(1024, BF16).rearrange("p (h n) -> p h n", h=4))
        pf = [k.ps(f"pf{i}", [128, 512], F32) for i in range(6)]
        pb = [k.ps(f"pb{i}", [128, 1024], BF16) for i in range(2)]
        rr = {"f": 0, "b": 0, "w": 0, "io": 0}

        def nf():
            i = rr["f"] % 4
            rr["f"] += 1
            return pf[i], f"pf{i}"

        nf_all = nf

        def nb():
            i = rr["b"] % 2
            rr["b"] += 1
            return pb[i], f"pb{i}"

        def nio():
            i = rr["io"] % 2
            rr["io"] += 1
            return io[i], f"io{i}"

        sqT = R2[:, :].rearrange("p (a n) -> p a n", a=8)
        mT = sqT
        k_rot = R2[:, 0:2048].rearrange("p (a n) -> p a n", a=4)
        k_dec = R2[:, 2048:4096].rearrange("p (a n) -> p a n", a=4)
        actT = R1[:, :].rearrange("p (a n) -> p a n", a=22)
        v_tm = R1[:, 0:4096].rearrange("p (a n) -> p a n", a=4)
        sg_tm = R1[:, 4096:8192].rearrange("p (a n) -> p a n", a=4)
        q_rot = R1[:, 8192:10240].rearrange("p (a n) -> p a n", a=4)
        qTa = qTa_own
        ofT = R1[:, 4096:8192].rearrange("p (a n) -> p a n", a=8)

        for n in cs:
            k.dma("pool", cs[n][:], C[n], writes=["c_" + n], dkey="d_const")
        k.dma("pool", gq[:], I["q_norm_g"].partition_broadcast(128), writes=["gq"], dkey="d_const")
        k.dma("pool", gk[:], I["k_norm_g"].partition_broadcast(128), writes=["gk"], dkey="d_const")
        k.dma("pool", bfg[:], I["b_forget"].partition_broadcast(128), writes=["bfg"], dkey="d_const")
        for n in gcols:
            k.dma("pool", gcols[n][:], I[n].rearrange("(kc p) -> p kc", p=128), writes=["gc_" + n],
                  dkey="d_const", slow=True)
        for n in ["c_" + n for n in cs] + ["gc_" + n for n in gcols] + ["gq", "gk", "bfg"]:
            if n in k.res:
                k.res[n]["w"] = ("d_const", k.cnt["d_const"])
        k.ts("dve", gq[:], gq[:], 0.125, None, ALU.mult, None, ["gq"], ["gq"])
        k.cp("dve", ident_b[:], cs["ident"][:], ["c_ident"], ["ident_b"])
        k.cp("dve", ones_b[:], cs["ones"][:], ["c_ones"], ["ones_b"])
        k.cp("dve", maskT_b[:], cs["maskT"][:], ["c_maskT"], ["maskT_b"])
        k.cp("dve", masks_b[:], cs["mask_s"][:], ["c_mask_s"], ["masks_b"])
        for i in range(2):
            k.op("dve", lambda e, t=qa[i]: e.memset(t[:], 0.0), writes=[f"qa{i}"])
            k.op("dve", lambda e, t=ka[i]: e.memset(t[:], 1.0), writes=[f"ka{i}"])
        k.op("dve", lambda e: e.memset(va_flat[:, :], 1.0), writes=[f"va{T}" for T in range(16)])

        R1f = R1[:, :].bitcast(F32)
        kTaf = kTa[:, :, :].rearrange("p h n -> p (h n)").bitcast(F32)
        stg = [(xT[:, :, :].rearrange("p a n -> p (a n)"), "stg0"), (R1f, "stg1"),
               (kTaf[:, 0:4096], "stg2"), (kTaf[:, 4096:8192], "stg3")]
        seglist = []
        for ci, (cname, segs) in enumerate(chunks):
            for j, sg_ in enumerate(segs):
                seglist.append((ci, sg_, j == len(segs) - 1))
        ceng = ["dve", "act"]
        ce = [0]
        LA = 2

        def p_load(idx):
            ci, (src, npart, nkc, c0, w, gname, off), last = seglist[idx]
            st, stkey = stg[idx % 4]
            stv = st[0:npart, 0:nkc * w].rearrange("p (a n) -> p a n", a=nkc)
            k.dma("sp", stv, src[:, :, c0:c0 + w], writes=[stkey], dkey="d_" + stkey)

        def p_cast(idx):
            ci, (src, npart, nkc, c0, w, gname, off), last = seglist[idx]
            st, stkey = stg[idx % 4]
            slot, skey = wring[ci % NSLOT], f"wr{ci % NSLOT}"
            n_el = nkc * w
            stv = st[0:npart, 0:n_el].rearrange("p (a n) -> p a n", a=nkc)
            dst = slot[0:npart, off:off + n_el].rearrange("p (a n) -> p a n", a=nkc)
            if gname is None:
                eng = ceng[ce[0] % 2]
                ce[0] += 1
                k.cp(eng, dst, stv, [stkey], [skey])
            else:
                for kc in range(nkc):
                    eng = ceng[ce[0] % 2]
                    ce[0] += 1
                    gc = gcols[gname][:, kc:kc + 1]
                    if eng == "act":
                        k.act(dst[:, kc, :], stv[:, kc, :], AF.Copy, [stkey], [skey],
                              sreads=["gc_" + gname], scale=gc)
                    else:
                        k.ts(eng, dst[:, kc, :], stv[:, kc, :], gc, None, ALU.mult, None,
                             [stkey], [skey], sreads=["gc_" + gname])
            if last:
                k.dma("pool", wscr[ci], slot[:, :], reads=[skey], writes=[f"ws{ci}"], dkey="d_wst_" + skey)

        for idx in range(len(seglist) + LA):
            if idx < len(seglist):
                p_load(idx)
            if idx - LA >= 0:
                p_cast(idx - LA)
        k.barrier()

        def wload(cname):
            ci = cidx_of[cname]
            i = rr["w"] % NSLOT
            rr["w"] += 1
            k.dma("sp", wring[i][:, :], wscr[ci], reads=[f"ws{ci}"], writes=[f"wr{i}"],
                  dkey=f"d_wr{i}")
            return wring[i], f"wr{i}"

        def rmsnorm(n):
            ps, pk = nf()
            for kc in range(8):
                sk = "R2a" if kc < 4 else "R2b"
                k.act(sqT[:, kc, 0:n], xT[:, kc, 0:n], AF.Square, [f"xT{kc}"], [sk])
                k.mm(ps[:, 0:n], ones_b[:, :], sqT[:, kc, 0:n], kc == 0, kc == 7, ["ones_b", sk], [pk])
            k.act(rstd[:, 0:n], ps[:, 0:n], AF.Ln, [pk], ["rstd"], bias=EPS, scale=1.0 / D)
            k.act(rstd[:, 0:n], rstd[:, 0:n], AF.Exp, ["rstd"], ["rstd"], scale=-0.5)
            for kc in range(8):
                eng = "pool" if kc in (3, 6) else "dve"
                k.tt(eng, hT[:, kc, 0:n], xT[:, kc, 0:n], rstd[:, 0:n], ALU.mult, [f"xT{kc}", "rstd"], [f"hT{kc}"])

        def r1key(j):
            return "R1a" if j < 8 else ("R1b" if j < 16 else "R1c")

        def ffn(tag, n):
            rmsnorm(n)
            for pc in range(11):
                w, wk = wload(f"{tag}a{pc}")
                for sub in range(2):
                    j = pc * 2 + sub
                    gp, gkey = nf()
                    up, ukey = nf()
                    for kc in range(8):
                        k.mm(gp[:, 0:n], w[:, kc * 256 + sub * 128: kc * 256 + sub * 128 + 128],
                             hT[:, kc, 0:n], kc == 0, kc == 7, [wk, f"hT{kc}"], [gkey])
                    for kc in range(8):
                        k.mm(up[:, 0:n], w[:, 2048 + kc * 256 + sub * 128: 2048 + kc * 256 + sub * 128 + 128],
                             hT[:, kc, 0:n], kc == 0, kc == 7, [wk, f"hT{kc}"], [ukey])
                    sg = sgt[j % 2]
                    k.act(sg[:, 0:n], gp[:, 0:n], AF.Silu, [gkey], [f"sgt{j % 2}"])
                    k.tt("dve", actT[:, j, 0:n], sg[:, 0:n], up[:, 0:n], ALU.mult,
                         [f"sgt{j % 2}", ukey], [r1key(j)])
            for dc in range(8):
                w, wk = wload(f"{tag}b{dc}")
                yp, yk = nf()
                for j in range(22):
                    k.mm(yp[:, 0:n], w[:, j * 128:(j + 1) * 128], actT[:, j, 0:n], j == 0, j == 21,
                         [wk, r1key(j)], [yk])
                k.stt("dve", xT[:, dc, 0:n], yp[:, 0:n], 0.5, xT[:, dc, 0:n], ALU.mult, ALU.add,
                      [yk, f"xT{dc}"], [f"xT{dc}"])

        def load_group(xsrc_tiles, psrc_tiles, n):
            for (xparts, pparts, t0, np_) in zip_tiles(xsrc_tiles, psrc_tiles):
                xi, xk = nio()
                if len(xparts) > 1 or xparts[0][2] != 128:
                    k.op("pool", lambda e, t=xi: e.memset(t[:], 0.0), writes=[xk])
                for (ap, r0, nr) in xparts:
                    k.dma("pool", xi[r0:r0 + nr, :], ap, writes=[xk], dkey="d_" + xk)
                for half in range(2):
                    ps, pk = nf()
                    for q4 in range(4):
                        kc = half * 4 + q4
                        k.tr(ps[:, q4 * 128: q4 * 128 + np_], xi[0:np_, kc * 128:(kc + 1) * 128],
                             cs["ident"][0:np_, 0:np_], [xk, "c_ident"], [pk])
                    src = ps[:, :].rearrange("p (a n) -> p a n", a=4)[:, :, 0:np_]
                    k.cp("act" if half == 0 else "dve", xT[:, half * 4: half * 4 + 4, t0:t0 + np_], src,
                         [pk], [f"xT{half * 4 + q4}" for q4 in range(4)])
                pt, ptk = pst[0], "pst0"
                if len(pparts) > 1 or pparts[0][2] != 128:
                    k.op("pool", lambda e, t=pt: e.memset(t[:], 0.0), writes=[ptk])
                for (ap, r0, nr) in pparts:
                    k.dma("pool", pt[r0:r0 + nr, :], ap, writes=[ptk], dkey="d_" + ptk)
                k.cp("act", pbf[0:np_, :], pt[0:np_, :], [ptk], ["pbf"])
                bp, bk = nb()
                for kc in range(2):
                    k.tr(bp[:, kc * 128: kc * 128 + np_], pbf[0:np_, kc * 128:(kc + 1) * 128],
                         ident_b[0:np_, 0:np_], ["pbf", "ident_b"], [bk])
                src = bp[:, 0:256].rearrange("p (a n) -> p a n", a=2)[:, :, 0:np_]
                k.cp("dve", pT[:, :, t0:t0 + np_], src, [bk], ["pT"])

        def zip_tiles(xs, ps_):
            return [(x[0], p[0], x[1], x[2]) for x, p in zip(xs, ps_)]

        def store_group(ydst_tiles, n):
            for (parts, t0, np_) in ydst_tiles:
                yo, yk = nio()
                for half in range(2):
                    ps, pk = nf()
                    for q4 in range(4):
                        kc = half * 4 + q4
                        k.tr(ps[0:np_, q4 * 128:(q4 + 1) * 128], xT[:, kc, t0:t0 + np_],
                             cs["ident"][:, :], [f"xT{kc}", "c_ident"], [pk])
                    k.cp("act" if half == 0 else "dve", yo[0:np_, half * 512:(half + 1) * 512],
                         ps[0:np_, :], [pk], [yk])
                for (ap, r0, nr) in parts:
                    k.dma("pool", ap, yo[r0:r0 + nr, :], reads=[yk], dkey="d_" + yk)

        def ple(n):
            rmsnorm(n)
            wp, wpk = wload("pl")
            for cidx in range(2):
                w, wk = wload(f"pg{cidx}")
                for dl in range(4):
                    dc = cidx * 4 + dl
                    gp, gkey = nf()
                    pp, pkey = nf()
                    for kc in range(8):
                        k.mm(gp[:, 0:n], w[:, kc * 512 + dl * 128: kc * 512 + dl * 128 + 128],
                             hT[:, kc, 0:n], kc == 0, kc == 7, [wk, f"hT{kc}"], [gkey])
                    for kc in range(2):
                        k.mm(pp[:, 0:n], wp[:, kc * 1024 + dc * 128: kc * 1024 + dc * 128 + 128],
                             pT[:, kc, 0:n], kc == 0, kc == 1, [wpk, "pT"], [pkey])
                    sg = sgt[dc % 2]
                    k.act(sg[:, 0:n], gp[:, 0:n], AF.Sigmoid, [gkey], [f"sgt{dc % 2}"])
                    k.tt("dve", sg[:, 0:n], sg[:, 0:n], pp[:, 0:n], ALU.mult, [f"sgt{dc % 2}", pkey],
                         [f"sgt{dc % 2}"])
                    k.tt("pool", xT[:, dc, 0:n], xT[:, dc, 0:n], sg[:, 0:n], ALU.add,
                         [f"xT{dc}", f"sgt{dc % 2}"], [f"xT{dc}"])

        def rotary(ps, pk, dst, cos_ap, sin_ap, np_, dkey_):
            x = ps[0:np_, :].rearrange("p (h t d) -> p h t d", h=4, t=2)
            x1 = x[:, :, 0, :]
            x2 = x[:, :, 1, :]
            cb = cos_ap.unsqueeze(1).broadcast_to([np_, 4, 64])
            sb_ = sin_ap.unsqueeze(1).broadcast_to([np_, 4, 64])
            t = [rot[0:np_, i, :].rearrange("p (h d) -> p h d", h=4) for i in range(4)]
            d = dst.rearrange("p (h t d) -> p h t d", h=4, t=2)
            k.tt("dve", t[0], x1, cb, ALU.mult, [pk, "cosg"], ["rot0"])
            k.tt("dve", t[1], x2, sb_, ALU.mult, [pk, "sing"], ["rot1"])
            k.tt("dve", t[2], x1, sb_, ALU.mult, [pk, "sing"], ["rot2"])
            k.tt("dve", t[3], x2, cb, ALU.mult, [pk, "cosg"], ["rot3"])
            k.tt("pool", d[:, :, 0, :], t[0], t[1], ALU.subtract, ["rot0", "rot1"], [dkey_])
            k.tt("pool", d[:, :, 1, :], t[2], t[3], ALU.add, ["rot2", "rot3"], [dkey_])

        def qknorm(ps, pk, np_, gtab, gkey):
            k.act(sq512[0:np_, :], ps[0:np_, :], AF.Square, [pk], ["sq512"])
            k.op("dve", lambda e: e.reduce_sum(out=sm[0:np_, 0:8],
                                               in_=sq512[0:np_, :].rearrange("p (h d) -> p h d", h=8),
                                               axis=AX.X), reads=["sq512"], writes=["sm"])
            k.act(sm[0:np_, 0:8], sm[0:np_, 0:8], AF.Sqrt, ["sm"], ["sm"], bias=EPS, scale=1.0 / 64)
            k.op("dve", lambda e: e.reciprocal(out=sm[0:np_, 0:8], in_=sm[0:np_, 0:8]),
                 reads=["sm"], writes=["sm"])
            k3 = kn32[0:np_, :].rearrange("p (h d) -> p h d", h=8)
            k.tt("dve", k3, ps[0:np_, :].rearrange("p (h d) -> p h d", h=8),
                 sm[0:np_, 0:8].unsqueeze(2).broadcast_to([np_, 8, 64]), ALU.mult, [pk, "sm"], ["kn32"])
            k.tt("dve", k3, k3, gtab[0:np_, :].unsqueeze(1).broadcast_to([np_, 8, 64]), ALU.mult,
                 ["kn32", gkey], ["kn32"])
            return k3

        def mixer(n, tiles, ret_cfg, fox_fn, interleave):
            nt = len(tiles)
            rmsnorm(n)
            def proj_gen(cname, width, evac, fb=nf):
                w, wk = wload(cname)
                for ti, (t0, np_, cA, sA) in enumerate(tiles):
                    ps, pk = fb()
                    for kc in range(8):
                        k.mm(ps[0:np_, 0:width], hT[:, kc, t0:t0 + np_], w[:, kc * width:(kc + 1) * width],
                             kc == 0, kc == 7, [f"hT{kc}", wk], [pk])
                    evac(ti, t0, np_, cA, sA, ps, pk)
                    yield

            def proj(cname, width, evac):
                for _ in proj_gen(cname, width, evac):
                    pass

            def mk_rot(lst):
                st_ = [0]

                def fb():
                    i = lst[st_[0] % len(lst)]
                    st_[0] += 1
                    return i
                return fb

            proj("mx0", 512, lambda ti, t0, np_, cA, sA, ps, pk:
                 rotary(ps, pk, q_rot[0:np_, ti, :], cA, sA, np_, "R1c"))
            proj("mx1", 512, lambda ti, t0, np_, cA, sA, ps, pk:
                 rotary(ps, pk, k_rot[0:np_, ti, :], cA, sA, np_, "R2a"))
            for half in range(2):
                proj(f"mx{2 + half}", 512, lambda ti, t0, np_, cA, sA, ps, pk, half=half:
                     k.cp("act", v_tm[0:np_, ti, half * 512:(half + 1) * 512], ps[0:np_, :], [pk], ["R1a"]))
            for half in range(2):
                proj(f"mx{4 + half}", 512, lambda ti, t0, np_, cA, sA, ps, pk, half=half:
                     k.act(sg_tm[0:np_, ti, half * 512:(half + 1) * 512], ps[0:np_, :], AF.Silu, [pk], ["R1b"]))
            pend_om = []

            def retention_gen(nf, nb):
              for ti, (t0, np_, cA, sA) in enumerate(tiles):
                cfgs = ret_cfg[ti]
                b = ti % 2
                bp, bk = nb()
                for h in range(4):
                    k.tr(bp[:, h * 128: h * 128 + np_], q_rot[0:np_, ti, h * 128:(h + 1) * 128],
                         ident_b[0:np_, 0:np_], ["R1c", "ident_b"], [bk])
                qps = bp[:, 0:512].rearrange("p (h n) -> p h n", h=4)[:, :, 0:np_]
                k.cp("act", qTt[b][:, :, 0:np_], qps, [bk], [f"qTt{b}"])
                qd_list = []
                for si, cfg in enumerate(cfgs):
                    qd = qdTt[b] if si == 0 else qdTt2
                    qdk = f"qdTt{b}" if si == 0 else "qdTt2"
                    k.tt("dve", qd[:, :, 0:np_], qTt[b][:, :, 0:np_], cs[cfg["idec"]][:, :, 0:np_], ALU.mult,
                         [f"qTt{b}", "c_" + cfg["idec"]], [qdk])
                    qd_list.append((qd, qdk))
                for si, cfg in enumerate(cfgs):
                    for h in range(4):
                        col = cfg["kdec_col"] + h
                        dstk = k_dec[0:np_, ti, h * 128:(h + 1) * 128] if si == 0 else \
                            kd2[0:np_, h * 128:(h + 1) * 128]
                        k.ts("pool", dstk, k_rot[0:np_, ti, h * 128:(h + 1) * 128],
                             cs[cfg["kdec"]][0:np_, col:col + 1], None, ALU.mult, None,
                             ["R2a"], ["R2b" if si == 0 else "kd2"], sreads=["c_" + cfg["kdec"]])
                yield
                bp2, bk2 = nb()
                for h in range(4):
                    k.tr(bp2[:, h * 128: h * 128 + np_], k_rot[0:np_, ti, h * 128:(h + 1) * 128],
                         ident_b[0:np_, 0:np_], ["R2a", "ident_b"], [bk2])
                k.cp("act", kTt[b][:, :, 0:np_],
                     bp2[:, 0:512].rearrange("p (h n) -> p h n", h=4)[:, :, 0:np_], [bk2], [f"kTt{b}"])
                omt, omk = om[0], "om0"
                nh = 4
                ap_, apk = nf()
                for h in range(nh):
                    k.mm(ap_[0:np_, h * 128: h * 128 + np_], kTt[b][:, h, 0:np_], qTt[b][:, h, 0:np_], True, True,
                         [f"kTt{b}", f"qTt{b}"], [apk])
                yield
                for h in range(nh):
                    k.tt("dve", attT[h][0:np_, 0:np_], ap_[0:np_, h * 128: h * 128 + np_],
                         cs[cfgs[0]["DT"]][0:np_, h, 0:np_], ALU.mult, [apk, "c_" + cfgs[0]["DT"]], [f"attT{h}"])
                yield
                obank = {}
                for h in range(nh):
                    if h % 2 == 0:
                        ob, obk = pf[4 + h // 2], f"pf{4 + h // 2}"
                    obank[h] = (ob[0:np_, (h % 2) * 256:(h % 2) * 256 + 256], obk)
                    op_, opk = obank[h]
                    for si, cfg in enumerate(cfgs):
                        qd, qdk = qd_list[si]
                        k.mm(op_, qd[:, h, 0:np_], cfg["state_b"][:, h, :], si == 0, False,
                             [qdk, cfg["sbkey"]], [opk])
                    k.mm(op_, attT[h][0:np_, 0:np_], v_tm[0:np_, ti, h * 256:(h + 1) * 256],
                         False, True, [f"attT{h}", "R1a"], [opk])
                sbank = {}
                cnt_s = 0
                for h in range(nh):
                    for si, cfg in enumerate(cfgs):
                        if cnt_s % 2 == 0:
                            sb2, sbk2 = nf()
                        sp_ = sb2[:, (cnt_s % 2) * 256:(cnt_s % 2) * 256 + 256]
                        cnt_s += 1
                        kd_ap = k_dec[0:np_, ti, h * 128:(h + 1) * 128] if si == 0 else \
                            kd2[0:np_, h * 128:(h + 1) * 128]
                        k.mm(sp_, kd_ap, v_tm[0:np_, ti, h * 256:(h + 1) * 256], True, True,
                             ["R2b" if si == 0 else "kd2", "R1a"], [sbk2])
                        sbank[(h, si)] = (sp_, sbk2)
                if pend_om:
                    pend_om.pop(0)()
                yield
                for h in range(nh):
                    for si, cfg in enumerate(cfgs):
                        sp_, spk = sbank[(h, si)]
                        k.stt("dve", cfg["state"][:, h, :], cfg["state"][:, h, :], cfg["cdec"][h],
                              sp_, ALU.mult, ALU.add, [cfg["skey"], spk], [cfg["skey"]])
                        k.cp("pool", cfg["state_b"][:, h, :], cfg["state"][:, h, :], [cfg["skey"]],
                             [cfg["sbkey"]])
                yield
                for h in range(nh):
                    op_, opk = obank[h]
                    k.act(junk4v[h][0][0:np_, :], op_, AF.Square, [opk], [junk4v[h][1]])
                for h in range(nh):
                    k.op("dve", lambda e, h=h, np_=np_: e.reduce_sum(out=sm[0:np_, 16 + h:17 + h],
                                                                    in_=junk4v[h][0][0:np_, :], axis=AX.X),
                         reads=[junk4v[h][1]], writes=["smo"])
                if nh:
                    k.act(sm[0:np_, 16:20], sm[0:np_, 16:20], AF.Sqrt, ["smo"], ["smo"], bias=EPS, scale=1.0 / 256)
                    k.op("dve", lambda e, np_=np_: e.reciprocal(out=sm[0:np_, 16:20], in_=sm[0:np_, 16:20]),
                         reads=["smo"], writes=["smo"])
                for h in range(nh):
                    op_, opk = obank[h]
                    k.stt("dve", omt[0:np_, h * 256:(h + 1) * 256], op_,
                          sm[0:np_, 16 + h:17 + h], sg_tm[0:np_, ti, h * 256:(h + 1) * 256],
                          ALU.mult, ALU.mult, [opk, "R1b"], [omk], sreads=["smo"])

                def om_tr(t0=t0, np_=np_, omt=omt, omk=omk):
                    bp3, bk3 = nb()
                    for kc in range(8):
                        k.tr(bp3[:, kc * 128: kc * 128 + np_], omt[0:np_, kc * 128:(kc + 1) * 128],
                             ident_b[0:np_, 0:np_], [omk, "ident_b"], [bk3])
                    k.cp("act", omT[:, :, t0:t0 + np_],
                         bp3[:, :].rearrange("p (a n) -> p a n", a=8)[:, :, 0:np_], [bk3], ["omT"])
                pend_om.append(om_tr)
                yield
              while pend_om:
                pend_om.pop(0)()
            pbf = [pb[0][:, :].bitcast(F32), pb[1][:, :].bitcast(F32)]

            def merge1_gen():
                for dc in range(8):
                    w, wk = wload(f"mgA{dc}")
                    for kc in range(8):
                        k.mm(pbf[0][:, 0:n], w[:, kc * 128:(kc + 1) * 128], omT[:, kc, 0:n], kc == 0, kc == 7,
                             [wk, "omT"], ["pb0"])
                        yield
                    for kc in range(8):
                        k.mm(pbf[1][:, 0:n], w[:, 1024 + kc * 128: 1024 + (kc + 1) * 128], hT[:, kc, 0:n],
                             kc == 0, kc == 7, [wk, f"hT{kc}"], ["pb1"])
                        yield
                    sg = sgt[dc % 2]
                    sk = f"sgt{dc % 2}"
                    k.act(sg[:, 0:n], pbf[1][:, 0:n], AF.Exp, ["pb1"], [sk], scale=-1.0)
                    k.ts("dve", sg[:, 0:n], sg[:, 0:n], 1.0, None, ALU.add, None, [sk], [sk])
                    k.op("dve", lambda e, sg=sg: e.reciprocal(out=sg[:, 0:n], in_=sg[:, 0:n]),
                         reads=[sk], writes=[sk])
                    k.tt("dve", mT[:, dc, 0:n], sg[:, 0:n], pbf[0][:, 0:n], ALU.mult, [sk, "pb0"],
                         ["R2a" if dc < 4 else "R2b"])
                    yield

            filler = merge1_gen()
            if interleave:
                fb_rf = mk_rot([(pf[2], "pf2"), (pf[3], "pf3")])
                fb_rb = mk_rot([(pb[0], "pb0")])
                fb_ff = mk_rot([(pf[0], "pf0"), (pf[1], "pf1")])
                fb_fb = mk_rot([(pb[1], "pb1")])
            else:
                fb_rf, fb_rb, fb_ff, fb_fb = nf, nb, nf, nb
            gen_r = retention_gen(fb_rf, fb_rb)
            gen_f = fox_fn(proj_gen, filler, fb_ff, fb_fb)
            if interleave:
                r_alive, f_alive = True, True
                while r_alive or f_alive:
                    if r_alive:
                        r_alive = next(gen_r, "END") != "END"
                    if f_alive:
                        v_ = next(gen_f, "END")
                        if v_ == "ATT" or v_ == "END":
                            f_alive = False
            else:
                for _ in gen_r:
                    pass
            for _ in gen_f:
                pass
            for _ in filler:
                pass
            for dc in range(8):
                w, wk = wload(f"mgB{dc}")
                bfp, bfk = nf()
                for h in range(8):
                    k.mm(bfp[:, 0:n], w[0:64, h * 128:(h + 1) * 128], ofT[0:64, h, 0:n],
                         h == 0, h == 7, [wk, "R1b"], [bfk])
                gfp, gfk = nf()
                for kc in range(8):
                    k.mm(gfp[:, 0:n], w[:, 1024 + kc * 128: 1024 + (kc + 1) * 128], hT[:, kc, 0:n],
                         kc == 0, kc == 7, [wk, f"hT{kc}"], [gfk])
                sg2 = sgt[dc % 2]
                sk2 = f"sgt{dc % 2}"
                mkey = "R2a" if dc < 4 else "R2b"
                k.act(sg2[:, 0:n], gfp[:, 0:n], AF.Sigmoid, [gfk], [sk2])
                k.tt("dve", sg2[:, 0:n], sg2[:, 0:n], bfp[:, 0:n], ALU.mult, [sk2, bfk], [sk2])
                k.tt("pool", mT[:, dc, 0:n], mT[:, dc, 0:n], sg2[:, 0:n], ALU.add, [mkey, sk2], [mkey])
            for cidx in range(2):
                w, wk = wload(f"wo{cidx}")
                for dl in range(4):
                    dc = cidx * 4 + dl
                    yp, yk = nf()
                    for kc in range(8):
                        k.mm(yp[:, 0:n], w[:, kc * 512 + dl * 128: kc * 512 + dl * 128 + 128], mT[:, kc, 0:n],
                             kc == 0, kc == 7, [wk, "R2a" if kc < 4 else "R2b"], [yk])
                    k.tt("dve", xT[:, dc, 0:n], xT[:, dc, 0:n], yp[:, 0:n], ALU.add, [f"xT{dc}", yk], [f"xT{dc}"])

        def logf_tile(ps, pk, np_, dst_sm):
            u = sm[0:np_, 32:40]
            nu = sm[0:np_, 40:48]
            k.tt("dve", u, ps[0:np_, 0:8], bfg[0:np_, :], ALU.add, [pk, "bfg"], ["smu"])
            k.ts("dve", nu, u, -1.0, None, ALU.mult, None, ["smu"], ["smnu"])
            k.tt("dve", nu, nu, u, ALU.min, ["smnu", "smu"], ["smnu"])
            k.act(nu, nu, AF.Exp, ["smnu"], ["smnu"])
            k.act(nu, nu, AF.Ln, ["smnu"], ["smnu"], bias=1.0)
            k.ts("dve", u, u, 0.0, None, ALU.min, None, ["smu"], ["smu"])
            k.tt("dve", sm[0:np_, dst_sm:dst_sm + 8], u, nu, ALU.subtract, ["smu", "smnu"], ["smlf"])

        def fox_prompt_factory(seq, g):
            def fox(proj_gen, filler, nf, nb):
                n = G
                tiles_g = [g * 4 + ti for ti in range(4)]

                def ev_f(ti, t0, np_, cA, sA, ps, pk):
                    T = tiles_g[ti]
                    logf_tile(ps, pk, np_, 48)
                    lf = sm[0:np_, 48:56]
                    k.cp("dve", sm[0:np_, 64 + ti * 8: 72 + ti * 8], lf, ["smlf"], [f"lfo{ti}"])
                    k.dma("pool", O["lf_p"][seq, T * 128:(T + 1) * 128, :], sm[0:np_, 64 + ti * 8: 72 + ti * 8],
                          reads=[f"lfo{ti}"], dkey=f"d_lfo{ti}")
                    cp_, cpk = nf()
                    k.mm(cp_[0:np_, 0:8], cs["tri"][0:np_, 0:np_], lf, True, True, ["c_tri", "smlf"], [cpk])
                    tp_, tpk = nf()
                    k.mm(tp_[:, 0:8], cs["ones"][0:np_, :], lf, True, True, ["c_ones", "smlf"], [tpk])
                    cq = cqs[0:np_, ti, :]
                    k.tt("dve", cq, cp_[0:np_, 0:8], lfc[0:np_, :], ALU.add, [cpk, "lfc"], [f"cq{ti}"])
                    k.tt("dve", lfc[:, :], lfc[:, :], tp_[:, 0:8], ALU.add, ["lfc", tpk], ["lfc"])
                    k.ts("dve", biasK[0:np_, T, :], cq, -1.0, None, ALU.mult, None, [f"cq{ti}"], [f"bK{T}"])

                def ev_q(ti, t0, np_, cA, sA, ps, pk):
                    k3 = qknorm(ps, pk, np_, gq, "gq")
                    b = ti % 2
                    k.cp("act", qa[b][0:np_, :, 0:64], k3, ["kn32"], [f"qa{b}"])
                    k.cp("dve", qa[b][0:np_, :, 64:65], cqs[0:np_, ti, :].unsqueeze(2), [f"cq{ti}"], [f"qa{b}"])
                    bp, bk = nb()
                    for h in range(8):
                        k.tr(bp[0:65, h * 128:(h + 1) * 128], qa[b][0:np_, h, 0:65], ident_b[:, :],
                             [f"qa{b}", "ident_b"], [bk])
                    k.cp("act", qTa[0:65, :, t0:t0 + np_],
                         bp[0:65, :].rearrange("p (h n) -> p h n", h=8), [bk], ["qTa"])

                def ev_k(ti, t0, np_, cA, sA, ps, pk):
                    k3 = qknorm(ps, pk, np_, gk, "gk")
                    b = ti % 2
                    k.cp("act", ka[b][0:np_, :, 0:64], k3, ["kn32"], [f"ka{b}"])
                    T = tiles_g[ti]
                    k.dma("pool", O["k_p"][seq, T * 128:(T + 1) * 128, :], kn32[0:np_, :], reads=["kn32"],
                          dkey="d_kn32")
                    bp, bk = nb()
                    for h in range(8):
                        k.tr(bp[0:65, h * 128:(h + 1) * 128], ka[b][0:np_, h, 0:65], ident_b[:, :],
                             [f"ka{b}", "ident_b"], [bk])
                    k.cp("dve", kTa[0:65, :, T * 128:(T + 1) * 128],
                         bp[0:65, :].rearrange("p (h n) -> p h n", h=8), [bk], [f"kTa{T}"])

                def ev_v(ti, t0, np_, cA, sA, ps, pk):
                    T = tiles_g[ti]
                    vo, vk = nio()
                    k.cp("act", vo[0:np_, 0:512], ps[0:np_, :], [pk], [vk])
                    k.dma("pool", O["v_p"][seq, T * 128:(T + 1) * 128, :], vo[0:np_, 0:512], reads=[vk],
                          dkey="d_" + vk)
                    k.cp("dve", va[0:np_, T, :, 0:64], vo[0:np_, 0:512].rearrange("p (h d) -> p h d", h=8),
                         [vk], [f"va{T}"])

                yield from proj_gen("mxff", 8, ev_f, nf)
                yield from proj_gen("mx6", 512, ev_q, nf)
                yield from proj_gen("mx7", 512, ev_k, nf)
                yield from proj_gen("mx8", 512, ev_v, nf)
                yield "ATT"
                nf = nf_all
                nkt = g * 4 + 4
                steps = [(h, kt) for h in range(8) for kt in range(nkt)]
                Sinfo = {}

                def emit_S(i):
                    h, kt = steps[i]
                    sp_, spk = nf()
                    lhs = kTa[0:65, h, kt * 128:(kt + 1) * 128]
                    if kt < g * 4:
                        q0 = 0
                        k.mm(sp_[:, 0:G], lhs, qTa[0:65, h, 0:G], True, True, [f"kTa{kt}", "qTa"], [spk])
                    else:
                        q0 = (kt - g * 4) * 128
                        k.mm(sp_[:, q0:q0 + 128], lhs, qTa[0:65, h, q0:q0 + 128], True, False,
                             [f"kTa{kt}", "qTa"], [spk])
                        k.mm(sp_[:, q0:q0 + 128], ident_b[:, :], maskT_b[:, :], False, True,
                             ["ident_b", "maskT_b"], [spk])
                        if q0 + 128 < G:
                            k.mm(sp_[:, q0 + 128:G], lhs, qTa[0:65, h, q0 + 128:G], True, True,
                                 [f"kTa{kt}", "qTa"], [spk])
                    Sinfo[i] = (sp_, spk, q0)

                def finalize(h):
                    acc, acck = pf[4 + h % 2], f"pf{4 + h % 2}"
                    k.op("dve", lambda e, acc=acc: e.reciprocal(out=rden[64:65, :], in_=acc[64:65, :]),
                         reads=[acck], writes=["rden", "rot0", "rot1"])
                    bc, bck = nf()
                    k.mm(bc[0:64, :], cs["ones"][64:65, 0:64], rden[64:65, :], True, True, ["c_ones", "rden"],
                         [bck])
                    k.cp("act", bcs[0:64, :], bc[0:64, :], [bck], ["bcs", "rot0", "rot1"])
                    k.tt("dve", ofT[0:64, h, :], acc[0:64, :], bcs[0:64, :], ALU.mult, [acck, "bcs"], ["R1b"])

                pending = []
                nfill = -(-128 // max(len(steps), 1))
                for i, (h, kt) in enumerate(steps):
                    for _ in range(nfill):
                        next(filler, None)
                    if i == 0:
                        emit_S(0)
                    if i + 1 < len(steps):
                        emit_S(i + 1)
                    sp_, spk, q0 = Sinfo.pop(i)
                    acc, acck = pf[4 + h % 2], f"pf{4 + h % 2}"
                    pt, ptk = PT[i % 3], f"PT{i % 3}"
                    k.act(pt[:, q0:G], sp_[:, q0:G], AF.Exp, [spk], [ptk], sreads=[f"bK{kt}"],
                          bias=biasK[:, kt, h:h + 1])
                    k.mm(acc[0:65, q0:G], va[:, kt, h, 0:65], pt[:, q0:G], kt == 0, kt == nkt - 1,
                         [f"va{kt}", ptk], [acck])
                    if kt == nkt - 1:
                        pending.append((i + 2, h))
                    while pending and pending[0][0] <= i:
                        finalize(pending.pop(0)[1])
                for _, h in pending:
                    finalize(h)
            return fox

        k.op("dve", lambda e: e.memset(lfc[:, :], 0.0), writes=["lfc"])
        for seq in range(nseq):
            k.op("pool", lambda e: e.memset(state[0][:], 0.0), writes=["state0"])
            k.op("pool", lambda e: e.memset(state_b[0][:], 0.0), writes=["stateb0"])
            k.op("dve", lambda e: e.memset(lfc[:, :], 0.0), writes=["lfc"])
            for g in range(ngroups):
                n = G
                xs, ps_, ys = [], [], []
                tiles = []
                for ti in range(4):
                    r0 = g * G + ti * 128
                    xs.append(([(I["x_p"][seq, r0:r0 + 128, :], 0, 128)], ti * 128, 128))
                    ps_.append(([(I["p_p"][seq, r0:r0 + 128, :], 0, 128)], ti * 128, 128))
                    ys.append(([(O["y_p"][seq, r0:r0 + 128, :], 0, 128)], ti * 128, 128))
                    tiles.append((ti * 128, 128, cosg[:, ti, :], sing[:, ti, :]))
                k.dma("pool", cosg[:, :, :], C["cos"][:, g * 4:(g + 1) * 4, :], writes=["cosg"], dkey="d_cosg")
                k.dma("pool", sing[:, :, :], C["sin"][:, g * 4:(g + 1) * 4, :], writes=["sing"], dkey="d_sing")
                load_group(xs, ps_, n)
                ffn("f1", n)
                cfg = {"idec": "idec", "kdec": "kdec", "kdec_col": 0, "DT": "DT", "state": state[0],
                       "state_b": state_b[0], "skey": "state0", "sbkey": "stateb0", "cdec": CT["cdec"]}
                mixer(n, tiles, [[cfg]] * 4, fox_prompt_factory(seq, g), True)
                ffn("f2", n)
                ple(n)
                store_group(ys, n)
            for h in range(4):
                k.dma("pool", O["st_p"][seq, h], state[0][:, h, :], reads=["state0"], dkey="d_stout")

        if do_sample:
            k.barrier()
            n = 48
            for nm in ("DT_s", "idec_s0", "idec_s1"):
                cs[nm] = late[nm]
                k.dma("pool", late[nm], C[nm], writes=["c_" + nm], dkey="d_late_" + nm)
            k.dma("pool", cosg[:, 0, :], C["cos_s"], writes=["cosg"], dkey="d_cosg")
            k.dma("pool", sing[:, 0, :], C["sin_s"], writes=["sing"], dkey="d_sing")
            for i in range(2):
                k.op("dve", lambda e, t=vca[i]: e.memset(t, 1.0), writes=[f"vca{i}"])
            xs = [([(I["x_s"][0], 0, 16), (I["x_s"][1], 32, 16)], 0, 48)]
            ps_ = [([(I["p_s"][0], 0, 16), (I["p_s"][1], 32, 16)], 0, 48)]
            ys = [([(O["y_s"][0], 0, 16), (O["y_s"][1], 32, 16)], 0, 48)]
            tiles = [(0, 48, cosg[0:48, 0, :], sing[0:48, 0, :])]
            for sq in range(2):
                for h in range(4):
                    k.dma("pool", state[sq][:, h, :], I["st_in"][sq, h], writes=[f"state{sq}"],
                          dkey=f"d_stin{sq}")
                k.cp("dve", state_b[sq][:, :, :], state[sq][:, :, :], [f"state{sq}"], [f"stateb{sq}"])
            load_group(xs, ps_, n)
            ffn("f1", n)
            cfgs = [{"idec": f"idec_s{sq}", "kdec": "kdec_s", "kdec_col": sq * 4, "DT": "DT_s",
                     "state": state[sq], "state_b": state_b[sq], "skey": f"state{sq}",
                     "sbkey": f"stateb{sq}", "cdec": CT["cdec_s"]} for sq in range(2)]

            def fox_sample(proj_gen, filler, nf, nb):
                np_ = 48

                def ev_f(ti, t0, np_, cA, sA, ps, pk):
                    logf_tile(ps, pk, np_, 48)
                    lf = sm[0:np_, 48:56]
                    k.cp("dve", sm[0:np_, 64:72], lf, ["smlf"], ["lfo0"])
                    for sq in range(2):
                        k.dma("pool", O["lf_s"][sq], sm[sq * 32: sq * 32 + 16, 64:72], reads=["lfo0"],
                              dkey="d_lfo0")
                    cp_, cpk = nf()
                    k.mm(cp_[0:np_, 0:8], cs["tri_s"][0:np_, 0:np_], lf, True, True, ["c_tri_s", "smlf"], [cpk])
                    cq = cqs[0:np_, 0, :]
                    k.cp("dve", cq, cp_[0:np_, 0:8], [cpk], ["cq0"])
                    k.ts("dve", biasK[0:np_, 0, :], cq, -1.0, None, ALU.mult, None, ["cq0"], ["bK0"])

                def ev_q(ti, t0, np_, cA, sA, ps, pk):
                    k3 = qknorm(ps, pk, np_, gq, "gq")
                    k.op("dve", lambda e: e.memset(qa[0][:], 0.0), writes=["qa0"])
                    k.cp("act", qa[0][0:np_, :, 0:64], k3, ["kn32"], ["qa0"])
                    k.cp("dve", qa[0][0:np_, :, 64:65], cqs[0:np_, 0, :].unsqueeze(2), ["cq0"], ["qa0"])
                    bp, bk = nb()
                    for h in range(8):
                        k.tr(bp[0:65, h * 128: h * 128 + np_], qa[0][0:np_, h, 0:65], ident_b[0:np_, 0:np_],
                             ["qa0", "ident_b"], [bk])
                    k.cp("act", qTa[0:65, :, 0:np_],
                         bp[0:65, :].rearrange("p (h n) -> p h n", h=8)[:, :, 0:np_], [bk], ["qTa"])

                def ev_k(ti, t0, np_, cA, sA, ps, pk):
                    k3 = qknorm(ps, pk, np_, gk, "gk")
                    k.op("dve", lambda e: e.memset(ka[0][:], 1.0), writes=["ka0"])
                    k.cp("act", ka[0][0:np_, :, 0:64], k3, ["kn32"], ["ka0"])
                    for sq in range(2):
                        k.dma("pool", O["k_s"][sq], kn32[sq * 32: sq * 32 + 16, :], reads=["kn32"],
                              dkey="d_kn32")
                    bp, bk = nb()
                    for h in range(8):
                        k.tr(bp[0:65, h * 128: h * 128 + np_], ka[0][0:np_, h, 0:65], ident_b[0:np_, 0:np_],
                             ["ka0", "ident_b"], [bk])
                    k.cp("dve", kTs[0:65, :, 0:np_],
                         bp[0:65, :].rearrange("p (h n) -> p h n", h=8)[:, :, 0:np_], [bk], ["kTs"])

                def ev_v(ti, t0, np_, cA, sA, ps, pk):
                    vo, vk = nio()
                    k.cp("act", vo[0:np_, 0:512], ps[0:np_, :], [pk], [vk])
                    for sq in range(2):
                        k.dma("pool", O["v_s"][sq], vo[sq * 32: sq * 32 + 16, 0:512], reads=[vk], dkey="d_" + vk)
                    k.cp("dve", va[0:np_, 0, :, 0:64], vo[0:np_, 0:512].rearrange("p (h d) -> p h d", h=8),
                         [vk], ["va0"])

                yield from proj_gen("mxff", 8, ev_f, nf)
                yield from proj_gen("mx6", 512, ev_q, nf)
                yield from proj_gen("mx7", 512, ev_k, nf)
                yield from proj_gen("mx8", 512, ev_v, nf)
                yield "ATT"
                k.op("dve", lambda e: e.memset(ofT[0:64, :, 0:48], 0.0), writes=["R1b"])
                for sq in range(2):
                    qs = sq * 32
                    k.dma("pool", clf_sb[:, :, :], I["clf"][sq].rearrange("(t p) h -> p t h", p=128),
                          writes=["clf_sb"], dkey="d_clf")
                    wp_, wpk = nf()
                    k.mm(wp_[:, 0:256], cs["triS"][:, :], clf_sb[:, :, :].rearrange("p t h -> p (t h)"),
                         True, True, ["c_triS", "clf_sb"], [wpk])
                    tp_, tpk = nf()
                    k.mm(tp_[:, 0:256], cs["ones"][:, :], clf_sb[:, :, :].rearrange("p t h -> p (t h)"),
                         True, True, ["c_ones", "clf_sb"], [tpk])
                    k.op("dve", lambda e: e.memset(sufc[:, 32, :], 0.0), writes=["sufc"])
                    tot = tp_[:, 0:256].rearrange("p (t h) -> p t h", h=8)
                    for t in range(31, 0, -1):
                        k.tt("dve", sufc[:, t, :], sufc[:, t + 1, :], tot[:, t, :], ALU.add, ["sufc", tpk], ["sufc"])
                    k.tt("dve", suf[:, 0:31, :], wp_[:, 0:248].rearrange("p (t h) -> p t h", h=8),
                         sufc[:, 1:32, :], ALU.add, [wpk, "sufc"], ["suf"])
                    k.cp("dve", suf[:, 31, :], wp_[:, 248:256], [wpk], ["suf"])
                    accs, accsk = sgt[0], "sgt0"
                    slots = {}
                    banks = {}

                    def st0(t):
                        b = t % 2
                        if t % 2 == 0:
                            ks_ = cstage[cst_i[0] % 8]
                            vs_ = cstage[(cst_i[0] + 1) % 8]
                            cst_i[0] += 2
                            k.dma("sp", ks_[0].rearrange("p (t c) -> p t c", t=2),
                                  I["ck"][sq, t * 128:(t + 2) * 128, :].rearrange("(t p) c -> p t c", p=128),
                                  writes=[ks_[1]], dkey="d_sp_" + ks_[1])
                            k.dma("sp", vs_[0].rearrange("p (t c) -> p t c", t=2),
                                  I["cv"][sq, t * 128:(t + 2) * 128, :].rearrange("(t p) c -> p t c", p=128),
                                  writes=[vs_[1]], dkey="d_sp_" + vs_[1])
                            slots[t // 2] = (ks_, vs_)
                        (kslot, kik), (vslot, vik) = slots[t // 2]
                        ki = kslot[:, (t % 2) * 512:(t % 2 + 1) * 512]
                        vi = vslot[:, (t % 2) * 512:(t % 2 + 1) * 512]
                        k.cp("act", ka[b][:, :, 0:64], ki.rearrange("p (h d) -> p h d", h=8), [kik], [f"ka{b}"])
                        k.cp("dve", vca[b][:, :, 0:64], vi.rearrange("p (h d) -> p h d", h=8), [vik],
                             [f"vca{b}"])
                        bp, bk = nb()
                        for h in range(8):
                            k.tr(bp[0:65, h * 128:(h + 1) * 128], ka[b][:, h, 0:65], ident_b[:, :],
                                 [f"ka{b}", "ident_b"], [bk])
                        banks[("bp", t)] = (bp, bk)

                    def st1(t):
                        b = t % 2
                        bp, bk = banks.pop(("bp", t))
                        k.cp("act", kcT[b][0:65, :, :], bp[0:65, :].rearrange("p (h n) -> p h n", h=8), [bk],
                             [f"kcT{b}"])
                        sp_, spk = nf()
                        for h in range(8):
                            k.mm(sp_[:, h * 16:(h + 1) * 16], kcT[b][0:65, h, :], qTa[0:65, h, qs:qs + 16],
                                 True, True, [f"kcT{b}", "qTa"], [spk])
                        banks[("sp", t)] = (sp_, spk)

                    def st2(t):
                        b = t % 2
                        sp_, spk = banks.pop(("sp", t))
                        for h in range(8):
                            k.act(PTs[b][:, h, :], sp_[:, h * 16:(h + 1) * 16], AF.Exp, [spk], [f"PTs{b}"],
                                  sreads=["suf"], bias=suf[:, t, h:h + 1])
                        op2, op2k = nf()
                        for h in range(8):
                            k.mm(op2[0:65, h * 16:(h + 1) * 16], vca[b][:, h, 0:65], PTs[b][:, h, :],
                                 True, True, [f"vca{b}", f"PTs{b}"], [op2k])
                        if t == 0:
                            k.cp("dve", accs[0:65, 0:128], op2[0:65, 0:128], [op2k], [accsk])
                        else:
                            k.tt("dve", accs[0:65, 0:128], accs[0:65, 0:128], op2[0:65, 0:128], ALU.add,
                                 [accsk, op2k], [accsk])

                    for it in range(32 + 2):
                        if 0 <= it - 2 < 32:
                            st2(it - 2)
                        if 0 <= it - 1 < 32:
                            st1(it - 1)
                        if it < 32:
                            st0(it)
                    sp_, spk = nf()
                    for h in range(8):
                        k.mm(sp_[0:48, h * 16:(h + 1) * 16], kTs[0:65, h, 0:48], qTa[0:65, h, qs:qs + 16],
                             True, False, ["kTs", "qTa"], [spk])
                        k.mm(sp_[0:48, h * 16:(h + 1) * 16], ident_b[0:48, 0:48], masks_b[0:48, qs:qs + 16],
                             False, True, ["ident_b", "masks_b"], [spk])
                    for h in range(8):
                        k.act(PTs[0][0:48, h, :], sp_[0:48, h * 16:(h + 1) * 16], AF.Exp, [spk], ["PTs0"],
                              sreads=["bK0"], bias=biasK[0:48, 0, h:h + 1])
                    op2, op2k = nf()
                    for h in range(8):
                        k.mm(op2[0:65, h * 16:(h + 1) * 16], va[0:48, 0, h, 0:65], PTs[0][0:48, h, :],
                             True, True, ["va0", "PTs0"], [op2k])
                    k.tt("dve", accs[0:65, 0:128], accs[0:65, 0:128], op2[0:65, 0:128], ALU.add,
                         [accsk, op2k], [accsk])
                    k.op("dve", lambda e: e.reciprocal(out=rden[64:65, 0:128], in_=accs[64:65, 0:128]),
                         reads=[accsk], writes=["rden", "rot0", "rot1"])
                    bc, bck = nf()
                    k.mm(bc[0:64, 0:128], cs["ones"][64:65, 0:64], rden[64:65, 0:128], True, True,
                         ["c_ones", "rden"], [bck])
                    k.cp("act", bcs[0:64, 0:128], bc[0:64, 0:128], [bck], ["bcs", "rot0", "rot1"])
                    k.tt("dve", ofT[0:64, :, qs:qs + 16], accs[0:64, 0:128].rearrange("p (h q) -> p h q", h=8),
                         bcs[0:64, 0:128].rearrange("p (h q) -> p h q", h=8), ALU.mult, [accsk, "bcs"], ["R1b"])

            mixer(n, tiles, [cfgs], fox_sample, False)
            for sq in range(2):
                for h in range(4):
                    k.dma("pool", O["st_s"][sq, h], state[sq][:, h, :], reads=[f"state{sq}"], dkey="d_stout")
            ffn("f2", n)
            ple(n)
            store_group(ys, n)

        k.wait_all("pool", [kk for kk in k.sem if str(kk).startswith("d_")])
        k.replay()
        print("SBUF bytes/partition:", k.sbytes, " instr:", {e: len(s) for e, s in k.streams.items()})
    return nc


_PROG = {}


def kernel(**inp):
    f = lambda a: np.ascontiguousarray(np.asarray(a, dtype=np.float32))
    if "nc" not in _PROG:
        _PROG["nc"] = build_program()
    nc = _PROG["nc"]
    CT = _consts()
    in_maps = []
    for c in range(NCORES):
        m = {}
        m["x_p"] = f(inp["x_prompt"][c * NSEQ:(c + 1) * NSEQ])
        m["p_p"] = f(inp["p_prompt"][0, c * NSEQ:(c + 1) * NSEQ])
        m["x_s"] = f(inp["x_sample"][c * NSS:(c + 1) * NSS])
        m["p_s"] = f(inp["p_sample"][0, c * NSS:(c + 1) * NSS])
        m["st_in"] = f(inp["state_ret"][0, c * NSS:(c + 1) * NSS])
        m["ck"] = f(inp["cache_fox_k"][0, c * NSS:(c + 1) * NSS]).reshape(NSS, PAST, 512)
        m["cv"] = f(inp["cache_fox_v"][0, c * NSS:(c + 1) * NSS]).reshape(NSS, PAST, 512)
        m["clf"] = f(inp["cache_fox_logf"][0, c * NSS:(c + 1) * NSS])
        for n in ("norm_ffn1_g", "ffn1_w_in", "ffn1_w_out", "norm_mix_g", "w_in_mix", "b_forget",
                  "q_norm_g", "k_norm_g", "w_br_ret", "w_br_fox", "w_out", "norm_ffn2_g", "ffn2_w_in",
                  "ffn2_w_out", "norm_ple_g", "w_ple", "w_ple_gate"):
            m[n] = f(inp[n][0])
        for n in CONST_SHAPES:
            m["c_" + n] = f(CT[n])
        in_maps.append(m)
    res = run_bass_kernel_spmd(nc, in_maps, core_ids=list(range(NCORES)))
    R = res.results
    cat = lambda n: np.concatenate([np.asarray(r[n], dtype=np.float32) for r in R], axis=0)
    y_p = cat("y_p")
    y_s = cat("y_s")
    st_p = cat("st_p")[None]
    k_p = cat("k_p").reshape(1, 32, S, 8, 64)
    v_p = cat("v_p").reshape(1, 32, S, 8, 64)
    lf_p = cat("lf_p")[None]
    st_s = cat("st_s")[None]
    k_s = cat("k_s").reshape(1, 16, L, 8, 64)
    v_s = cat("v_s").reshape(1, 16, L, 8, 64)
    lf_s = cat("lf_s")[None]
    return (y_p, y_s, st_p, k_p, v_p, lf_p, st_s, k_s, v_s, lf_s)
```

```python
import contextlib
import numpy as np
import concourse.bass as bass
import concourse.mybir as mybir
from concourse.bass_utils import run_bass_kernel_spmd

F32 = mybir.dt.float32
BF16 = mybir.dt.bfloat16
AF = mybir.ActivationFunctionType
ALU = mybir.AluOpType
AX = mybir.AxisListType

NCORES = 8
D = 1024
S = 2048
NSEQ = 4
NSS = 2
L = 16
PAST = 4096
DFF = 2816
INC = 6664
EPS = 1e-6
G = 512
NG = S // G
SLOT = 4096
NSLOT = 3
NEG = -30000.0
ENGS = ("pe", "act", "dve", "pool", "sp")


class KB:
    def __init__(self, nc, es):
        self.nc = nc
        self.es = es
        self.streams = {e: [] for e in ENGS}
        self.sem = {}
        self.cnt = {}
        self.waited = {e: {} for e in ENGS}
        self.res = {}
        self.sbytes = 0
        for e in ENGS:
            self._mksem(e)

    def _mksem(self, key):
        if key not in self.sem:
            self.sem[key] = self.es.enter_context(self.nc.semaphore("s_" + str(key)))
            self.cnt[key] = 0
        return self.sem[key]

    def sb(self, name, shape, dt):
        n = 1
        for s in shape[1:]:
            n *= s
        self.sbytes += n * (4 if dt == F32 else 2)
        return self.es.enter_context(self.nc.sbuf_tensor(name, list(shape), dt))

    def ps(self, name, shape, dt):
        return self.es.enter_context(self.nc.psum_tensor(name, list(shape), dt))

    def _deps(self, eng, reads, writes, sreads):
        deps = {}

        def need(k, v):
            if v > deps.get(k, 0):
                deps[k] = v

        skip = eng if eng == "pe" else None
        for r in reads:
            st = self.res.get(r)
            if st and st["w"] and st["w"][0] != skip:
                need(*st["w"])
        for r in sreads:
            st = self.res.get(r)
            if st and st["w"]:
                need(*st["w"])
        for w in writes:
            st = self.res.get(w)
            if st:
                if st["w"] and st["w"][0] != skip:
                    need(*st["w"])
                for k, v in st["r"].items():
                    if k != skip:
                        need(k, v)
        waits = []
        wd = self.waited[eng]
        for k, v in deps.items():
            if wd.get(k, 0) < v:
                wd[k] = v
                waits.append((k, v))
        return waits

    def _mark(self, key, val, reads, writes):
        for r in reads:
            st = self.res.setdefault(r, {"w": None, "r": {}})
            st["r"][key] = val
        for w in writes:
            self.res[w] = {"w": (key, val), "r": {}}

    def op(self, eng, fn, reads=(), writes=(), sreads=()):
        waits = self._deps(eng, reads, writes, sreads)
        self.cnt[eng] += 1
        self.streams[eng].append((waits, fn, (eng, 1)))
        self._mark(eng, self.cnt[eng], tuple(reads) + tuple(sreads), writes)

    def dma(self, queue, out, in_, reads=(), writes=(), dkey=None, slow=False):
        self._mksem(dkey)
        waits = self._deps(queue, reads, writes, ())
        self.cnt[dkey] += 16
        if slow:
            fn = lambda e, o=out, i=in_: e.dma_start(out=o, in_=i, allow_slow_non_contiguous=True)
        else:
            fn = lambda e, o=out, i=in_: e.dma_start(out=o, in_=i)
        self.streams[queue].append((waits, fn, (dkey, 16)))
        self._mark(dkey, self.cnt[dkey], reads, writes)

    def barrier(self):
        for e in ENGS:
            waits = []
            for kk, v in self.cnt.items():
                if kk != e and v > 0 and self.waited[e].get(kk, 0) < v:
                    self.waited[e][kk] = v
                    waits.append((kk, v))
            self.streams[e].append((waits, None, None))

    def wait_all(self, eng, keys):
        waits = [(k, self.cnt[k]) for k in keys if self.cnt[k] > 0]
        self.streams[eng].append((waits, None, None))

    def replay(self):
        emap = {"pe": "tensor", "act": "scalar", "dve": "vector", "pool": "gpsimd", "sp": "sync"}
        with self.nc.Block() as block:
            for e in ENGS:
                stream = self.streams[e]
                if not stream:
                    continue

                def body(engine, stream=stream):
                    for waits, fn, inc in stream:
                        for kk, v in waits:
                            engine.wait_ge(self.sem[kk], v)
                        if fn is not None:
                            ins = fn(engine)
                            ins.then_inc(self.sem[inc[0]], inc[1])

                getattr(block, emap[e])(body)

    def mm(self, out, lhsT, rhs, start, stop, reads, writes):
        self.op("pe", lambda e: e.matmul(out, lhsT, rhs, start=start, stop=stop),
                reads=reads, writes=writes)

    def tr(self, out, in_, ident, reads, writes):
        self.op("pe", lambda e: e.transpose(out, in_, ident), reads=reads, writes=writes)

    def act(self, out, in_, func, reads, writes, sreads=(), bias=None, scale=None, accum_out=None):
        kw = {}
        if bias is not None:
            kw["bias"] = bias
        if scale is not None:
            kw["scale"] = scale
        if accum_out is not None:
            kw["accum_out"] = accum_out
        self.op("act", lambda e: e.activation(out=out, in_=in_, func=func, **kw),
                reads=reads, writes=writes, sreads=sreads)

    def cp(self, eng, out, in_, reads, writes):
        if eng == "act":
            self.op("act", lambda e: e.copy(out=out, in_=in_), reads=reads, writes=writes)
        else:
            self.op(eng, lambda e: e.tensor_copy(out=out, in_=in_), reads=reads, writes=writes)

    def tt(self, eng, out, in0, in1, op, reads, writes):
        self.op(eng, lambda e: e.tensor_tensor(out=out, in0=in0, in1=in1, op=op),
                reads=reads, writes=writes)

    def ts(self, eng, out, in0, s1, s2, op0, op1, reads, writes, sreads=()):
        if s2 is None:
            self.op(eng, lambda e: e.tensor_scalar(out=out, in0=in0, scalar1=s1, scalar2=None, op0=op0),
                    reads=reads, writes=writes, sreads=sreads)
        else:
            self.op(eng, lambda e: e.tensor_scalar(out=out, in0=in0, scalar1=s1, scalar2=s2, op0=op0, op1=op1),
                    reads=reads, writes=writes, sreads=sreads)

    def stt(self, eng, out, in0, scalar, in1, op0, op1, reads, writes, sreads=()):
        self.op(eng, lambda e: e.scalar_tensor_tensor(out=out, in0=in0, scalar=scalar, in1=in1,
                                                      op0=op0, op1=op1),
                reads=reads, writes=writes, sreads=sreads)


def _consts():
    c = {}
    c["ident"] = np.eye(128, dtype=np.float32)
    s = np.arange(128)
    c["tri"] = (s[:, None] <= s[None, :]).astype(np.float32)
    c["triS"] = (s[:, None] > s[None, :]).astype(np.float32)
    c["ones"] = np.ones((128, 128), np.float32)
    c["maskT"] = np.where(s[:, None] <= s[None, :], 0.0, NEG).astype(np.float32)
    blk = np.full(48, -1)
    blk[0:16] = 0
    blk[32:48] = 1
    pos = np.zeros(48)
    pos[0:16] = np.arange(16)
    pos[32:48] = np.arange(16)
    s48 = np.arange(48)
    same = (blk[:, None] == blk[None, :]) & (blk[:, None] >= 0)
    c["tri_s"] = np.zeros((128, 128), np.float32)
    c["tri_s"][:48, :48] = (same & (pos[:, None] <= pos[None, :])).astype(np.float32)
    c["mask_s"] = np.full((128, 128), NEG, np.float32)
    c["mask_s"][:48, :48] = np.where(same & (pos[:, None] <= pos[None, :]), 0.0, NEG)
    lg = np.log(1.0 - 2.0 ** (-5.0 - np.arange(4, dtype=np.float64)))
    i = np.arange(128)
    ch = i // 64
    DT = np.zeros((128, 4, 128), np.float64)
    for h in range(4):
        jj, ii = np.meshgrid(i, i, indexing="ij")
        samec = ch[jj] == ch[ii]
        later = (ch[ii] == 1) & (ch[jj] == 0)
        DT[:, h, :] = np.where(samec, np.exp(lg[h] * np.abs(ii - jj)),
                               np.where(later, np.exp(lg[h] * (ii - jj)), 0.0))
    KS = 128.0 ** -0.5
    c["DT"] = (DT * KS).astype(np.float32)
    idec = np.exp(lg[None, :, None] * (i[None, None, :] + 1.0))
    c["idec"] = np.broadcast_to(idec, (128, 4, 128)).astype(np.float32).copy()
    c["kdec"] = np.zeros((128, 8), np.float32)
    c["kdec"][:, 0:4] = np.exp(lg[None, :] * (127.0 - i[:, None])) * KS
    c["cdec"] = [float(np.exp(lg[h] * 128.0)) for h in range(4)]
    DTs = np.zeros((128, 4, 48), np.float64)
    for h in range(4):
        DTs[:48, h, :48] = np.where(same, np.exp(lg[h] * np.abs(pos[None, :] - pos[:, None])), 0.0)
    c["DT_s"] = (DTs * KS).astype(np.float32)
    for sq in range(2):
        t = np.zeros((128, 4, 48), np.float64)
        sel = blk == sq
        for h in range(4):
            t[:, h, :48] = np.where(sel, np.exp(lg[h] * (pos + 1.0)), 0.0)[None, :]
        c["idec_s%d" % sq] = t.astype(np.float32)
    kd = np.zeros((128, 8), np.float32)
    for sq in range(2):
        sel = blk == sq
        for h in range(4):
            kd[:48, sq * 4 + h] = np.where(sel, np.exp(lg[h] * (15.0 - pos)), 0.0) * KS
    c["kdec_s"] = kd
    c["cdec_s"] = [float(np.exp(lg[h] * 16.0)) for h in range(4)]
    half = 64
    freqs = (10000.0 ** (-np.arange(half, dtype=np.float32) / half)).astype(np.float32)
    posp = np.arange(S, dtype=np.float32)
    ang = (posp[:, None] * freqs[None, :]).astype(np.float32)
    cs = np.cos(ang).astype(np.float32).reshape(16, 128, 64).transpose(1, 0, 2)
    sn = np.sin(ang).astype(np.float32).reshape(16, 128, 64).transpose(1, 0, 2)
    c["cos"] = np.ascontiguousarray(cs)
    c["sin"] = np.ascontiguousarray(sn)
    poss = np.zeros(128, np.float32)
    poss[:48] = (PAST + pos).astype(np.float32)
    angs = (poss[:, None] * freqs[None, :]).astype(np.float32)
    c["cos_s"] = np.cos(angs).astype(np.float32)
    c["sin_s"] = np.sin(angs).astype(np.float32)
    return c


CONST_SHAPES = {
    "ident": [128, 128], "tri": [128, 128], "triS": [128, 128], "ones": [128, 128],
    "maskT": [128, 128], "tri_s": [128, 128], "mask_s": [128, 128],
    "DT": [128, 4, 128], "idec": [128, 4, 128], "kdec": [128, 8],
    "DT_s": [128, 4, 48], "idec_s0": [128, 4, 48], "idec_s1": [128, 4, 48], "kdec_s": [128, 8],
    "cos": [128, 16, 64], "sin": [128, 16, 64], "cos_s": [128, 64], "sin_s": [128, 64],
}
LATE_CONSTS = ("DT_s", "idec_s0", "idec_s1", "cos", "sin", "cos_s", "sin_s")

IN_SHAPES = {
    "x_p": [NSEQ, S, D], "p_p": [NSEQ, S, 256], "x_s": [NSS, L, D], "p_s": [NSS, L, 256],
    "st_in": [NSS, 4, 128, 256], "ck": [NSS, PAST, 512], "cv": [NSS, PAST, 512], "clf": [NSS, PAST, 8],
    "norm_ffn1_g": [D], "ffn1_w_in": [D, 2 * DFF], "ffn1_w_out": [DFF, D], "norm_mix_g": [D],
    "w_in_mix": [D, INC], "b_forget": [8], "q_norm_g": [64], "k_norm_g": [64],
    "w_br_ret": [D, D], "w_br_fox": [512, D], "w_out": [D, D], "norm_ffn2_g": [D],
    "ffn2_w_in": [D, 2 * DFF], "ffn2_w_out": [DFF, D], "norm_ple_g": [D], "w_ple": [256, D],
    "w_ple_gate": [D, D],
}
OUT_SHAPES = {
    "y_p": [NSEQ, S, D], "y_s": [NSS, L, D], "st_p": [NSEQ, 4, 128, 256], "k_p": [NSEQ, S, 512],
    "v_p": [NSEQ, S, 512], "lf_p": [NSEQ, S, 8], "st_s": [NSS, 4, 128, 256], "k_s": [NSS, L, 512],
    "v_s": [NSS, L, 512], "lf_s": [NSS, L, 8],
}


def build_program(nseq=NSEQ, do_sample=True, ngroups=NG):
    CT = _consts()
    nc = bass.Bass("TRN2", target_bir_lowering=False)
    I = {n: nc.dram_tensor(n, s, F32, kind="ExternalInput").ap() for n, s in IN_SHAPES.items()}
    C = {n: nc.dram_tensor("c_" + n, s, F32, kind="ExternalInput").ap() for n, s in CONST_SHAPES.items()}
    O = {n: nc.dram_tensor(n, s, F32, kind="ExternalOutput").ap() for n, s in OUT_SHAPES.items()}

    wv = {
        "f1a": I["ffn1_w_in"].rearrange("(kc p) n -> p kc n", p=128),
        "f1b": I["ffn1_w_out"].rearrange("(kc p) n -> p kc n", p=128),
        "mix": I["w_in_mix"].rearrange("(kc p) n -> p kc n", p=128),
        "brr": I["w_br_ret"].rearrange("(kc p) n -> p kc n", p=128),
        "brf": I["w_br_fox"].rearrange("(h p) n -> p h n", p=64),
        "wo": I["w_out"].rearrange("(kc p) n -> p kc n", p=128),
        "f2a": I["ffn2_w_in"].rearrange("(kc p) n -> p kc n", p=128),
        "f2b": I["ffn2_w_out"].rearrange("(kc p) n -> p kc n", p=128),
        "pg": I["w_ple_gate"].rearrange("(kc p) n -> p kc n", p=128),
        "pl": I["w_ple"].rearrange("(kc p) n -> p kc n", p=128),
    }
    chunks = []

    def ffn_chunks(tag, a, b, gname):
        for pc in range(11):
            chunks.append((f"{tag}a{pc}", [(wv[a], 128, 8, pc * 256, 256, gname, 0),
                                           (wv[a], 128, 8, DFF + pc * 256, 256, gname, 2048)]))
        for dc in range(8):
            chunks.append((f"{tag}b{dc}", [(wv[b], 128, 22, dc * 128, 128, None, 0)]))

    ffn_chunks("f1", "f1a", "f1b", "norm_ffn1_g")
    for cidx in range(9):
        chunks.append((f"mx{cidx}", [(wv["mix"], 128, 8, cidx * 512, 512, "norm_mix_g", 0)]))
    chunks.append(("mxff", [(wv["mix"], 128, 8, 4608, 8, "norm_mix_g", 0)]))
    for dc in range(8):
        chunks.append((f"mgA{dc}", [(wv["brr"], 128, 8, dc * 128, 128, None, 0),
                                    (wv["mix"], 128, 8, 4616 + dc * 128, 128, "norm_mix_g", 1024)]))
        chunks.append((f"mgB{dc}", [(wv["brf"], 64, 8, dc * 128, 128, None, 0),
                                    (wv["mix"], 128, 8, 4616 + 1024 + dc * 128, 128, "norm_mix_g", 1024)]))
    for cidx in range(2):
        chunks.append((f"wo{cidx}", [(wv["wo"], 128, 8, cidx * 512, 512, None, 0)]))
    ffn_chunks("f2", "f2a", "f2b", "norm_ffn2_g")
    for cidx in range(2):
        chunks.append((f"pg{cidx}", [(wv["pg"], 128, 8, cidx * 512, 512, "norm_ple_g", 0)]))
    chunks.append(("pl", [(wv["pl"], 128, 2, 0, 1024, None, 0)]))
    NCH = len(chunks)
    cidx_of = {name: i for i, (name, _) in enumerate(chunks)}
    wscr = nc.dram_tensor("wscr", [NCH, 128, SLOT], BF16).ap()

    with contextlib.ExitStack() as es:
        k = KB(nc, es)
        xT = k.sb("xT", [128, 8, G], F32)
        hT = k.sb("hT", [128, 8, G], BF16)
        R2 = k.sb("R2", [128, 4096], BF16)
        R1 = k.sb("R1", [128, 22 * G], BF16)
        rstd = k.sb("rstd", [128, G], F32)
        wring = [k.sb(f"wr{i}", [128, SLOT], BF16) for i in range(NSLOT)]
        io = [k.sb(f"io{i}", [128, 1024], F32) for i in range(2)]
        pst = [k.sb("pst0", [128, 256], F32)]
        pbf = k.sb("pbf", [128, 256], BF16)
        pT = k.sb("pT", [128, 2, G], BF16)
        cs = {n: k.sb("k_" + n, s, F32) for n, s in CONST_SHAPES.items() if n not in LATE_CONSTS}
        cosg = k.sb("cosg", [128, 4, 64], F32)
        sing = k.sb("sing", [128, 4, 64], F32)
        ident_b = k.sb("ident_b", [128, 128], BF16)
        ones_b = k.sb("ones_b", [128, 128], BF16)
        maskT_b = k.sb("maskT_b", [128, 128], BF16)
        masks_b = k.sb("masks_b", [128, 128], BF16)
        gq = k.sb("gq", [128, 64], F32)
        gk = k.sb("gk", [128, 64], F32)
        bfg = k.sb("bfg", [128, 8], F32)
        gcols = {n: k.sb("gc_" + n, [128, 8], F32) for n in
                 ("norm_ffn1_g", "norm_mix_g", "norm_ffn2_g", "norm_ple_g")}
        qa = [k.sb(f"qa{i}", [128, 8, 66], BF16) for i in range(2)]

        cqs = k.sb("cqs", [128, 4, 8], F32)
        ka = [k.sb(f"ka{i}", [128, 8, 66], BF16) for i in range(2)]
        qTt = [k.sb(f"qTt{i}", [128, 4, 128], BF16) for i in range(2)]
        qdTt = [k.sb(f"qdTt{i}", [128, 4, 128], BF16) for i in range(2)]
        kTt = [k.sb(f"kTt{i}", [128, 4, 128], BF16) for i in range(2)]
        attT = [k.sb(f"attT{i}", [128, 128], BF16) for i in range(4)]
        om = [k.sb("om0", [128, 1024], BF16)]
        omT = k.sb("omT", [128, 8, G], BF16)
        PT = [k.sb(f"PT{i}", [128, G], BF16) for i in range(3)]
        kTa = k.sb("kTa", [128, 8, S], BF16)
        va_flat = k.sb("va", [128, 16 * 8 * 66], BF16)
        va = va_flat[:, :].rearrange("p (a h c) -> p a h c", a=16, h=8)
        biasK = k.sb("biasK", [128, 16, 8], F32)
        state = [k.sb("state0", [128, 4, 256], F32)]
        state_b = [k.sb("stateb0", [128, 4, 256], BF16)]
        junk = sq512_ph = None
        sm = k.sb("sm", [128, 256], F32)
        rot = k.sb("rot", [128, 4, 256], F32)
        sq512 = k.sb("sq512", [128, 512], F32)
        kn32 = k.sb("kn32", [128, 512], F32)
        lfc = k.sb("lfc", [128, 8], F32)
        rden = rot[:, 0:2, :].rearrange("p a n -> p (a n)")
        bcs = rden
        qTa_own = k.sb("qTa", [128, 8, G], BF16)
        kTs = k.sb("kTs", [128, 8, 48], BF16)
        sgt = [k.sb(f"sgt{i}", [128, G], F32) for i in range(2)]
        junk = sq512[:, 0:256]
        junk4v = [(sgt[0][:, 0:256], "sgt0"), (sgt[0][:, 256:512], "sgt0"),
                  (sgt[1][:, 0:256], "sgt1"), (sgt[1][:, 256:512], "sgt1")]
        vaf = kTa[:, :, :].rearrange("p h n -> p (h n)")
        voff = [0]

        def carve(nel, dt):
            nb = nel * (2 if dt == F32 else 1)
            a = vaf[:, voff[0]: voff[0] + nb]
            voff[0] += nb
            assert voff[0] <= 8 * S
            return a.bitcast(F32) if dt == F32 else a

        clf_sb = carve(256, F32).rearrange("p (t h) -> p t h", h=8)
        suf = carve(256, F32).rearrange("p (t h) -> p t h", h=8)
        sufc = carve(264, F32).rearrange("p (t h) -> p t h", h=8)
        kcT = [carve(1024, BF16).rearrange("p (h n) -> p h n", h=8) for i in range(2)]
        vca = [carve(528, BF16).rearrange("p (h c) -> p h c", h=8) for i in range(2)]
        PTs = [carve(128, BF16).rearrange("p (h q) -> p h q", h=8) for i in range(2)]
        kd2 = carve(512, BF16)
        qdTt2 = carve(512, BF16).rearrange("p (h n) -> p h n", h=4)
        late = {n: carve(192, F32).rearrange("p (h n) -> p h n", h=4) for n in ("DT_s", "idec_s0", "idec_s1")}
        cstage = [(io[0][:, :], "io0"), (io[1][:, :], "io1")]
        for i_ in range(3):
            cstage.append((carve(1024, F32), f"cst{i_}"))
        for i_ in range(3):
            cstage.append((va_flat[:, 528 + i_ * 2048: 528 + (i_ + 1) * 2048].bitcast(F32), f"cst{3 + i_}"))
        cst_i = [0]
        state.append(carve(1024, F32).rearrange("p (h n) -> p h n", h=4))
        state_b.append(carve(1024, BF16).rearrange("p (h n) -> p h n", h=4))
        pf = [k.ps(f"pf{i}", [128, 512], F32) for i in range(6)]
        pb = [k.ps(f"pb{i}", [128, 1024], BF16) for i in range(2)]
        rr = {"f": 0, "b": 0, "w": 0, "io": 0}

        def nf():
            i = rr["f"] % 4
            rr["f"] += 1
            return pf[i], f"pf{i}"

        nf_all = nf

        def nb():
            i = rr["b"] % 2
            rr["b"] += 1
            return pb[i], f"pb{i}"

        def nio():
            i = rr["io"] % 2
            rr["io"] += 1
            return io[i], f"io{i}"

        sqT = R2[:, :].rearrange("p (a n) -> p a n", a=8)
        mT = sqT
        k_rot = R2[:, 0:2048].rearrange("p (a n) -> p a n", a=4)
        k_dec = R2[:, 2048:4096].rearrange("p (a n) -> p a n", a=4)
        actT = R1[:, :].rearrange("p (a n) -> p a n", a=22)
        v_tm = R1[:, 0:4096].rearrange("p (a n) -> p a n", a=4)
        sg_tm = R1[:, 4096:8192].rearrange("p (a n) -> p a n", a=4)
        q_rot = R1[:, 8192:10240].rearrange("p (a n) -> p a n", a=4)
        qTa = qTa_own
        ofT = R1[:, 4096:8192].rearrange("p (a n) -> p a n", a=8)

        for n in cs:
            k.dma("pool", cs[n][:], C[n], writes=["c_" + n], dkey="d_const")
        k.dma("pool", gq[:], I["q_norm_g"].partition_broadcast(128), writes=["gq"], dkey="d_const")
        k.dma("pool", gk[:], I["k_norm_g"].partition_broadcast(128), writes=["gk"], dkey="d_const")
        k.dma("pool", bfg[:], I["b_forget"].partition_broadcast(128), writes=["bfg"], dkey="d_const")
        for n in gcols:
            k.dma("pool", gcols[n][:], I[n].rearrange("(kc p) -> p kc", p=128), writes=["gc_" + n],
                  dkey="d_const", slow=True)
        for n in ["c_" + n for n in cs] + ["gc_" + n for n in gcols] + ["gq", "gk", "bfg"]:
            if n in k.res:
                k.res[n]["w"] = ("d_const", k.cnt["d_const"])
        k.ts("dve", gq[:], gq[:], 0.125, None, ALU.mult, None, ["gq"], ["gq"])
        k.cp("dve", ident_b[:], cs["ident"][:], ["c_ident"], ["ident_b"])
        k.cp("dve", ones_b[:], cs["ones"][:], ["c_ones"], ["ones_b"])
        k.cp("dve", maskT_b[:], cs["maskT"][:], ["c_maskT"], ["maskT_b"])
        k.cp("dve", masks_b[:], cs["mask_s"][:], ["c_mask_s"], ["masks_b"])
        for i in range(2):
            k.op("dve", lambda e, t=qa[i]: e.memset(t[:], 0.0), writes=[f"qa{i}"])
            k.op("dve", lambda e, t=ka[i]: e.memset(t[:], 1.0), writes=[f"ka{i}"])
        k.op("dve", lambda e: e.memset(va_flat[:, :], 1.0), writes=[f"va{T}" for T in range(16)])

        R1f = R1[:, :].bitcast(F32)
        kTaf = kTa[:, :, :].rearrange("p h n -> p (h n)").bitcast(F32)
        stg = [(xT[:, :, :].rearrange("p a n -> p (a n)"), "stg0"), (R1f, "stg1"),
               (kTaf[:, 0:4096], "stg2"), (kTaf[:, 4096:8192], "stg3")]
        seglist = []
        for ci, (cname, segs) in enumerate(chunks):
            for j, sg_ in enumerate(segs):
                seglist.append((ci, sg_, j == len(segs) - 1))
        ceng = ["dve", "act"]
        ce = [0]
        LA = 2

        def p_load(idx):
            ci, (src, npart, nkc, c0, w, gname, off), last = seglist[idx]
            st, stkey = stg[idx % 4]
            stv = st[0:npart, 0:nkc * w].rearrange("p (a n) -> p a n", a=nkc)
            k.dma("sp", stv, src[:, :, c0:c0 + w], writes=[stkey], dkey="d_" + stkey)

        def p_cast(idx):
            ci, (src, npart, nkc, c0, w, gname, off), last = seglist[idx]
            st, stkey = stg[idx % 4]
            slot, skey = wring[ci % NSLOT], f"wr{ci % NSLOT}"
            n_el = nkc * w
            stv = st[0:npart, 0:n_el].rearrange("p (a n) -> p a n", a=nkc)
            dst = slot[0:npart, off:off + n_el].rearrange("p (a n) -> p a n", a=nkc)
            if gname is None:
                eng = ceng[ce[0] % 2]
                ce[0] += 1
                k.cp(eng, dst, stv, [stkey], [skey])
            else:
                for kc in range(nkc):
                    eng = ceng[ce[0] % 2]
                    ce[0] += 1
                    gc = gcols[gname][:, kc:kc + 1]
                    if eng == "act":
                        k.act(dst[:, kc, :], stv[:, kc, :], AF.Copy, [stkey], [skey],
                              sreads=["gc_" + gname], scale=gc)
                    else:
                        k.ts(eng, dst[:, kc, :], stv[:, kc, :], gc, None, ALU.mult, None,
                             [stkey], [skey], sreads=["gc_" + gname])
            if last:
                k.dma("pool", wscr[ci], slot[:, :], reads=[skey], writes=[f"ws{ci}"], dkey="d_wst_" + skey)

        for idx in range(len(seglist) + LA):
            if idx < len(seglist):
                p_load(idx)
            if idx - LA >= 0:
                p_cast(idx - LA)
        k.barrier()

        def wload(cname):
            ci = cidx_of[cname]
            i = rr["w"] % NSLOT
            rr["w"] += 1
            k.dma("sp", wring[i][:, :], wscr[ci], reads=[f"ws{ci}"], writes=[f"wr{i}"],
                  dkey=f"d_wr{i}")
            return wring[i], f"wr{i}"

        def rmsnorm(n):
            ps, pk = nf()
            for kc in range(8):
                sk = "R2a" if kc < 4 else "R2b"
                k.act(sqT[:, kc, 0:n], xT[:, kc, 0:n], AF.Square, [f"xT{kc}"], [sk])
                k.mm(ps[:, 0:n], ones_b[:, :], sqT[:, kc, 0:n], kc == 0, kc == 7, ["ones_b", sk], [pk])
            k.act(rstd[:, 0:n], ps[:, 0:n], AF.Ln, [pk], ["rstd"], bias=EPS, scale=1.0 / D)
            k.act(rstd[:, 0:n], rstd[:, 0:n], AF.Exp, ["rstd"], ["rstd"], scale=-0.5)
            for kc in range(8):
                eng = "pool" if kc in (3, 6) else "dve"
                k.tt(eng, hT[:, kc, 0:n], xT[:, kc, 0:n], rstd[:, 0:n], ALU.mult, [f"xT{kc}", "rstd"], [f"hT{kc}"])

        def r1key(j):
            return "R1a" if j < 8 else ("R1b" if j < 16 else "R1c")

        def ffn(tag, n):
            rmsnorm(n)
            for pc in range(11):
                w, wk = wload(f"{tag}a{pc}")
                for sub in range(2):
                    j = pc * 2 + sub
                    gp, gkey = nf()
                    up, ukey = nf()
                    for kc in range(8):
                        k.mm(gp[:, 0:n], w[:, kc * 256 + sub * 128: kc * 256 + sub * 128 + 128],
                             hT[:, kc, 0:n], kc == 0, kc == 7, [wk, f"hT{kc}"], [gkey])
                    for kc in range(8):
                        k.mm(up[:, 0:n], w[:, 2048 + kc * 256 + sub * 128: 2048 + kc * 256 + sub * 128 + 128],
                             hT[:, kc, 0:n], kc == 0, kc == 7, [wk, f"hT{kc}"], [ukey])
                    sg = sgt[j % 2]
                    k.act(sg[:, 0:n], gp[:, 0:n], AF.Silu, [gkey], [f"sgt{j % 2}"])
                    k.tt("dve", actT[:, j, 0:n], sg[:, 0:n], up[:, 0:n], ALU.mult,
                         [f"sgt{j % 2}", ukey], [r1key(j)])
            for dc in range(8):
                w, wk = wload(f"{tag}b{dc}")
                yp, yk = nf()
                for j in range(22):
                    k.mm(yp[:, 0:n], w[:, j * 128:(j + 1) * 128], actT[:, j, 0:n], j == 0, j == 21,
                         [wk, r1key(j)], [yk])
                k.stt("dve", xT[:, dc, 0:n], yp[:, 0:n], 0.5, xT[:, dc, 0:n], ALU.mult, ALU.add,
                      [yk, f"xT{dc}"], [f"xT{dc}"])

        def load_group(xsrc_tiles, psrc_tiles, n):
            for (xparts, pparts, t0, np_) in zip_tiles(xsrc_tiles, psrc_tiles):
                xi, xk = nio()
                if len(xparts) > 1 or xparts[0][2] != 128:
                    k.op("pool", lambda e, t=xi: e.memset(t[:], 0.0), writes=[xk])
                for (ap, r0, nr) in xparts:
                    k.dma("pool", xi[r0:r0 + nr, :], ap, writes=[xk], dkey="d_" + xk)
                for half in range(2):
                    ps, pk = nf()
                    for q4 in range(4):
                        kc = half * 4 + q4
                        k.tr(ps[:, q4 * 128: q4 * 128 + np_], xi[0:np_, kc * 128:(kc + 1) * 128],
                             cs["ident"][0:np_, 0:np_], [xk, "c_ident"], [pk])
                    src = ps[:, :].rearrange("p (a n) -> p a n", a=4)[:, :, 0:np_]
                    k.cp("act" if half == 0 else "dve", xT[:, half * 4: half * 4 + 4, t0:t0 + np_], src,
                         [pk], [f"xT{half * 4 + q4}" for q4 in range(4)])
                pt, ptk = pst[0], "pst0"
                if len(pparts) > 1 or pparts[0][2] != 128:
                    k.op("pool", lambda e, t=pt: e.memset(t[:], 0.0), writes=[ptk])
                for (ap, r0, nr) in pparts:
                    k.dma("pool", pt[r0:r0 + nr, :], ap, writes=[ptk], dkey="d_" + ptk)
                k.cp("act", pbf[0:np_, :], pt[0:np_, :], [ptk], ["pbf"])
                bp, bk = nb()
                for kc in range(2):
                    k.tr(bp[:, kc * 128: kc * 128 + np_], pbf[0:np_, kc * 128:(kc + 1) * 128],
                         ident_b[0:np_, 0:np_], ["pbf", "ident_b"], [bk])
                src = bp[:, 0:256].rearrange("p (a n) -> p a n", a=2)[:, :, 0:np_]
                k.cp("dve", pT[:, :, t0:t0 + np_], src, [bk], ["pT"])

        def zip_tiles(xs, ps_):
            return [(x[0], p[0], x[1], x[2]) for x, p in zip(xs, ps_)]

        def store_group(ydst_tiles, n):
            for (parts, t0, np_) in ydst_tiles:
                yo, yk = nio()
                for half in range(2):
                    ps, pk = nf()
                    for q4 in range(4):
                        kc = half * 4 + q4
                        k.tr(ps[0:np_, q4 * 128:(q4 + 1) * 128], xT[:, kc, t0:t0 + np_],
                             cs["ident"][:, :], [f"xT{kc}", "c_ident"], [pk])
                    k.cp("act" if half == 0 else "dve", yo[0:np_, half * 512:(half + 1) * 512],
                         ps[0:np_, :], [pk], [yk])
                for (ap, r0, nr) in parts:
                    k.dma("pool", ap, yo[r0:r0 + nr, :], reads=[yk], dkey="d_" + yk)

        def ple(n):
            rmsnorm(n)
            wp, wpk = wload("pl")
            for cidx in range(2):
                w, wk = wload(f"pg{cidx}")
                for dl in range(4):
                    dc = cidx * 4 + dl
                    gp, gkey = nf()
                    pp, pkey = nf()
                    for kc in range(8):
                        k.mm(gp[:, 0:n], w[:, kc * 512 + dl * 128: kc * 512 + dl * 128 + 128],
                             hT[:, kc, 0:n], kc == 0, kc == 7, [wk, f"hT{kc}"], [gkey])
                    for kc in range(2):
                        k.mm(pp[:, 0:n], wp[:, kc * 1024 + dc * 128: kc * 1024 + dc * 128 + 128],
                             pT[:, kc, 0:n], kc == 0, kc == 1, [wpk, "pT"], [pkey])
                    sg = sgt[dc % 2]
                    k.act(sg[:, 0:n], gp[:, 0:n], AF.Sigmoid, [gkey], [f"sgt{dc % 2}"])
                    k.tt("dve", sg[:, 0:n], sg[:, 0:n], pp[:, 0:n], ALU.mult, [f"sgt{dc % 2}", pkey],
                         [f"sgt{dc % 2}"])
                    k.tt("dve", xT[:, dc, 0:n], xT[:, dc, 0:n], sg[:, 0:n], ALU.add,
                         [f"xT{dc}", f"sgt{dc % 2}"], [f"xT{dc}"])

        def rotary(ps, pk, dst, cos_ap, sin_ap, np_, dkey_):
            x = ps[0:np_, :].rearrange("p (h t d) -> p h t d", h=4, t=2)
            x1 = x[:, :, 0, :]
            x2 = x[:, :, 1, :]
            cb = cos_ap.unsqueeze(1).broadcast_to([np_, 4, 64])
            sb_ = sin_ap.unsqueeze(1).broadcast_to([np_, 4, 64])
            t = [rot[0:np_, i, :].rearrange("p (h d) -> p h d", h=4) for i in range(4)]
            d = dst.rearrange("p (h t d) -> p h t d", h=4, t=2)
            k.tt("dve", t[0], x1, cb, ALU.mult, [pk, "cosg"], ["rot0"])
            k.tt("dve", t[1], x2, sb_, ALU.mult, [pk, "sing"], ["rot1"])
            k.tt("dve", t[2], x1, sb_, ALU.mult, [pk, "sing"], ["rot2"])
            k.tt("dve", t[3], x2, cb, ALU.mult, [pk, "cosg"], ["rot3"])
            k.tt("dve", d[:, :, 0, :], t[0], t[1], ALU.subtract, ["rot0", "rot1"], [dkey_])
            k.tt("dve", d[:, :, 1, :], t[2], t[3], ALU.add, ["rot2", "rot3"], [dkey_])

        def qknorm(ps, pk, np_, gtab, gkey):
            k.act(sq512[0:np_, :], ps[0:np_, :], AF.Square, [pk], ["sq512"])
            k.op("dve", lambda e: e.reduce_sum(out=sm[0:np_, 0:8],
                                               in_=sq512[0:np_, :].rearrange("p (h d) -> p h d", h=8),
                                               axis=AX.X), reads=["sq512"], writes=["sm"])
            k.act(sm[0:np_, 0:8], sm[0:np_, 0:8], AF.Sqrt, ["sm"], ["sm"], bias=EPS, scale=1.0 / 64)
            k.op("dve", lambda e: e.reciprocal(out=sm[0:np_, 0:8], in_=sm[0:np_, 0:8]),
                 reads=["sm"], writes=["sm"])
            k3 = kn32[0:np_, :].rearrange("p (h d) -> p h d", h=8)
            k.tt("dve", k3, ps[0:np_, :].rearrange("p (h d) -> p h d", h=8),
                 sm[0:np_, 0:8].unsqueeze(2).broadcast_to([np_, 8, 64]), ALU.mult, [pk, "sm"], ["kn32"])
            k.tt("dve", k3, k3, gtab[0:np_, :].unsqueeze(1).broadcast_to([np_, 8, 64]), ALU.mult,
                 ["kn32", gkey], ["kn32"])
            return k3

        def mixer(n, tiles, ret_cfg, fox_fn, interleave):
            nt = len(tiles)
            rmsnorm(n)
            def proj_gen(cname, width, evac, fb=nf):
                w, wk = wload(cname)
                for ti, (t0, np_, cA, sA) in enumerate(tiles):
                    ps, pk = fb()
                    for kc in range(8):
                        k.mm(ps[0:np_, 0:width], hT[:, kc, t0:t0 + np_], w[:, kc * width:(kc + 1) * width],
                             kc == 0, kc == 7, [f"hT{kc}", wk], [pk])
                    evac(ti, t0, np_, cA, sA, ps, pk)
                    yield

            def proj(cname, width, evac):
                for _ in proj_gen(cname, width, evac):
                    pass

            def mk_rot(lst):
                st_ = [0]

                def fb():
                    i = lst[st_[0] % len(lst)]
                    st_[0] += 1
                    return i
                return fb

            proj("mx0", 512, lambda ti, t0, np_, cA, sA, ps, pk:
                 rotary(ps, pk, q_rot[0:np_, ti, :], cA, sA, np_, "R1c"))
            proj("mx1", 512, lambda ti, t0, np_, cA, sA, ps, pk:
                 rotary(ps, pk, k_rot[0:np_, ti, :], cA, sA, np_, "R2a"))
            for half in range(2):
                proj(f"mx{2 + half}", 512, lambda ti, t0, np_, cA, sA, ps, pk, half=half:
                     k.cp("act", v_tm[0:np_, ti, half * 512:(half + 1) * 512], ps[0:np_, :], [pk], ["R1a"]))
            for half in range(2):
                proj(f"mx{4 + half}", 512, lambda ti, t0, np_, cA, sA, ps, pk, half=half:
                     k.act(sg_tm[0:np_, ti, half * 512:(half + 1) * 512], ps[0:np_, :], AF.Silu, [pk], ["R1b"]))
            pend_om = []

            def retention_gen(nf, nb):
              for ti, (t0, np_, cA, sA) in enumerate(tiles):
                cfgs = ret_cfg[ti]
                b = ti % 2
                bp, bk = nb()
                for h in range(4):
                    k.tr(bp[:, h * 128: h * 128 + np_], q_rot[0:np_, ti, h * 128:(h + 1) * 128],
                         ident_b[0:np_, 0:np_], ["R1c", "ident_b"], [bk])
                qps = bp[:, 0:512].rearrange("p (h n) -> p h n", h=4)[:, :, 0:np_]
                k.cp("act", qTt[b][:, :, 0:np_], qps, [bk], [f"qTt{b}"])
                qd_list = []
                for si, cfg in enumerate(cfgs):
                    qd = qdTt[b] if si == 0 else qdTt2
                    qdk = f"qdTt{b}" if si == 0 else "qdTt2"
                    k.tt("dve", qd[:, :, 0:np_], qTt[b][:, :, 0:np_], cs[cfg["idec"]][:, :, 0:np_], ALU.mult,
                         [f"qTt{b}", "c_" + cfg["idec"]], [qdk])
                    qd_list.append((qd, qdk))
                for si, cfg in enumerate(cfgs):
                    for h in range(4):
                        col = cfg["kdec_col"] + h
                        dstk = k_dec[0:np_, ti, h * 128:(h + 1) * 128] if si == 0 else \
                            kd2[0:np_, h * 128:(h + 1) * 128]
                        k.ts("pool", dstk, k_rot[0:np_, ti, h * 128:(h + 1) * 128],
                             cs[cfg["kdec"]][0:np_, col:col + 1], None, ALU.mult, None,
                             ["R2a"], ["R2b" if si == 0 else "kd2"], sreads=["c_" + cfg["kdec"]])
                yield
                bp2, bk2 = nb()
                for h in range(4):
                    k.tr(bp2[:, h * 128: h * 128 + np_], k_rot[0:np_, ti, h * 128:(h + 1) * 128],
                         ident_b[0:np_, 0:np_], ["R2a", "ident_b"], [bk2])
                k.cp("act", kTt[b][:, :, 0:np_],
                     bp2[:, 0:512].rearrange("p (h n) -> p h n", h=4)[:, :, 0:np_], [bk2], [f"kTt{b}"])
                omt, omk = om[0], "om0"
                nh = 4
                ap_, apk = nf()
                for h in range(nh):
                    k.mm(ap_[0:np_, h * 128: h * 128 + np_], kTt[b][:, h, 0:np_], qTt[b][:, h, 0:np_], True, True,
                         [f"kTt{b}", f"qTt{b}"], [apk])
                yield
                for h in range(nh):
                    k.tt("dve", attT[h][0:np_, 0:np_], ap_[0:np_, h * 128: h * 128 + np_],
                         cs[cfgs[0]["DT"]][0:np_, h, 0:np_], ALU.mult, [apk, "c_" + cfgs[0]["DT"]], [f"attT{h}"])
                yield
                obank = {}
                for h in range(nh):
                    if h % 2 == 0:
                        ob, obk = pf[4 + h // 2], f"pf{4 + h // 2}"
                    obank[h] = (ob[0:np_, (h % 2) * 256:(h % 2) * 256 + 256], obk)
                    op_, opk = obank[h]
                    for si, cfg in enumerate(cfgs):
                        qd, qdk = qd_list[si]
                        k.mm(op_, qd[:, h, 0:np_], cfg["state_b"][:, h, :], si == 0, False,
                             [qdk, cfg["sbkey"]], [opk])
                    k.mm(op_, attT[h][0:np_, 0:np_], v_tm[0:np_, ti, h * 256:(h + 1) * 256],
                         False, True, [f"attT{h}", "R1a"], [opk])
                sbank = {}
                cnt_s = 0
                for h in range(nh):
                    for si, cfg in enumerate(cfgs):
                        if cnt_s % 2 == 0:
                            sb2, sbk2 = nf()
                        sp_ = sb2[:, (cnt_s % 2) * 256:(cnt_s % 2) * 256 + 256]
                        cnt_s += 1
                        kd_ap = k_dec[0:np_, ti, h * 128:(h + 1) * 128] if si == 0 else \
                            kd2[0:np_, h * 128:(h + 1) * 128]
                        k.mm(sp_, kd_ap, v_tm[0:np_, ti, h * 256:(h + 1) * 256], True, True,
                             ["R2b" if si == 0 else "kd2", "R1a"], [sbk2])
                        sbank[(h, si)] = (sp_, sbk2)
                if pend_om:
                    pend_om.pop(0)()
                yield
                for h in range(nh):
                    for si, cfg in enumerate(cfgs):
                        sp_, spk = sbank[(h, si)]
                        k.stt("dve", cfg["state"][:, h, :], cfg["state"][:, h, :], cfg["cdec"][h],
                              sp_, ALU.mult, ALU.add, [cfg["skey"], spk], [cfg["skey"]])
                        k.cp("dve", cfg["state_b"][:, h, :], cfg["state"][:, h, :], [cfg["skey"]],
                             [cfg["sbkey"]])
                yield
                for h in range(nh):
                    op_, opk = obank[h]
                    k.act(junk4v[h][0][0:np_, :], op_, AF.Square, [opk], [junk4v[h][1]])
                for h in range(nh):
                    k.op("dve", lambda e, h=h, np_=np_: e.reduce_sum(out=sm[0:np_, 16 + h:17 + h],
                                                                    in_=junk4v[h][0][0:np_, :], axis=AX.X),
                         reads=[junk4v[h][1]], writes=["smo"])
                if nh:
                    k.act(sm[0:np_, 16:20], sm[0:np_, 16:20], AF.Sqrt, ["smo"], ["smo"], bias=EPS, scale=1.0 / 256)
                    k.op("dve", lambda e, np_=np_: e.reciprocal(out=sm[0:np_, 16:20], in_=sm[0:np_, 16:20]),
                         reads=["smo"], writes=["smo"])
                for h in range(nh):
                    op_, opk = obank[h]
                    k.stt("dve", omt[0:np_, h * 256:(h + 1) * 256], op_,
                          sm[0:np_, 16 + h:17 + h], sg_tm[0:np_, ti, h * 256:(h + 1) * 256],
                          ALU.mult, ALU.mult, [opk, "R1b"], [omk], sreads=["smo"])

                def om_tr(t0=t0, np_=np_, omt=omt, omk=omk):
                    bp3, bk3 = nb()
                    for kc in range(8):
                        k.tr(bp3[:, kc * 128: kc * 128 + np_], omt[0:np_, kc * 128:(kc + 1) * 128],
                             ident_b[0:np_, 0:np_], [omk, "ident_b"], [bk3])
                    k.cp("act", omT[:, :, t0:t0 + np_],
                         bp3[:, :].rearrange("p (a n) -> p a n", a=8)[:, :, 0:np_], [bk3], ["omT"])
                pend_om.append(om_tr)
                yield
              while pend_om:
                pend_om.pop(0)()
            pbf = [pb[0][:, :].bitcast(F32), pb[1][:, :].bitcast(F32)]

            def merge1_gen():
                for dc in range(8):
                    w, wk = wload(f"mgA{dc}")
                    for kc in range(8):
                        k.mm(pbf[0][:, 0:n], w[:, kc * 128:(kc + 1) * 128], omT[:, kc, 0:n], kc == 0, kc == 7,
                             [wk, "omT"], ["pb0"])
                        yield
                    for kc in range(8):
                        k.mm(pbf[1][:, 0:n], w[:, 1024 + kc * 128: 1024 + (kc + 1) * 128], hT[:, kc, 0:n],
                             kc == 0, kc == 7, [wk, f"hT{kc}"], ["pb1"])
                        yield
                    sg = sgt[dc % 2]
                    sk = f"sgt{dc % 2}"
                    k.act(sg[:, 0:n], pbf[1][:, 0:n], AF.Exp, ["pb1"], [sk], scale=-1.0)
                    k.ts("dve", sg[:, 0:n], sg[:, 0:n], 1.0, None, ALU.add, None, [sk], [sk])
                    k.op("dve", lambda e, sg=sg: e.reciprocal(out=sg[:, 0:n], in_=sg[:, 0:n]),
                         reads=[sk], writes=[sk])
                    k.tt("dve", mT[:, dc, 0:n], sg[:, 0:n], pbf[0][:, 0:n], ALU.mult, [sk, "pb0"],
                         ["R2a" if dc < 4 else "R2b"])
                    yield

            filler = merge1_gen()
            if interleave:
                fb_rf = mk_rot([(pf[2], "pf2"), (pf[3], "pf3")])
                fb_rb = mk_rot([(pb[0], "pb0")])
                fb_ff = mk_rot([(pf[0], "pf0"), (pf[1], "pf1")])
                fb_fb = mk_rot([(pb[1], "pb1")])
            else:
                fb_rf, fb_rb, fb_ff, fb_fb = nf, nb, nf, nb
            gen_r = retention_gen(fb_rf, fb_rb)
            gen_f = fox_fn(proj_gen, filler, fb_ff, fb_fb)
            if interleave:
                r_alive, f_alive = True, True
                while r_alive or f_alive:
                    if r_alive:
                        r_alive = next(gen_r, "END") != "END"
                    if f_alive:
                        v_ = next(gen_f, "END")
                        if v_ == "ATT" or v_ == "END":
                            f_alive = False
            else:
                for _ in gen_r:
                    pass
            for _ in gen_f:
                pass
            for _ in filler:
                pass
            for dc in range(8):
                w, wk = wload(f"mgB{dc}")
                bfp, bfk = nf()
                for h in range(8):
                    k.mm(bfp[:, 0:n], w[0:64, h * 128:(h + 1) * 128], ofT[0:64, h, 0:n],
                         h == 0, h == 7, [wk, "R1b"], [bfk])
                gfp, gfk = nf()
                for kc in range(8):
                    k.mm(gfp[:, 0:n], w[:, 1024 + kc * 128: 1024 + (kc + 1) * 128], hT[:, kc, 0:n],
                         kc == 0, kc == 7, [wk, f"hT{kc}"], [gfk])
                sg2 = sgt[dc % 2]
                sk2 = f"sgt{dc % 2}"
                mkey = "R2a" if dc < 4 else "R2b"
                k.act(sg2[:, 0:n], gfp[:, 0:n], AF.Sigmoid, [gfk], [sk2])
                k.tt("dve", sg2[:, 0:n], sg2[:, 0:n], bfp[:, 0:n], ALU.mult, [sk2, bfk], [sk2])
                k.tt("dve", mT[:, dc, 0:n], mT[:, dc, 0:n], sg2[:, 0:n], ALU.add, [mkey, sk2], [mkey])
            for cidx in range(2):
                w, wk = wload(f"wo{cidx}")
                for dl in range(4):
                    dc = cidx * 4 + dl
                    yp, yk = nf()
                    for kc in range(8):
                        k.mm(yp[:, 0:n], w[:, kc * 512 + dl * 128: kc * 512 + dl * 128 + 128], mT[:, kc, 0:n],
                             kc == 0, kc == 7, [wk, "R2a" if kc < 4 else "R2b"], [yk])
                    k.tt("dve", xT[:, dc, 0:n], xT[:, dc, 0:n], yp[:, 0:n], ALU.add, [f"xT{dc}", yk], [f"xT{dc}"])

        def logf_tile(ps, pk, np_, dst_sm):
            u = sm[0:np_, 32:40]
            nu = sm[0:np_, 40:48]
            k.tt("dve", u, ps[0:np_, 0:8], bfg[0:np_, :], ALU.add, [pk, "bfg"], ["smu"])
            k.ts("dve", nu, u, -1.0, None, ALU.mult, None, ["smu"], ["smnu"])
            k.tt("dve", nu, nu, u, ALU.min, ["smnu", "smu"], ["smnu"])
            k.act(nu, nu, AF.Exp, ["smnu"], ["smnu"])
            k.act(nu, nu, AF.Ln, ["smnu"], ["smnu"], bias=1.0)
            k.ts("dve", u, u, 0.0, None, ALU.min, None, ["smu"], ["smu"])
            k.tt("dve", sm[0:np_, dst_sm:dst_sm + 8], u, nu, ALU.subtract, ["smu", "smnu"], ["smlf"])

        def fox_prompt_factory(seq, g):
            def fox(proj_gen, filler, nf, nb):
                n = G
                tiles_g = [g * 4 + ti for ti in range(4)]

                def ev_f(ti, t0, np_, cA, sA, ps, pk):
                    T = tiles_g[ti]
                    logf_tile(ps, pk, np_, 48)
                    lf = sm[0:np_, 48:56]
                    k.cp("dve", sm[0:np_, 64 + ti * 8: 72 + ti * 8], lf, ["smlf"], [f"lfo{ti}"])
                    k.dma("pool", O["lf_p"][seq, T * 128:(T + 1) * 128, :], sm[0:np_, 64 + ti * 8: 72 + ti * 8],
                          reads=[f"lfo{ti}"], dkey=f"d_lfo{ti}")
                    cp_, cpk = nf()
                    k.mm(cp_[0:np_, 0:8], cs["tri"][0:np_, 0:np_], lf, True, True, ["c_tri", "smlf"], [cpk])
                    tp_, tpk = nf()
                    k.mm(tp_[:, 0:8], cs["ones"][0:np_, :], lf, True, True, ["c_ones", "smlf"], [tpk])
                    cq = cqs[0:np_, ti, :]
                    k.tt("dve", cq, cp_[0:np_, 0:8], lfc[0:np_, :], ALU.add, [cpk, "lfc"], [f"cq{ti}"])
                    k.tt("dve", lfc[:, :], lfc[:, :], tp_[:, 0:8], ALU.add, ["lfc", tpk], ["lfc"])
                    k.ts("dve", biasK[0:np_, T, :], cq, -1.0, None, ALU.mult, None, [f"cq{ti}"], [f"bK{T}"])

                def ev_q(ti, t0, np_, cA, sA, ps, pk):
                    k3 = qknorm(ps, pk, np_, gq, "gq")
                    b = ti % 2
                    k.cp("act", qa[b][0:np_, :, 0:64], k3, ["kn32"], [f"qa{b}"])
                    k.cp("dve", qa[b][0:np_, :, 64:65], cqs[0:np_, ti, :].unsqueeze(2), [f"cq{ti}"], [f"qa{b}"])
                    bp, bk = nb()
                    for h in range(8):
                        k.tr(bp[0:65, h * 128:(h + 1) * 128], qa[b][0:np_, h, 0:65], ident_b[:, :],
                             [f"qa{b}", "ident_b"], [bk])
                    k.cp("act", qTa[0:65, :, t0:t0 + np_],
                         bp[0:65, :].rearrange("p (h n) -> p h n", h=8), [bk], ["qTa"])

                def ev_k(ti, t0, np_, cA, sA, ps, pk):
                    k3 = qknorm(ps, pk, np_, gk, "gk")
                    b = ti % 2
                    k.cp("act", ka[b][0:np_, :, 0:64], k3, ["kn32"], [f"ka{b}"])
                    T = tiles_g[ti]
                    k.dma("pool", O["k_p"][seq, T * 128:(T + 1) * 128, :], kn32[0:np_, :], reads=["kn32"],
                          dkey="d_kn32")
                    bp, bk = nb()
                    for h in range(8):
                        k.tr(bp[0:65, h * 128:(h + 1) * 128], ka[b][0:np_, h, 0:65], ident_b[:, :],
                             [f"ka{b}", "ident_b"], [bk])
                    k.cp("dve", kTa[0:65, :, T * 128:(T + 1) * 128],
                         bp[0:65, :].rearrange("p (h n) -> p h n", h=8), [bk], [f"kTa{T}"])

                def ev_v(ti, t0, np_, cA, sA, ps, pk):
                    T = tiles_g[ti]
                    vo, vk = nio()
                    k.cp("act", vo[0:np_, 0:512], ps[0:np_, :], [pk], [vk])
                    k.dma("pool", O["v_p"][seq, T * 128:(T + 1) * 128, :], vo[0:np_, 0:512], reads=[vk],
                          dkey="d_" + vk)
                    k.cp("dve", va[0:np_, T, :, 0:64], vo[0:np_, 0:512].rearrange("p (h d) -> p h d", h=8),
                         [vk], [f"va{T}"])

                yield from proj_gen("mxff", 8, ev_f, nf)
                yield from proj_gen("mx6", 512, ev_q, nf)
                yield from proj_gen("mx7", 512, ev_k, nf)
                yield from proj_gen("mx8", 512, ev_v, nf)
                yield "ATT"
                nf = nf_all
                nkt = g * 4 + 4
                steps = [(h, kt) for h in range(8) for kt in range(nkt)]
                Sinfo = {}

                def emit_S(i):
                    h, kt = steps[i]
                    sp_, spk = nf()
                    lhs = kTa[0:65, h, kt * 128:(kt + 1) * 128]
                    if kt < g * 4:
                        q0 = 0
                        k.mm(sp_[:, 0:G], lhs, qTa[0:65, h, 0:G], True, True, [f"kTa{kt}", "qTa"], [spk])
                    else:
                        q0 = (kt - g * 4) * 128
                        k.mm(sp_[:, q0:q0 + 128], lhs, qTa[0:65, h, q0:q0 + 128], True, False,
                             [f"kTa{kt}", "qTa"], [spk])
                        k.mm(sp_[:, q0:q0 + 128], ident_b[:, :], maskT_b[:, :], False, True,
                             ["ident_b", "maskT_b"], [spk])
                        if q0 + 128 < G:
                            k.mm(sp_[:, q0 + 128:G], lhs, qTa[0:65, h, q0 + 128:G], True, True,
                                 [f"kTa{kt}", "qTa"], [spk])
                    Sinfo[i] = (sp_, spk, q0)

                def finalize(h):
                    acc, acck = pf[4 + h % 2], f"pf{4 + h % 2}"
                    k.op("dve", lambda e, acc=acc: e.reciprocal(out=rden[64:65, :], in_=acc[64:65, :]),
                         reads=[acck], writes=["rden", "rot0", "rot1"])
                    bc, bck = nf()
                    k.mm(bc[0:64, :], cs["ones"][64:65, 0:64], rden[64:65, :], True, True, ["c_ones", "rden"],
                         [bck])
                    k.cp("act", bcs[0:64, :], bc[0:64, :], [bck], ["bcs", "rot0", "rot1"])
                    k.tt("dve", ofT[0:64, h, :], acc[0:64, :], bcs[0:64, :], ALU.mult, [acck, "bcs"], ["R1b"])

                pending = []
                nfill = -(-140 // max(len(steps), 1))
                for i, (h, kt) in enumerate(steps):
                    for _ in range(nfill):
                        next(filler, None)
                    if i == 0:
                        emit_S(0)
                    if i + 1 < len(steps):
                        emit_S(i + 1)
                    sp_, spk, q0 = Sinfo.pop(i)
                    acc, acck = pf[4 + h % 2], f"pf{4 + h % 2}"
                    pt, ptk = PT[i % 3], f"PT{i % 3}"
                    k.act(pt[:, q0:G], sp_[:, q0:G], AF.Exp, [spk], [ptk], sreads=[f"bK{kt}"],
                          bias=biasK[:, kt, h:h + 1])
                    k.mm(acc[0:65, q0:G], va[:, kt, h, 0:65], pt[:, q0:G], kt == 0, kt == nkt - 1,
                         [f"va{kt}", ptk], [acck])
                    if kt == nkt - 1:
                        pending.append((i + 2, h))
                    while pending and pending[0][0] <= i:
                        finalize(pending.pop(0)[1])
                for _, h in pending:
                    finalize(h)
            return fox

        k.op("dve", lambda e: e.memset(lfc[:, :], 0.0), writes=["lfc"])
        for seq in range(nseq):
            k.op("pool", lambda e: e.memset(state[0][:], 0.0), writes=["state0"])
            k.op("pool", lambda e: e.memset(state_b[0][:], 0.0), writes=["stateb0"])
            k.op("dve", lambda e: e.memset(lfc[:, :], 0.0), writes=["lfc"])
            for g in range(ngroups):
                n = G
                xs, ps_, ys = [], [], []
                tiles = []
                for ti in range(4):
                    r0 = g * G + ti * 128
                    xs.append(([(I["x_p"][seq, r0:r0 + 128, :], 0, 128)], ti * 128, 128))
                    ps_.append(([(I["p_p"][seq, r0:r0 + 128, :], 0, 128)], ti * 128, 128))
                    ys.append(([(O["y_p"][seq, r0:r0 + 128, :], 0, 128)], ti * 128, 128))
                    tiles.append((ti * 128, 128, cosg[:, ti, :], sing[:, ti, :]))
                k.dma("pool", cosg[:, :, :], C["cos"][:, g * 4:(g + 1) * 4, :], writes=["cosg"], dkey="d_cosg")
                k.dma("pool", sing[:, :, :], C["sin"][:, g * 4:(g + 1) * 4, :], writes=["sing"], dkey="d_sing")
                load_group(xs, ps_, n)
                ffn("f1", n)
                cfg = {"idec": "idec", "kdec": "kdec", "kdec_col": 0, "DT": "DT", "state": state[0],
                       "state_b": state_b[0], "skey": "state0", "sbkey": "stateb0", "cdec": CT["cdec"]}
                mixer(n, tiles, [[cfg]] * 4, fox_prompt_factory(seq, g), True)
                ffn("f2", n)
                ple(n)
                store_group(ys, n)
            for h in range(4):
                k.dma("pool", O["st_p"][seq, h], state[0][:, h, :], reads=["state0"], dkey="d_stout")

        if do_sample:
            k.barrier()
            n = 48
            for nm in ("DT_s", "idec_s0", "idec_s1"):
                cs[nm] = late[nm]
                k.dma("pool", late[nm], C[nm], writes=["c_" + nm], dkey="d_late_" + nm)
            k.dma("pool", cosg[:, 0, :], C["cos_s"], writes=["cosg"], dkey="d_cosg")
            k.dma("pool", sing[:, 0, :], C["sin_s"], writes=["sing"], dkey="d_sing")
            for i in range(2):
                k.op("dve", lambda e, t=vca[i]: e.memset(t, 1.0), writes=[f"vca{i}"])
            xs = [([(I["x_s"][0], 0, 16), (I["x_s"][1], 32, 16)], 0, 48)]
            ps_ = [([(I["p_s"][0], 0, 16), (I["p_s"][1], 32, 16)], 0, 48)]
            ys = [([(O["y_s"][0], 0, 16), (O["y_s"][1], 32, 16)], 0, 48)]
            tiles = [(0, 48, cosg[0:48, 0, :], sing[0:48, 0, :])]
            for sq in range(2):
                for h in range(4):
                    k.dma("pool", state[sq][:, h, :], I["st_in"][sq, h], writes=[f"state{sq}"],
                          dkey=f"d_stin{sq}")
                k.cp("dve", state_b[sq][:, :, :], state[sq][:, :, :], [f"state{sq}"], [f"stateb{sq}"])
            load_group(xs, ps_, n)
            ffn("f1", n)
            cfgs = [{"idec": f"idec_s{sq}", "kdec": "kdec_s", "kdec_col": sq * 4, "DT": "DT_s",
                     "state": state[sq], "state_b": state_b[sq], "skey": f"state{sq}",
                     "sbkey": f"stateb{sq}", "cdec": CT["cdec_s"]} for sq in range(2)]

            def fox_sample(proj_gen, filler, nf, nb):
                np_ = 48

                def ev_f(ti, t0, np_, cA, sA, ps, pk):
                    logf_tile(ps, pk, np_, 48)
                    lf = sm[0:np_, 48:56]
                    k.cp("dve", sm[0:np_, 64:72], lf, ["smlf"], ["lfo0"])
                    for sq in range(2):
                        k.dma("pool", O["lf_s"][sq], sm[sq * 32: sq * 32 + 16, 64:72], reads=["lfo0"],
                              dkey="d_lfo0")
                    cp_, cpk = nf()
                    k.mm(cp_[0:np_, 0:8], cs["tri_s"][0:np_, 0:np_], lf, True, True, ["c_tri_s", "smlf"], [cpk])
                    cq = cqs[0:np_, 0, :]
                    k.cp("dve", cq, cp_[0:np_, 0:8], [cpk], ["cq0"])
                    k.ts("dve", biasK[0:np_, 0, :], cq, -1.0, None, ALU.mult, None, ["cq0"], ["bK0"])

                def ev_q(ti, t0, np_, cA, sA, ps, pk):
                    k3 = qknorm(ps, pk, np_, gq, "gq")
                    k.op("dve", lambda e: e.memset(qa[0][:], 0.0), writes=["qa0"])
                    k.cp("act", qa[0][0:np_, :, 0:64], k3, ["kn32"], ["qa0"])
                    k.cp("dve", qa[0][0:np_, :, 64:65], cqs[0:np_, 0, :].unsqueeze(2), ["cq0"], ["qa0"])
                    bp, bk = nb()
                    for h in range(8):
                        k.tr(bp[0:65, h * 128: h * 128 + np_], qa[0][0:np_, h, 0:65], ident_b[0:np_, 0:np_],
                             ["qa0", "ident_b"], [bk])
                    k.cp("act", qTa[0:65, :, 0:np_],
                         bp[0:65, :].rearrange("p (h n) -> p h n", h=8)[:, :, 0:np_], [bk], ["qTa"])

                def ev_k(ti, t0, np_, cA, sA, ps, pk):
                    k3 = qknorm(ps, pk, np_, gk, "gk")
                    k.op("dve", lambda e: e.memset(ka[0][:], 1.0), writes=["ka0"])
                    k.cp("act", ka[0][0:np_, :, 0:64], k3, ["kn32"], ["ka0"])
                    for sq in range(2):
                        k.dma("pool", O["k_s"][sq], kn32[sq * 32: sq * 32 + 16, :], reads=["kn32"],
                              dkey="d_kn32")
                    bp, bk = nb()
                    for h in range(8):
                        k.tr(bp[0:65, h * 128: h * 128 + np_], ka[0][0:np_, h, 0:65], ident_b[0:np_, 0:np_],
                             ["ka0", "ident_b"], [bk])
                    k.cp("dve", kTs[0:65, :, 0:np_],
                         bp[0:65, :].rearrange("p (h n) -> p h n", h=8)[:, :, 0:np_], [bk], ["kTs"])

                def ev_v(ti, t0, np_, cA, sA, ps, pk):
                    vo, vk = nio()
                    k.cp("act", vo[0:np_, 0:512], ps[0:np_, :], [pk], [vk])
                    for sq in range(2):
                        k.dma("pool", O["v_s"][sq], vo[sq * 32: sq * 32 + 16, 0:512], reads=[vk], dkey="d_" + vk)
                    k.cp("dve", va[0:np_, 0, :, 0:64], vo[0:np_, 0:512].rearrange("p (h d) -> p h d", h=8),
                         [vk], ["va0"])

                yield from proj_gen("mxff", 8, ev_f, nf)
                yield from proj_gen("mx6", 512, ev_q, nf)
                yield from proj_gen("mx7", 512, ev_k, nf)
                yield from proj_gen("mx8", 512, ev_v, nf)
                yield "ATT"
                k.op("dve", lambda e: e.memset(ofT[0:64, :, 0:48], 0.0), writes=["R1b"])
                for sq in range(2):
                    qs = sq * 32
                    k.dma("pool", clf_sb[:, :, :], I["clf"][sq].rearrange("(t p) h -> p t h", p=128),
                          writes=["clf_sb"], dkey="d_clf")
                    wp_, wpk = nf()
                    k.mm(wp_[:, 0:256], cs["triS"][:, :], clf_sb[:, :, :].rearrange("p t h -> p (t h)"),
                         True, True, ["c_triS", "clf_sb"], [wpk])
                    tp_, tpk = nf()
                    k.mm(tp_[:, 0:256], cs["ones"][:, :], clf_sb[:, :, :].rearrange("p t h -> p (t h)"),
                         True, True, ["c_ones", "clf_sb"], [tpk])
                    k.op("dve", lambda e: e.memset(sufc[:, 32, :], 0.0), writes=["sufc"])
                    tot = tp_[:, 0:256].rearrange("p (t h) -> p t h", h=8)
                    for t in range(31, 0, -1):
                        k.tt("dve", sufc[:, t, :], sufc[:, t + 1, :], tot[:, t, :], ALU.add, ["sufc", tpk], ["sufc"])
                    k.tt("dve", suf[:, 0:31, :], wp_[:, 0:248].rearrange("p (t h) -> p t h", h=8),
                         sufc[:, 1:32, :], ALU.add, [wpk, "sufc"], ["suf"])
                    k.cp("dve", suf[:, 31, :], wp_[:, 248:256], [wpk], ["suf"])
                    accs, accsk = sgt[0], "sgt0"
                    slots = {}
                    banks = {}

                    def st0(t):
                        b = t % 2
                        if t % 2 == 0:
                            ks_ = cstage[cst_i[0] % 8]
                            vs_ = cstage[(cst_i[0] + 1) % 8]
                            cst_i[0] += 2
                            k.dma("sp", ks_[0].rearrange("p (t c) -> p t c", t=2),
                                  I["ck"][sq, t * 128:(t + 2) * 128, :].rearrange("(t p) c -> p t c", p=128),
                                  writes=[ks_[1]], dkey="d_sp_" + ks_[1])
                            k.dma("sp", vs_[0].rearrange("p (t c) -> p t c", t=2),
                                  I["cv"][sq, t * 128:(t + 2) * 128, :].rearrange("(t p) c -> p t c", p=128),
                                  writes=[vs_[1]], dkey="d_sp_" + vs_[1])
                            slots[t // 2] = (ks_, vs_)
                        (kslot, kik), (vslot, vik) = slots[t // 2]
                        ki = kslot[:, (t % 2) * 512:(t % 2 + 1) * 512]
                        vi = vslot[:, (t % 2) * 512:(t % 2 + 1) * 512]
                        k.cp("act", ka[b][:, :, 0:64], ki.rearrange("p (h d) -> p h d", h=8), [kik], [f"ka{b}"])
                        k.cp("dve", vca[b][:, :, 0:64], vi.rearrange("p (h d) -> p h d", h=8), [vik],
                             [f"vca{b}"])
                        bp, bk = nb()
                        for h in range(8):
                            k.tr(bp[0:65, h * 128:(h + 1) * 128], ka[b][:, h, 0:65], ident_b[:, :],
                                 [f"ka{b}", "ident_b"], [bk])
                        banks[("bp", t)] = (bp, bk)

                    def st1(t):
                        b = t % 2
                        bp, bk = banks.pop(("bp", t))
                        k.cp("act", kcT[b][0:65, :, :], bp[0:65, :].rearrange("p (h n) -> p h n", h=8), [bk],
                             [f"kcT{b}"])
                        sp_, spk = nf()
                        for h in range(8):
                            k.mm(sp_[:, h * 16:(h + 1) * 16], kcT[b][0:65, h, :], qTa[0:65, h, qs:qs + 16],
                                 True, True, [f"kcT{b}", "qTa"], [spk])
                        banks[("sp", t)] = (sp_, spk)

                    def st2(t):
                        b = t % 2
                        sp_, spk = banks.pop(("sp", t))
                        for h in range(8):
                            k.act(PTs[b][:, h, :], sp_[:, h * 16:(h + 1) * 16], AF.Exp, [spk], [f"PTs{b}"],
                                  sreads=["suf"], bias=suf[:, t, h:h + 1])
                        op2, op2k = nf()
                        for h in range(8):
                            k.mm(op2[0:65, h * 16:(h + 1) * 16], vca[b][:, h, 0:65], PTs[b][:, h, :],
                                 True, True, [f"vca{b}", f"PTs{b}"], [op2k])
                        if t == 0:
                            k.cp("dve", accs[0:65, 0:128], op2[0:65, 0:128], [op2k], [accsk])
                        else:
                            k.tt("dve", accs[0:65, 0:128], accs[0:65, 0:128], op2[0:65, 0:128], ALU.add,
                                 [accsk, op2k], [accsk])

                    for it in range(32 + 2):
                        if 0 <= it - 2 < 32:
                            st2(it - 2)
                        if 0 <= it - 1 < 32:
                            st1(it - 1)
                        if it < 32:
                            st0(it)
                    sp_, spk = nf()
                    for h in range(8):
                        k.mm(sp_[0:48, h * 16:(h + 1) * 16], kTs[0:65, h, 0:48], qTa[0:65, h, qs:qs + 16],
                             True, False, ["kTs", "qTa"], [spk])
                        k.mm(sp_[0:48, h * 16:(h + 1) * 16], ident_b[0:48, 0:48], masks_b[0:48, qs:qs + 16],
                             False, True, ["ident_b", "masks_b"], [spk])
                    for h in range(8):
                        k.act(PTs[0][0:48, h, :], sp_[0:48, h * 16:(h + 1) * 16], AF.Exp, [spk], ["PTs0"],
                              sreads=["bK0"], bias=biasK[0:48, 0, h:h + 1])
                    op2, op2k = nf()
                    for h in range(8):
                        k.mm(op2[0:65, h * 16:(h + 1) * 16], va[0:48, 0, h, 0:65], PTs[0][0:48, h, :],
                             True, True, ["va0", "PTs0"], [op2k])
                    k.tt("dve", accs[0:65, 0:128], accs[0:65, 0:128], op2[0:65, 0:128], ALU.add,
                         [accsk, op2k], [accsk])
                    k.op("dve", lambda e: e.reciprocal(out=rden[64:65, 0:128], in_=accs[64:65, 0:128]),
                         reads=[accsk], writes=["rden", "rot0", "rot1"])
                    bc, bck = nf()
                    k.mm(bc[0:64, 0:128], cs["ones"][64:65, 0:64], rden[64:65, 0:128], True, True,
                         ["c_ones", "rden"], [bck])
                    k.cp("act", bcs[0:64, 0:128], bc[0:64, 0:128], [bck], ["bcs", "rot0", "rot1"])
                    k.tt("dve", ofT[0:64, :, qs:qs + 16], accs[0:64, 0:128].rearrange("p (h q) -> p h q", h=8),
                         bcs[0:64, 0:128].rearrange("p (h q) -> p h q", h=8), ALU.mult, [accsk, "bcs"], ["R1b"])

            mixer(n, tiles, [cfgs], fox_sample, False)
            for sq in range(2):
                for h in range(4):
                    k.dma("pool", O["st_s"][sq, h], state[sq][:, h, :], reads=[f"state{sq}"], dkey="d_stout")
            ffn("f2", n)
            ple(n)
            store_group(ys, n)

        k.wait_all("pool", [kk for kk in k.sem if str(kk).startswith("d_")])
        k.replay()
        print("SBUF bytes/partition:", k.sbytes, " instr:", {e: len(s) for e, s in k.streams.items()})
    return nc


_PROG = {}


def kernel(**inp):
    f = lambda a: np.ascontiguousarray(np.asarray(a, dtype=np.float32))
    if "nc" not in _PROG:
        _PROG["nc"] = build_program()
    nc = _PROG["nc"]
    CT = _consts()
    in_maps = []
    for c in range(NCORES):
        m = {}
        m["x_p"] = f(inp["x_prompt"][c * NSEQ:(c + 1) * NSEQ])
        m["p_p"] = f(inp["p_prompt"][0, c * NSEQ:(c + 1) * NSEQ])
        m["x_s"] = f(inp["x_sample"][c * NSS:(c + 1) * NSS])
        m["p_s"] = f(inp["p_sample"][0, c * NSS:(c + 1) * NSS])
        m["st_in"] = f(inp["state_ret"][0, c * NSS:(c + 1) * NSS])
        m["ck"] = f(inp["cache_fox_k"][0, c * NSS:(c + 1) * NSS]).reshape(NSS, PAST, 512)
        m["cv"] = f(inp["cache_fox_v"][0, c * NSS:(c + 1) * NSS]).reshape(NSS, PAST, 512)
        m["clf"] = f(inp["cache_fox_logf"][0, c * NSS:(c + 1) * NSS])
        for n in ("norm_ffn1_g", "ffn1_w_in", "ffn1_w_out", "norm_mix_g", "w_in_mix", "b_forget",
                  "q_norm_g", "k_norm_g", "w_br_ret", "w_br_fox", "w_out", "norm_ffn2_g", "ffn2_w_in",
                  "ffn2_w_out", "norm_ple_g", "w_ple", "w_ple_gate"):
            m[n] = f(inp[n][0])
        for n in CONST_SHAPES:
            m["c_" + n] = f(CT[n])
        in_maps.append(m)
    res = run_bass_kernel_spmd(nc, in_maps, core_ids=list(range(NCORES)))
    R = res.results
    cat = lambda n: np.concatenate([np.asarray(r[n], dtype=np.float32) for r in R], axis=0)
    y_p = cat("y_p")
    y_s = cat("y_s")
    st_p = cat("st_p")[None]
    k_p = cat("k_p").reshape(1, 32, S, 8, 64)
    v_p = cat("v_p").reshape(1, 32, S, 8, 64)
    lf_p = cat("lf_p")[None]
    st_s = cat("st_s")[None]
    k_s = cat("k_s").reshape(1, 16, L, 8, 64)
    v_s = cat("v_s").reshape(1, 16, L, 8, 64)
    lf_s = cat("lf_s")[None]
    return (y_p, y_s, st_p, k_p, v_p, lf_p, st_s, k_s, v_s, lf_s)
```
